# Optimizing a Trainium2 kernel written in Bass

```python
import jax, jax.numpy as jnp
from jax import lax
import numpy as np

D_MODEL = 2048
BATCH = 4
SEQ = 2048
DEPTH = 1

NSA_HEADS = 16
NSA_KV_GROUPS = 4
NSA_HEAD_DIM = 128
CMP_BLOCK = 32
CMP_STRIDE = 16
SLC_BLOCK = 64
SLC_TOP_N = 16
WINDOW = 512
WIN_Q_BLOCK = 128
SLC_Q_CHUNK = 32
RET_HEADS = 8
RET_KEY_DIM = 128
RET_VAL_DIM = 256
RET_CHUNK = 128
D_FF = 5632
N_ADA = 9
EPS = 1e-6
SEL_FORCE = 1e4

NSA_WIDTH = NSA_HEADS * NSA_HEAD_DIM
KV_WIDTH = NSA_KV_GROUPS * NSA_HEAD_DIM
RET_QK_WIDTH = RET_HEADS * RET_KEY_DIM
RET_V_WIDTH = RET_HEADS * RET_VAL_DIM
IN_SPLITS = (NSA_WIDTH, 6 * KV_WIDTH, 3 * NSA_HEADS, RET_QK_WIDTH, RET_QK_WIDTH, RET_V_WIDTH, RET_V_WIDTH, D_MODEL, D_MODEL)
N_IN = sum(IN_SPLITS)

kernel_name = 'hybrid_nsa_retention_macaron'


def rms_norm(x, g):
    xf = x.astype(jnp.float32)
    y = xf * lax.rsqrt(jnp.mean(xf * xf, -1, keepdims=True) + EPS)
    return (y * g.astype(jnp.float32)).astype(x.dtype)


def modulate(x, g, shift, scale):
    return rms_norm(x, g) * (1 + scale) + shift


def swiglu(x, w_gate, w_up, w_down):
    return (jax.nn.silu(x @ w_gate) * (x @ w_up)) @ w_down


def masked_softmax(s, mask):
    s = jnp.where(mask, s.astype(jnp.float32), -jnp.inf)
    m = jnp.max(s, -1, keepdims=True)
    m = jnp.where(jnp.isfinite(m), m, 0.0)
    p = jnp.where(mask, jnp.exp(s - m), 0.0)
    return p / jnp.maximum(jnp.sum(p, -1, keepdims=True), jnp.finfo(jnp.float32).tiny)


def alibi_slopes(n):
    return jnp.exp2(-8.0 * jnp.arange(1, n + 1, dtype=jnp.float32) / n)


def nsa_attention(q, kv, gate_logits, g_qk, cmp_pos, cmp_w1, cmp_b1, cmp_w2):
    B, S, _ = q.shape
    G, dh = NSA_KV_GROUPS, NSA_HEAD_DIM
    Hg = NSA_HEADS // G
    scale = dh ** -0.5
    t = jnp.arange(S)
    slopes = alibi_slopes(NSA_HEADS).reshape(G, Hg)
    q = rms_norm(q.reshape(B, S, G, Hg, dh), g_qk[0])
    kv = kv.reshape(B, S, 6, G, dh)
    k_cmp_raw, v_cmp_raw, k_slc, v_slc, k_win, v_win = [kv[:, :, i] for i in range(6)]
    k_slc = rms_norm(k_slc, g_qk[2])
    k_win = rms_norm(k_win, g_qk[3])

    n_cmp = (S - CMP_BLOCK) // CMP_STRIDE + 1
    starts = jnp.arange(n_cmp) * CMP_STRIDE
    cidx = starts[:, None] + jnp.arange(CMP_BLOCK)[None, :]

    def compress(z, i):
        blk = z[:, cidx] + cmp_pos[i][:, None, :]
        blk = blk.transpose(0, 3, 1, 2, 4).reshape(B, G, n_cmp, CMP_BLOCK * dh)
        return jax.nn.gelu(blk @ cmp_w1[i] + cmp_b1[i]) @ cmp_w2[i]

    k_cmp = rms_norm(compress(k_cmp_raw, 0), g_qk[1])
    v_cmp = compress(v_cmp_raw, 1)
    s_c = jnp.einsum('bsghd,bgcd->bghsc', q, k_cmp).astype(jnp.float32) * scale
    centre = (starts + (CMP_BLOCK - 1) / 2).astype(jnp.float32)
    s_c = s_c - slopes[:, :, None, None] * (t[:, None].astype(jnp.float32) - centre[None, :])
    p_c = masked_softmax(s_c, (starts + CMP_BLOCK - 1)[None, :] <= t[:, None])
    o_cmp = jnp.einsum('bghsc,bgcd->bsghd', p_c.astype(v_cmp.dtype), v_cmp)

    n_slc = S // SLC_BLOCK
    top_n = min(SLC_TOP_N, n_slc)
    cs = starts[:, None]
    js = (jnp.arange(n_slc) * SLC_BLOCK)[None, :]
    overlap = jnp.clip(jnp.minimum(cs + CMP_BLOCK, js + SLC_BLOCK) - jnp.maximum(cs, js), 0, None)
    overlap = overlap.astype(jnp.float32) / CMP_BLOCK
    imp = jnp.einsum('bghsc,cj->bgsj', p_c, overlap)
    blk_t = t // SLC_BLOCK
    jj = jnp.arange(n_slc)[None, :]
    forced = (jj == 0) | (jj == blk_t[:, None]) | (jj == blk_t[:, None] - 1)
    imp = jnp.where(forced, SEL_FORCE, jnp.where(jj <= blk_t[:, None], imp, -SEL_FORCE))
    _, sel = lax.top_k(imp, top_n)

    kb = k_slc.reshape(B, n_slc, SLC_BLOCK, G, dh).transpose(0, 3, 1, 2, 4)
    vb = v_slc.reshape(B, n_slc, SLC_BLOCK, G, dh).transpose(0, 3, 1, 2, 4)
    nq = S // SLC_Q_CHUNK
    q_ch = q.reshape(B, nq, SLC_Q_CHUNK, G, Hg, dh).transpose(1, 0, 2, 3, 4, 5)
    sel_ch = sel.reshape(B, G, nq, SLC_Q_CHUNK, top_n).transpose(2, 0, 1, 3, 4)
    t_ch = t.reshape(nq, SLC_Q_CHUNK)
    bi = jnp.arange(B)[:, None, None, None]
    gi = jnp.arange(G)[None, :, None, None]
    n_keys = top_n * SLC_BLOCK

    def slc_chunk(args):
        qc, selc, tc = args
        kg = kb[bi, gi, selc]
        vg = vb[bi, gi, selc]
        s = jnp.einsum('bqghd,bgqnrd->bghqnr', qc, kg).astype(jnp.float32) * scale
        pos = selc[..., None] * SLC_BLOCK + jnp.arange(SLC_BLOCK)
        dist = tc[None, None, :, None, None] - pos
        s = s - slopes[None, :, :, None, None, None] * dist[:, :, None].astype(jnp.float32)
        mask = (dist >= 0)[:, :, None].reshape(B, G, 1, SLC_Q_CHUNK, n_keys)
        p = masked_softmax(s.reshape(B, G, Hg, SLC_Q_CHUNK, n_keys), mask)
        p = p.reshape(B, G, Hg, SLC_Q_CHUNK, top_n, SLC_BLOCK)
        return jnp.einsum('bghqnr,bgqnrd->bqghd', p.astype(vg.dtype), vg)

    o_slc = lax.map(slc_chunk, (q_ch, sel_ch, t_ch))
    o_slc = o_slc.transpose(1, 0, 2, 3, 4, 5).reshape(B, S, G, Hg, dh)

    nb = S // WIN_Q_BLOCK
    span = WIN_Q_BLOCK + WINDOW
    kpad = jnp.pad(k_win, ((0, 0), (WINDOW, 0), (0, 0), (0, 0)))
    vpad = jnp.pad(v_win, ((0, 0), (WINDOW, 0), (0, 0), (0, 0)))
    widx = jnp.arange(nb)[:, None] * WIN_Q_BLOCK + jnp.arange(span)[None, :]
    kw = kpad[:, widx]
    vw = vpad[:, widx]
    qw = q.reshape(B, nb, WIN_Q_BLOCK, G, Hg, dh)
    s_w = jnp.einsum('bnqghd,bnkgd->bghnqk', qw, kw).astype(jnp.float32) * scale
    sk = widx - WINDOW
    dist_w = t.reshape(nb, WIN_Q_BLOCK)[:, :, None] - sk[:, None, :]
    mask_w = (dist_w >= 0) & (dist_w < WINDOW) & (sk[:, None, :] >= 0)
    s_w = s_w - slopes[:, :, None, None, None] * dist_w.astype(jnp.float32)
    p_w = masked_softmax(s_w, mask_w)
    o_win = jnp.einsum('bghnqk,bnkgd->bnqghd', p_w.astype(vw.dtype), vw).reshape(B, S, G, Hg, dh)

    gates = jax.nn.sigmoid(gate_logits.reshape(B, S, 3, G, Hg, 1))
    o = gates[:, :, 0] * o_cmp + gates[:, :, 1] * o_slc + gates[:, :, 2] * o_win
    return o.reshape(B, S, NSA_WIDTH)


def retention(q, k, v, g, gn_gain):
    B, S, _ = q.shape
    H, dk, dv, C = RET_HEADS, RET_KEY_DIM, RET_VAL_DIM, RET_CHUNK
    nc = S // C
    f32 = jnp.float32
    qc = q.astype(f32).reshape(B, nc, C, H, dk)
    kc = k.astype(f32).reshape(B, nc, C, H, dk) * (dk ** -0.5)
    vc = v.astype(f32).reshape(B, nc, C, H, dv)
    log_gamma = jnp.log1p(-jnp.exp2(-5.0 - jnp.arange(H, dtype=f32)))
    n = jnp.arange(C, dtype=f32)
    diff = n[:, None] - n[None, :]
    decay = jnp.where(diff >= 0, jnp.exp(log_gamma[:, None, None] * jnp.maximum(diff, 0.0)), 0.0)
    scores = jnp.einsum('bcnhd,bcmhd->bchnm', qc, kc) * decay
    inner = jnp.einsum('bchnm,bcmhe->bcnhe', scores, vc)
    zeta = jnp.exp(log_gamma[:, None] * (C - 1 - n)[None, :])
    kv = jnp.einsum('bcmhd,hm,bcmhe->cbhde', kc, zeta, vc)
    chunk_decay = jnp.exp(log_gamma * C)[None, :, None, None]

    def step(state, kv_i):
        return state * chunk_decay + kv_i, state

    _, prev = lax.scan(step, jnp.zeros((B, H, dk, dv), f32), kv)
    xi = jnp.exp(log_gamma[:, None] * (n + 1.0)[None, :])
    cross = jnp.einsum('bcnhd,cbhde,hn->bcnhe', qc, prev, xi)
    y = (inner + cross).reshape(B, S, H, dv)
    yc = y - jnp.mean(y, -1, keepdims=True)
    y = yc * lax.rsqrt(jnp.mean(yc * yc, -1, keepdims=True) + EPS) * gn_gain.astype(f32)
    out = jax.nn.silu(g.astype(f32).reshape(B, S, H, dv)) * y
    return out.reshape(B, S, RET_V_WIDTH).astype(q.dtype)


def setup_inputs(seed: int = 0) -> dict:
    key = jax.random.key(seed)
    ks = jax.random.split(key, 20)
    f32 = jnp.float32
    dh = NSA_HEAD_DIM

    def nrm(k, shape, fan_in):
        return jax.random.normal(k, shape, f32) * (fan_in ** -0.5)

    return {
        'x': jax.random.normal(ks[0], (BATCH, SEQ, D_MODEL), f32),
        'c': jax.random.normal(ks[1], (BATCH, D_MODEL), f32),
        'w_ada': 0.5 * nrm(ks[2], (DEPTH, D_MODEL, N_ADA * D_MODEL), D_MODEL),
        'b_ada': 0.01 * jax.random.normal(ks[3], (DEPTH, N_ADA * D_MODEL), f32),
        'g_norm': 1.0 + 0.02 * jax.random.normal(ks[4], (DEPTH, 3, D_MODEL), f32),
        'w_ffn_gate': nrm(ks[5], (DEPTH, 2, D_MODEL, D_FF), D_MODEL),
        'w_ffn_up': nrm(ks[6], (DEPTH, 2, D_MODEL, D_FF), D_MODEL),
        'w_ffn_down': nrm(ks[7], (DEPTH, 2, D_FF, D_MODEL), D_FF),
        'w_in': nrm(ks[8], (DEPTH, D_MODEL, N_IN), D_MODEL),
        'g_qk': 1.0 + 0.02 * jax.random.normal(ks[9], (DEPTH, 4, dh), f32),
        'cmp_pos': 0.02 * jax.random.normal(ks[10], (DEPTH, 2, CMP_BLOCK, dh), f32),
        'cmp_w1': nrm(ks[11], (DEPTH, 2, CMP_BLOCK * dh, dh), CMP_BLOCK * dh),
        'cmp_b1': 0.01 * jax.random.normal(ks[12], (DEPTH, 2, dh), f32),
        'cmp_w2': nrm(ks[13], (DEPTH, 2, dh, dh), dh),
        'ret_gn_gain': 1.0 + 0.02 * jax.random.normal(ks[14], (DEPTH, RET_HEADS, RET_VAL_DIM), f32),
        'w_proj_nsa': nrm(ks[15], (DEPTH, NSA_WIDTH, D_MODEL), NSA_WIDTH),
        'w_proj_ret': nrm(ks[16], (DEPTH, RET_V_WIDTH, D_MODEL), RET_V_WIDTH),
        'w_out': nrm(ks[17], (DEPTH, D_MODEL, D_MODEL), D_MODEL),
    }


def reference(x, c, w_ada, b_ada, g_norm, w_ffn_gate, w_ffn_up, w_ffn_down, w_in, g_qk, cmp_pos, cmp_w1, cmp_b1, cmp_w2, ret_gn_gain, w_proj_nsa, w_proj_ret, w_out):
    B, S, D = x.shape
    cond = jax.nn.silu(c)
    split_at = tuple(np.cumsum(IN_SPLITS)[:-1].tolist())
    for l in range(DEPTH):
        ada = (cond @ w_ada[l] + b_ada[l]).reshape(B, N_ADA, 1, D)
        sh1, sc1, gt1, sh2, sc2, gt2, sh3, sc3, gt3 = [ada[:, i] for i in range(N_ADA)]
        h = modulate(x, g_norm[l, 0], sh1, sc1)
        x = x + 0.5 * gt1 * swiglu(h, w_ffn_gate[l, 0], w_ffn_up[l, 0], w_ffn_down[l, 0])
        u = modulate(x, g_norm[l, 1], sh2, sc2)
        q_nsa, kv_nsa, gl_nsa, q_r, k_r, v_r, g_r, ga, gb = jnp.split(u @ w_in[l], split_at, axis=-1)
        o_nsa = nsa_attention(q_nsa, kv_nsa, gl_nsa, g_qk[l], cmp_pos[l], cmp_w1[l], cmp_b1[l], cmp_w2[l])
        o_ret = retention(q_r, k_r, v_r, g_r, ret_gn_gain[l])
        merged = jax.nn.sigmoid(ga) * (o_nsa @ w_proj_nsa[l]) + jax.nn.sigmoid(gb) * (o_ret @ w_proj_ret[l])
        x = x + gt2 * (merged @ w_out[l])
        h = modulate(x, g_norm[l, 2], sh3, sc3)
        x = x + 0.5 * gt3 * swiglu(h, w_ffn_gate[l, 1], w_ffn_up[l, 1], w_ffn_down[l, 1])
    return x
```

```python
import contextlib
import os
from contextlib import ExitStack
import numpy as np
import ml_dtypes
import concourse.bass as bass
import concourse.mybir as mybir
from concourse.bass_utils import run_bass_kernel_spmd

F32 = mybir.dt.float32
BF16 = mybir.dt.bfloat16
ALU = mybir.AluOpType
AF = mybir.ActivationFunctionType

ENGS = ("pe", "act", "dve", "pool", "sp")
SAME_ENGINE_SYNC = True

D = 2048
DFF = 5632
NH = 16
NG = 4
DH = 128
RH = 8
RDK = 128
RDV = 256
EPS = 1e-6
NEG = -30000.0
IN_SPLITS = (2048, 3072, 48, 1024, 1024, 2048, 2048, 2048, 2048)
OFF = np.concatenate([[0], np.cumsum(IN_SPLITS)]).tolist()
O_Q, O_KV, O_GL, O_RQ, O_RK, O_RV, O_RG, O_GA, O_GB = OFF[:9]
NIN = OFF[9]


class Op:
    __slots__ = ("eng", "fn", "reads", "writes", "dma", "key", "deps", "sig", "sigidx", "dmacount", "idx", "waw", "dneed")


class Prog:
    def __init__(self):
        self.ops = []
        self.last_write = {}
        self.reads_since = {}
        self.dma_count = {}
        self.bar = set()
        self.last_eng = {}
        self.last_dma = {}

    def barrier(self):
        self.bar = set(self.last_eng.values()) | set(self.last_dma.values())

    def op(self, eng, fn, reads=(), writes=(), dma=False, key=None):
        o = Op()
        o.eng, o.fn, o.dma = eng, fn, dma
        o.reads, o.writes = tuple(reads), tuple(writes)
        o.idx = len(self.ops)
        o.sig = False
        o.sigidx = None
        o.waw = set()
        o.dneed = {}
        deps = set(self.bar)
        for r in o.reads:
            w = self.last_write.get(r)
            if w is not None:
                deps.add(w)
            if r.startswith("bank"):
                for rd in self.reads_since.get(r, ()):
                    if self.ops[rd].eng != eng:
                        deps.add(rd)
        for r in o.writes:
            w = self.last_write.get(r)
            if w is not None:
                deps.add(w)
                o.waw.add(w)
            for rd in self.reads_since.get(r, ()):
                deps.add(rd)
        o.deps = deps
        for d_ in deps:
            p_ = self.ops[d_]
            if p_.dma:
                o.dneed[p_.key] = p_.dmacount
        for r in list(o.reads) + list(o.writes):
            for d_ in [self.last_write.get(r)] + list(self.reads_since.get(r, ())):
                if d_ is not None and self.ops[d_].dma:
                    o.dneed[self.ops[d_].key] = self.dma_count[self.ops[d_].key]
        if dma:
            o.key = key if key is not None else o.writes[0]
            self.dma_count[o.key] = self.dma_count.get(o.key, 0) + 1
            o.dmacount = self.dma_count[o.key]
            self.last_dma[o.key] = o.idx
        else:
            o.key = None
            o.dmacount = 0
            self.last_eng[eng] = o.idx
        for r in o.reads:
            self.reads_since.setdefault(r, []).append(o.idx)
        for r in o.writes:
            self.last_write[r] = o.idx
            self.reads_since[r] = []
        self.ops.append(o)
        return o

    def emit(self, nc, final_keys=(), final_eng="sp"):
        ops = self.ops
        for o in ops:
            nd = set()
            for d in o.deps:
                p = ops[d]
                if p.dma:
                    if o.dma and p.key == o.key and d in o.waw:
                        continue
                    nd.add(d)
                else:
                    if p.eng == o.eng and not o.dma:
                        if p.eng == "pe":
                            continue
                        if not SAME_ENGINE_SYNC:
                            continue
                    nd.add(d)
            best = {}
            for d in nd:
                p = ops[d]
                k = ("d", p.key) if p.dma else ("e", p.eng)
                if k not in best or best[k] < d:
                    best[k] = d
            o.deps = set(best.values())
            for d in o.deps:
                if not ops[d].dma:
                    ops[d].sig = True
        cnt = {e: 0 for e in ENGS}
        for o in ops:
            if o.sig and not o.dma:
                cnt[o.eng] += 1
                o.sigidx = cnt[o.eng]
        with ExitStack() as st:
            esem = {e: st.enter_context(nc.semaphore("s_" + e)) for e in ENGS}
            dsem = {}
            for k in self.dma_count:
                dsem[k] = st.enter_context(nc.semaphore("d_%d" % len(dsem)))
            block = st.enter_context(nc.Block())

            def run_engine(ename, eng):
                waited = {}
                for o in ops:
                    if o.eng != ename:
                        continue
                    need = {}
                    for d in o.deps:
                        p = ops[d]
                        if p.dma:
                            k = ("d", p.key)
                            v = 16 * o.dneed[p.key]
                            s = dsem[p.key]
                        else:
                            k = ("e", p.eng)
                            v = p.sigidx
                            s = esem[p.eng]
                        if need.get(k, (0, None))[0] < v:
                            need[k] = (v, s)
                    for k, (v, s) in need.items():
                        if waited.get(k, 0) >= v:
                            continue
                        eng.wait_ge(s, v)
                        waited[k] = v
                    ins = o.fn(eng)
                    if o.dma:
                        ins.then_inc(dsem[o.key], 16)
                    elif o.sig:
                        ins.then_inc(esem[o.eng], 1)
                if ename == final_eng:
                    for k in self.dma_count:
                        eng.wait_ge(dsem[k], 16 * self.dma_count[k])

            @block.tensor
            def _(e):
                run_engine("pe", e)

            @block.scalar
            def _(e):
                run_engine("act", e)

            @block.vector
            def _(e):
                run_engine("dve", e)

            @block.gpsimd
            def _(e):
                run_engine("pool", e)

            @block.sync
            def _(e):
                run_engine("sp", e)

    def bar_of(self, o):
        return ()


def _split3(a):
    a = np.asarray(a, np.float32)
    hi = a.astype(ml_dtypes.bfloat16)
    r1 = a - hi.astype(np.float32)
    lo = r1.astype(ml_dtypes.bfloat16)
    r2 = r1 - lo.astype(np.float32)
    ll = r2.astype(ml_dtypes.bfloat16)
    return hi, lo, ll


def make_tables(j):
    bf = ml_dtypes.bfloat16
    T = {}
    slopes = np.exp2(-8.0 * np.arange(1, NH + 1, dtype=np.float32) / NH).astype(np.float32)
    rows = np.zeros((NG, 64, 8, 4, 128), np.float32).astype(bf)
    ql = np.arange(128, dtype=np.float32)
    for g in range(NG):
        for hh in range(4):
            s = slopes[4 * g + hh]
            s3 = _split3(np.full((128,), s, np.float32))
            for qb in range(8):
                tq = (1024 + 128 * qb + ql).astype(np.float32)
                a3 = _split3((-s * tq).astype(np.float32))
                for r in range(3):
                    rows[g, r, qb, hh] = a3[r]
                    rows[g, 3 + r, qb, hh] = s3[r]
                    rows[g, 6 + r, qb, hh] = s3[r]
                rows[g, 9, qb, hh] = (0.0 if j == 1 else NEG)
    T["rows"] = rows.reshape(NG, 64, 8 * 512)
    kr = np.zeros((64, 16, 128), np.float32)
    kl = np.arange(128, dtype=np.float32)
    for kb in range(16):
        kr[0:3, kb] = 1.0
        kr[3:6, kb] = kl[None, :]
        kr[6:9, kb] = 128.0 * kb
        kr[9, kb] = 1.0 if kb < 8 else 0.0
        for jj in range(32):
            kr[32 + jj, kb] = ((2 * kb + (np.arange(128) // 64)) == jj).astype(np.float32)
    T["krows"] = kr.astype(bf).reshape(64, 16 * 128)
    kc = np.zeros((10, 128), np.float32)
    c = np.arange(127, dtype=np.float32)
    kc[0:3, :127] = 1.0
    kc[3:6, :127] = 16.0 * c
    kc[6:9, :127] = 15.5
    T["kcrows"] = kc.astype(bf)
    cm = np.zeros((128, 8, 4, 128), np.float32)
    for qb in range(8):
        tq = 1024 + 128 * qb + np.arange(128)
        cc = np.arange(127)
        valid = (16 * cc[:, None] + 31) <= tq[None, :]
        if j == 0:
            valid = valid & (16 * cc[:, None] >= 1024)
        cm[:127, qb] = np.where(valid, 0.0, NEG)[:, None, :]
    T["cmask"] = cm.astype(bf).reshape(128, 8 * 512)
    lo = np.where(kl[:, None] <= ql[None, :], 0.0, NEG).astype(np.float32)
    up = np.where(kl[:, None] > ql[None, :], 0.0, NEG).astype(np.float32)
    T["tri"] = np.stack([np.repeat(lo[:, None, :], 4, 1), np.repeat(up[:, None, :], 4, 1)], 1).astype(bf).reshape(128, 2 * 512)
    cs = (16 * np.arange(127))[:, None]
    js = (64 * np.arange(32))[None, :]
    ov = np.clip(np.minimum(cs + 32, js + 64) - np.maximum(cs, js), 0, None).astype(np.float32) / 32.0
    ovp = np.zeros((128, 32), np.float32)
    ovp[:127] = ov
    T["ov"] = ovp.astype(bf)
    vm = np.zeros((128, 8, 32), np.float32)
    fb = np.zeros((128, 8, 32), np.float32)
    for qb in range(8):
        for q in range(128):
            tf = 1024 + 128 * qb + q
            ta = tf - (0 if j == 1 else 1024)
            bt = ta // 64
            for jf in range(32):
                ja = jf - (0 if j == 1 else 16)
                if ja < 0:
                    fb[q, qb, jf] = -2e4
                elif ja == 0 or ja == bt or ja == bt - 1:
                    fb[q, qb, jf] = 1e4
                elif ja <= bt:
                    vm[q, qb, jf] = 1.0
                else:
                    fb[q, qb, jf] = -1e4
    T["vm"] = vm.reshape(128, 256)
    T["fb"] = fb.reshape(128, 256)
    hh = np.arange(RH, dtype=np.float64)
    lg = np.log1p(-np.exp2(-5.0 - hh))
    n = np.arange(128, dtype=np.float64)
    diff = n[None, :] - n[:, None]
    dec = np.where(diff[None] >= 0, np.exp(lg[:, None, None] * np.maximum(diff[None], 0.0)), 0.0)
    T["decT"] = (dec * (RDK ** -0.5)).transpose(1, 0, 2).astype(np.float32).reshape(128, RH * 128)
    xi = np.exp(lg[:, None] * (n + 1.0)[None, :])
    T["xi"] = np.repeat(xi.astype(np.float32)[None], 128, 0).reshape(128, RH * 128)
    zeta = np.exp(lg[:, None] * (127 - n)[None, :]) * (RDK ** -0.5)
    zt = np.zeros((128, 16, RH), np.float32)
    for t in range(16):
        if t < 8:
            z = np.exp(lg[:, None] * (1023 - (128 * t + n))[None, :]) * (RDK ** -0.5)
            zt[:, t, :] = (z.T if j == 1 else 0.0)
        else:
            zt[:, t, :] = zeta.T
    T["zt"] = zt.reshape(128, 16 * RH)
    T["gC"] = [float(np.exp(lg[h] * 128)) for h in range(RH)]
    T["ident"] = np.eye(128, dtype=np.float32)
    T["identb"] = np.eye(128, dtype=np.float32).astype(bf)
    T["ones"] = np.ones((128, 128), np.float32)
    return T


TABLE_SPECS = [
    ("rows", [NG, 64, 4096], BF16), ("krows", [64, 2048], BF16), ("kcrows", [10, 128], BF16),
    ("cmask", [128, 4096], BF16), ("tri", [128, 1024], BF16), ("ov", [128, 32], BF16),
    ("vm", [128, 256], F32), ("fb", [128, 256], F32), ("decT", [128, 1024], F32), ("xi", [128, 1024], F32),
    ("zt", [128, 128], F32), ("ident", [128, 128], F32), ("identb", [128, 128], BF16), ("ones", [128, 128], F32),
]
GC = make_tables(1)["gC"]


def build(stop_after=None, dbg=(), mixer_only=False):
    nc = bass.Bass("TRN2", target_bir_lowering=False)
    P = Prog()

    def din(name, shape, dt=F32):
        return nc.dram_tensor(name, shape, dt, kind="ExternalInput").ap()

    xf = din("xf", [2048, D])
    c_l = din("c_l", [128, 16])
    if not mixer_only:
        w_ada = din("w_ada", [D, 9 * D])
        b_ada = din("b_ada", [128, 9 * D])
        gnorm = din("gnorm", [128, 3 * D])
        wg = din("wg", [2, D, DFF])
        wu = din("wu", [2, D, DFF])
        wd = din("wd", [2, DFF, D])
    w_in = din("w_in", [D, NIN])
    gqkT = din("gqkT", [128, 4])
    gqkR = din("gqkR", [128, 4 * 128])
    cpos = din("cpos", [128, 2 * 32])
    cw1 = din("cw1", [2, 4096, 128])
    cb1 = din("cb1", [128, 2])
    cw2 = din("cw2", [2, 128, 128])
    gnr = din("gnr", [128, RH * RDV])
    wpn = din("wpn", [D, D])
    wpr = din("wpr", [D, D])
    wo = din("wo", [D, D])
    tb_ap = {}
    for name, shape, dt in TABLE_SPECS:
        tb_ap[name] = din("t_" + name, shape, dt)
    out = nc.dram_tensor("out", [1024, D], F32, kind="ExternalOutput").ap()
    dbg_ap = {}
    for name, shape, dt in dbg:
        dbg_ap[name] = nc.dram_tensor("dbg_" + name, shape, dt, kind="ExternalOutput").ap()
    if mixer_only:
        ada_d = din("ada_in", [128, 9 * D])
        uT_d = din("uT_in", [2, 128, 16 * 1024], BF16)
        x1_i = din("x1_in", [128, 8 * D])
        x1_d = nc.dram_tensor("x1_d", [128, 8 * D], F32, kind="Internal").ap()
    else:
        ada_d = nc.dram_tensor("ada_d", [128, 9 * D], F32, kind="Internal").ap()
        uT_d = nc.dram_tensor("uT_d", [2, 128, 16 * 1024], BF16, kind="Internal").ap()
        x1_d = nc.dram_tensor("x1_d", [128, 8 * D], F32, kind="Internal").ap()
    orT_d = nc.dram_tensor("orT_d", [128, 16 * 1024], BF16, kind="Internal").ap()

    final_keys = []
    outer = ExitStack()
    with outer:
        uid = [0]

        def SB(st, name, shape, dt):
            uid[0] += 1
            return st.enter_context(nc.sbuf_tensor("%s_%d" % (name, uid[0]), shape, dt))

        banks = [outer.enter_context(nc.psum_tensor("bank%d" % i, [128, 512], F32)) for i in range(8)]
        ident = SB(outer, "ident", [128, 128], F32)
        identb = SB(outer, "identb", [128, 128], BF16)
        ones = SB(outer, "ones", [128, 128], F32)
        P.op("sp", lambda e: e.dma_start(out=ident[:], in_=tb_ap["ident"]), writes=["ident"], dma=True, key="tbl")
        P.op("sp", lambda e: e.dma_start(out=identb[:], in_=tb_ap["identb"]), writes=["identb"], dma=True, key="tbl")
        P.op("sp", lambda e: e.dma_start(out=ones[:], in_=tb_ap["ones"]), writes=["ones"], dma=True, key="tbl")

        def B(i):
            return "bank%d" % i

        def dump(name, src_ap, reads):
            if name in dbg_ap:
                k = "dbg_" + name
                P.op("sp", lambda e: e.dma_start(out=dbg_ap[name], in_=src_ap), reads=reads, writes=[k], dma=True, key=k)
                if k not in final_keys:
                    final_keys.append(k)

        ada_t = {}

        def ada_init(st):
            c_sb = SB(st, "c_sb", [128, 16], F32)
            cond = SB(st, "cond", [128, 16], F32)
            ada_t["condrep"] = SB(st, "condrep", [128, 16, 128], BF16)
            ada_t["wa"] = [SB(st, "wa%d" % i, [128, 16, 256], BF16) for i in range(2)]
            ada_t["ba"] = [SB(st, "ba%d" % i, [128, 256], F32) for i in range(2)]
            ada_t["rs"] = [SB(st, "rs%d" % i, [128, 256], F32) for i in range(2)]
            ada_t["gs"] = [SB(st, "gs%d" % i, [128, 256], F32) for i in range(2)]
            condrep = ada_t["condrep"]
            P.op("sp", lambda e: e.dma_start(out=c_sb[:], in_=c_l), writes=["c_sb"], dma=True, key="tbl")
            P.op("act", lambda e: e.activation(out=cond[:], in_=c_sb[:], func=AF.Silu), reads=["c_sb"], writes=["cond"])
            P.op("dve", lambda e: e.tensor_copy(out=condrep[:], in_=cond[:].unsqueeze(2).to_broadcast([128, 16, 128])),
                 reads=["cond"], writes=["condrep"])

        def ada_block(cb):
            condrep, wa, ba, rs, gs = ada_t["condrep"], ada_t["wa"], ada_t["ba"], ada_t["rs"], ada_t["gs"]
            wsrc = w_ada.rearrange("(k p) n -> p k n", p=128)
            s = cb % 2
            slot, fbk = cb // 8, cb % 8
            i, kind = slot // 3, slot % 3
            cs = slice(cb * 256, (cb + 1) * 256)
            bk = s
            P.op("pool", lambda e: e.dma_start(out=wa[s][:], in_=wsrc[:, :, cs]), writes=["wa%d" % s], dma=True)
            P.op("sp", lambda e: e.dma_start(out=ba[s][:], in_=b_ada[:, cs]), writes=["ba%d" % s], dma=True)
            for k in range(16):
                P.op("pe", lambda e, k=k: e.matmul(banks[bk][:, 0:256], lhsT=condrep[:, k, :], rhs=wa[s][:, k, :], start=(k == 0), stop=(k == 15)),
                     reads=["condrep", "wa%d" % s], writes=[B(bk)])
            P.op("dve", lambda e: e.tensor_tensor(out=rs[s][:], in0=banks[bk][:, 0:256], in1=ba[s][:], op=ALU.add),
                 reads=[B(bk), "ba%d" % s], writes=["rs%d" % s])
            if kind == 1:
                gsl = slice(i * D + fbk * 256, i * D + (fbk + 1) * 256)
                P.op("sp", lambda e: e.dma_start(out=gs[s][:], in_=gnorm[:, gsl]), writes=["gs%d" % s], dma=True)
                P.op("dve", lambda e: e.scalar_tensor_tensor(out=rs[s][:], in0=rs[s][:], scalar=1.0, in1=gs[s][:], op0=ALU.add, op1=ALU.mult),
                     reads=["rs%d" % s, "gs%d" % s], writes=["rs%d" % s])
            elif kind == 2 and i != 1:
                P.op("dve", lambda e: e.tensor_scalar_mul(out=rs[s][:], in0=rs[s][:], scalar1=0.5), reads=["rs%d" % s], writes=["rs%d" % s])
            P.op("sp", lambda e: e.dma_start(out=ada_d[:, cs], in_=rs[s][:]), reads=["rs%d" % s], writes=["ada_d%d" % cb], dma=True, key="st_rs%d" % s)

        def ada_names(slot):
            return ["ada_d%d" % cb for cb in range(slot * 8, slot * 8 + 8)]

        def norm_mod_T(x_sb, i, dstT, dst_name):
            with ExitStack() as st:
                G = SB(st, "G", [128, D], F32)
                SH = SB(st, "SH", [128, D], F32)
                ss = SB(st, "ss", [128, 8], F32)
                junk = SB(st, "junk", [128, D], F32)
                hf = [SB(st, "hf%d" % k, [128, D], F32) for k in range(2)]
                P.op("sp", lambda e: e.dma_start(out=G[:], in_=ada_d[:, (3 * i + 1) * D:(3 * i + 2) * D]),
                     reads=ada_names(3 * i + 1), writes=["G"], dma=True)
                P.op("sp", lambda e: e.dma_start(out=SH[:], in_=ada_d[:, (3 * i) * D:(3 * i + 1) * D]),
                     reads=ada_names(3 * i), writes=["SH"], dma=True)
                P.op("dve", lambda e: e.memset(ss[:], 0.0), writes=["ss"])
                for tb in range(8):
                    P.op("act", lambda e, tb=tb: e.activation(out=junk[:], in_=x_sb[:, tb, :], func=AF.Square, accum_out=ss[:, tb:tb + 1]),
                         reads=["x%d_%d" % (tb, f) for f in range(4)], writes=["junk", "ss"])
                P.op("dve", lambda e: e.tensor_scalar(out=ss[:], in0=ss[:], scalar1=1.0 / D, scalar2=EPS, op0=ALU.mult, op1=ALU.add), reads=["ss"], writes=["ss"])
                P.op("act", lambda e: e.activation(out=ss[:], in_=ss[:], func=AF.Sqrt), reads=["ss"], writes=["ss"])
                P.op("dve", lambda e: e.reciprocal(out=ss[:], in_=ss[:]), reads=["ss"], writes=["ss"])
                tcount = 0
                for tb in range(8):
                    s = tb % 2
                    P.op("dve", lambda e, tb=tb, s=s: e.scalar_tensor_tensor(out=hf[s][:], in0=x_sb[:, tb, :], scalar=ss[:, tb:tb + 1], in1=G[:], op0=ALU.mult, op1=ALU.mult),
                         reads=["x%d_%d" % (tb, f) for f in range(4)] + ["ss", "G"], writes=["hf%d" % s])
                    P.op("dve", lambda e, s=s: e.tensor_tensor(out=hf[s][:], in0=hf[s][:], in1=SH[:], op=ALU.add), reads=["hf%d" % s, "SH"], writes=["hf%d" % s])
                    for kq in range(4):
                        bk = tcount % 8
                        tcount += 1
                        for kk in range(4):
                            P.op("pe", lambda e, s=s, bk=bk, kk=kk, kq=kq: e.transpose(banks[bk][:, kk * 128:(kk + 1) * 128], hf[s][:, (4 * kq + kk) * 128:(4 * kq + kk + 1) * 128], ident[:]),
                                 reads=["hf%d" % s, "ident"], writes=[B(bk)])
                        eng = "act" if kq % 2 == 0 else "dve"
                        if eng == "act":
                            P.op("act", lambda e, bk=bk, kq=kq, tb=tb: e.activation(out=dstT[:, 4 * kq:4 * kq + 4, tb * 128:(tb + 1) * 128], in_=banks[bk][:].rearrange("p (k t) -> p k t", k=4), func=AF.Copy),
                                 reads=[B(bk)], writes=["%s_%d" % (dst_name, tb)])
                        else:
                            P.op("dve", lambda e, bk=bk, kq=kq, tb=tb: e.tensor_copy(out=dstT[:, 4 * kq:4 * kq + 4, tb * 128:(tb + 1) * 128], in_=banks[bk][:].rearrange("p (k t) -> p k t", k=4)),
                                 reads=[B(bk)], writes=["%s_%d" % (dst_name, tb)])
            P.barrier()

        def ffn_core(x_sb, hT, i, l, hook=None):
            Wg = wg[l].rearrange("(k p) n -> p k n", p=128)
            Wu = wu[l].rearrange("(k p) n -> p k n", p=128)
            Wd = wd[l].rearrange("(c p) n -> p c n", p=128)
            with ExitStack() as st:
                GT = SB(st, "GT", [128, D], F32)
                hid = SB(st, "hid", [128, 11, 1024], BF16)
                wgt = [SB(st, "wgt%d" % k, [128, 16, 128], BF16) for k in range(2)]
                wut = [SB(st, "wut%d" % k, [128, 16, 128], BF16) for k in range(2)]
                wdt = [SB(st, "wdt%d" % k, [128, 11, 512], BF16) for k in range(2)]
                stt = [SB(st, "stt%d" % k, [128, 512], F32) for k in range(2)]
                rt = [SB(st, "rt%d" % k, [128, 512], F32) for k in range(2)]
                hT_names = ["hT_%d" % tb for tb in range(8)]
                step = 0
                oset = 0
                for r in range(4):
                    for cc in range(11):
                        c = r * 11 + cc
                        s = c % 2
                        csl = slice(c * 128, (c + 1) * 128)
                        P.op("pool", lambda e, s=s, csl=csl: e.dma_start(out=wgt[s][:], in_=Wg[:, :, csl]), writes=["wgt%d" % s], dma=True)
                        P.op("pool", lambda e, s=s, csl=csl: e.dma_start(out=wut[s][:], in_=Wu[:, :, csl]), writes=["wut%d" % s], dma=True)
                        for half in range(2):
                            bg, bu = 4 + 2 * (step % 2), 5 + 2 * (step % 2)
                            ss_ = step % 2
                            step += 1
                            tsl = slice(half * 512, (half + 1) * 512)
                            hr = hT_names[half * 4:(half + 1) * 4]
                            for k in range(16):
                                P.op("pe", lambda e, s=s, k=k, bg=bg, tsl=tsl: e.matmul(banks[bg][:], lhsT=wgt[s][:, k, :], rhs=hT[:, k, tsl], start=(k == 0), stop=(k == 15)),
                                     reads=["wgt%d" % s] + hr, writes=[B(bg)])
                            for k in range(16):
                                P.op("pe", lambda e, s=s, k=k, bu=bu, tsl=tsl: e.matmul(banks[bu][:], lhsT=wut[s][:, k, :], rhs=hT[:, k, tsl], start=(k == 0), stop=(k == 15)),
                                     reads=["wut%d" % s] + hr, writes=[B(bu)])
                            P.op("act", lambda e, bg=bg, ss_=ss_: e.activation(out=stt[ss_][:], in_=banks[bg][:], func=AF.Silu), reads=[B(bg)], writes=["stt%d" % ss_])
                            P.op("dve", lambda e, bu=bu, ss_=ss_, cc=cc, tsl=tsl: e.tensor_tensor(out=hid[:, cc, tsl], in0=stt[ss_][:], in1=banks[bu][:], op=ALU.mult),
                                 reads=["stt%d" % ss_, B(bu)], writes=["hid%d_%d" % (cc, half)])
                        if hook is not None:
                            hook(c)
                    if r == 0:
                        P.op("sp", lambda e: e.dma_start(out=GT[:], in_=ada_d[:, (3 * i + 2) * D:(3 * i + 3) * D]),
                             reads=ada_names(3 * i + 2), writes=["GT"], dma=True)
                    for fbk in range(4):
                        ws = (r * 4 + fbk) % 2
                        fsl = slice(fbk * 512, (fbk + 1) * 512)
                        P.op("pool", lambda e, ws=ws, r=r, fsl=fsl: e.dma_start(out=wdt[ws][:], in_=Wd[:, r * 11:(r + 1) * 11, fsl]), writes=["wdt%d" % ws], dma=True)
                        for tp in range(4):
                            ob = [2 * (oset % 2), 2 * (oset % 2) + 1]
                            oset += 1
                            for cc in range(11):
                                for t2 in range(2):
                                    tb = 2 * tp + t2
                                    P.op("pe", lambda e, ws=ws, cc=cc, tb=tb, b_=ob[t2]: e.matmul(banks[b_][:], lhsT=hid[:, cc, tb * 128:(tb + 1) * 128], rhs=wdt[ws][:, cc, :], start=(cc == 0), stop=(cc == 10)),
                                         reads=["hid%d_%d" % (cc, tb // 4), "wdt%d" % ws], writes=[B(ob[t2])])
                            for t2 in range(2):
                                tb = 2 * tp + t2
                                P.op("dve", lambda e, t2=t2, b_=ob[t2], fsl=fsl: e.tensor_tensor(out=rt[t2][:], in0=banks[b_][:], in1=GT[:, fsl], op=ALU.mult),
                                     reads=[B(ob[t2]), "GT"], writes=["rt%d" % t2])
                                P.op("dve", lambda e, t2=t2, tb=tb, fsl=fsl: e.tensor_tensor(out=x_sb[:, tb, fsl], in0=x_sb[:, tb, fsl], in1=rt[t2][:], op=ALU.add),
                                     reads=["rt%d" % t2, "x%d_%d" % (tb, fbk)], writes=["x%d_%d" % (tb, fbk)])
            P.barrier()

        xnames = ["x%d_%d" % (tb, f) for tb in range(8) for f in range(4)]

        def load_x(x_sb, src):
            v = src.rearrange("(t p) d -> p t d", p=128)
            for tb in range(8):
                P.op("sp", lambda e, tb=tb: e.dma_start(out=x_sb[:, tb, :], in_=v[:, tb, :]), writes=["x%d_%d" % (tb, f) for f in range(4)], dma=True, key="ldx")

        sada = ExitStack()
        ada_pending = list(range(16, 72))
        if not mixer_only:
            ada_init(sada)
            for cb in range(16):
                ada_block(cb)
        else:
            with ExitStack() as sx0:
                xt0 = SB(sx0, "xt0", [128, 8 * D], F32)
                P.op("sp", lambda e: e.dma_start(out=xt0[:], in_=x1_i), writes=["xt0"], dma=True)
                P.op("sp", lambda e: e.dma_start(out=x1_d, in_=xt0[:]), reads=["xt0"], writes=["x1_d"], dma=True, key="st_x1")
            P.barrier()
        with ExitStack() as sx:
            if not mixer_only:
                x_sb = SB(sx, "x_sb", [128, 8, D], F32)
                hT = SB(sx, "hT", [128, 16, 1024], BF16)
                uT = hT
            for half in ((0, 1) if not mixer_only else ()):
                load_x(x_sb, xf[half * 1024:(half + 1) * 1024, :])
                norm_mod_T(x_sb, 0, hT, "hT")
                lim = 48 if half == 0 else 72

                def hook(c, lim=lim):
                    if ada_pending and ada_pending[0] < lim:
                        ada_block(ada_pending.pop(0))
                ffn_core(x_sb, hT, 0, 0, hook=hook)
                assert not ada_pending or ada_pending[0] >= lim
                norm_mod_T(x_sb, 1, uT, "uT")
                P.op("sp", lambda e, half=half: e.dma_start(out=uT_d[half], in_=uT[:].rearrange("p k t -> p (k t)")),
                     reads=["uT_%d" % tb for tb in range(8)], writes=["uT_d%d" % half], dma=True, key="st_uT")
                if half == 1:
                    P.op("sp", lambda e: e.dma_start(out=x1_d, in_=x_sb[:].rearrange("p t d -> p (t d)")), reads=xnames, writes=["x1_d"], dma=True, key="st_x1")
                    dump("x1", x_sb[:].rearrange("p t d -> p (t d)"), xnames)
                    dump("uT", uT[:].rearrange("p k t -> p (k t)"), ["uT_%d" % tb for tb in range(8)])
                P.barrier()
        sada.close()
        P.barrier()
        if stop_after == "ffn1":
            P.emit(nc, final_keys=final_keys)
            return nc

        Win = w_in.rearrange("(k p) n -> p k n", p=128)

        def load_w(tile, tname, col0, ncols, dst0=0):
            P.op("pool", lambda e: e.dma_start(out=tile[:, :, dst0:dst0 + ncols], in_=Win[:, :, col0:col0 + ncols]), writes=[tname], dma=True)

        def fm_proj(wt, wname, uT_, uname, chunks, bank_of, evac):
            for ci in chunks:
                bk = bank_of(ci)
                rn = ["%s_%d" % (uname, tb) for tb in range(4 * ci, 4 * ci + 4)]
                for k in range(16):
                    P.op("pe", lambda e, k=k, bk=bk, ci=ci: e.matmul(banks[bk][:], lhsT=wt[:, k, 0:128], rhs=uT_[:, k, ci * 512:(ci + 1) * 512], start=(k == 0), stop=(k == 15)),
                         reads=[wname] + rn, writes=[B(bk)])
                evac(bk, ci)

        def tm_proj(wt, wname, ncols, uT_, uname, tbs, bank_of, evac):
            for tb in tbs:
                bk = bank_of(tb)
                for k in range(16):
                    P.op("pe", lambda e, k=k, bk=bk, tb=tb: e.matmul(banks[bk][:, 0:ncols], lhsT=uT_[:, k, tb * 128:(tb + 1) * 128], rhs=wt[:, k, 0:ncols], start=(k == 0), stop=(k == 15)),
                         reads=[wname, "%s_%d" % (uname, tb)], writes=[B(bk)])
                evac(bk, tb)

        with ExitStack() as sm:
            uTo = SB(sm, "uTo", [128, 16, 1024], BF16)
            onT = SB(sm, "onT", [128, 16, 1024], BF16)
            uTo_n = ["uTo_%d" % tb for tb in range(8)]
            for tb in range(8):
                P.op("sp", lambda e, tb=tb: e.dma_start(out=uTo[:, :, tb * 128:(tb + 1) * 128], in_=uT_d[1].rearrange("p (k t) -> p k t", k=16)[:, :, tb * 128:(tb + 1) * 128]),
                     reads=["uT_d1"], writes=["uTo_%d" % tb], dma=True, key="ld_uTo")
            with ExitStack() as sa:
                uTc = SB(sa, "uTc", [128, 16, 1024], BF16)
                for tb in range(8):
                    P.op("sp", lambda e, tb=tb: e.dma_start(out=uTc[:, :, tb * 128:(tb + 1) * 128], in_=uT_d[0].rearrange("p (k t) -> p k t", k=16)[:, :, tb * 128:(tb + 1) * 128]),
                         reads=["uT_d0"], writes=["uTc_%d" % tb], dma=True, key="ld_uTc")
                gates = SB(sa, "gates", [128, 8, 48], F32)
                wt1 = [SB(sa, "wt1_%d" % k, [128, 16, 128], BF16) for k in range(3)]
                load_w(wt1[0], "wt1_0", O_GL, 48)

                def ev_gl(bk, tb):
                    P.op("act", lambda e: e.activation(out=gates[:, tb, :], in_=banks[bk][:, 0:48], func=AF.Sigmoid), reads=[B(bk)], writes=["gates"])
                tm_proj(wt1[0], "wt1_0", 48, uTo, "uTo", range(8), lambda tb: tb % 2, ev_gl)
                P.barrier()
                if stop_after == "gates":
                    P.op("sp", lambda e: e.dma_start(out=out[0:128, 0:384], in_=gates[:].rearrange("p a b -> p (a b)")), reads=["gates"], writes=["outg"], dma=True, key="st_out")
                    final_keys.append("st_out")
                    P.emit(nc, final_keys=final_keys)
                    return nc

                with ExitStack() as sr:
                    orT = SB(sr, "orT", [128, 16, 1024], BF16)
                    decT = SB(sr, "decT", [128, 1, 128], F32)
                    xi_t = SB(sr, "xi_t", [128, 1, 128], F32)
                    zt = SB(sr, "zt", [128, 16, RH], F32)
                    gnr_t = SB(sr, "gnr_t", [128, 1, RDV], F32)
                    P.op("sp", lambda e: e.dma_start(out=zt[:].rearrange("p t h -> p (t h)"), in_=tb_ap["zt"]), writes=["zt"], dma=True, key="tbl")
                    qT = SB(sr, "r_qT", [128, 1024], BF16)
                    qxT = SB(sr, "r_qxT", [128, 1024], BF16)
                    kT = SB(sr, "r_kT", [128, 1024], BF16)
                    kz = SB(sr, "r_kz", [128, 16, 128], BF16)
                    vv = SB(sr, "r_v", [128, 16, 256], BF16)
                    sg = SB(sr, "r_sg", [128, 8, 256], F32)
                    state = SB(sr, "r_state", [128, 256], F32)
                    stbf = SB(sr, "r_stbf", [128, 256], BF16)
                    sTb = [SB(sr, "r_sTb%d" % k, [128, 128], BF16) for k in range(2)]
                    t1 = [SB(sr, "r_t1%d" % k, [128, 256], F32) for k in range(2)]
                    ob = [SB(sr, "r_ob%d" % k, [128, 256], BF16) for k in range(2)]
                    ysum = SB(sr, "r_ysum", [128, 8], F32)
                    ysq = SB(sr, "r_ysq", [128, 8], F32)
                    mean = SB(sr, "r_mean", [128, 8], F32)
                    rstd = SB(sr, "r_rstd", [128, 8], F32)
                    junk2 = SB(sr, "r_junk", [128, 256], F32)
                    wkv = SB(sr, "r_wkv", [128, 16, 384], BF16)
                    wgr = SB(sr, "r_wgr", [128, 16, 256], BF16)
                    for h in range(RH):
                        P.op("sp", lambda e, h=h: e.dma_start(out=decT[:, 0, :], in_=tb_ap["decT"][:, h * 128:(h + 1) * 128]), writes=["decT"], dma=True)
                        P.op("sp", lambda e, h=h: e.dma_start(out=xi_t[:, 0, :], in_=tb_ap["xi"][:, h * 128:(h + 1) * 128]), writes=["xi_t"], dma=True)
                        P.op("sp", lambda e, h=h: e.dma_start(out=gnr_t[:, 0, :], in_=gnr[:, h * 256:(h + 1) * 256]), writes=["gnr_t"], dma=True)
                        load_w(wt1[1], "wt1_1", O_RQ + h * 128, 128)
                        load_w(wt1[2], "wt1_2", O_RK + h * 128, 128)
                        load_w(wkv, "r_wkv", O_RK + h * 128, 128, 0)
                        load_w(wkv, "r_wkv", O_RV + h * 256, 256, 128)
                        load_w(wgr, "r_wgr", O_RG + h * 256, 256)

                        RC = int(os.environ.get("RET_CUT2", "99"))
                        if RC < 2:
                            break

                        def ev_q(bk, ci, h=h):
                            P.op("act", lambda e: e.activation(out=qT[:, ci * 512:(ci + 1) * 512], in_=banks[bk][:], func=AF.Copy), reads=[B(bk)], writes=["r_qT%d" % ci])
                            for c4 in range(4):
                                P.op("dve", lambda e, c4=c4: e.tensor_tensor(out=qxT[:, ci * 512 + c4 * 128:ci * 512 + (c4 + 1) * 128], in0=banks[bk][:, c4 * 128:(c4 + 1) * 128],
                                                                             in1=xi_t[:, 0, :], op=ALU.mult), reads=[B(bk), "xi_t"], writes=["r_qxT%d" % ci])
                        fm_proj(wt1[1], "wt1_1", uTo, "uTo", range(2), lambda ci: ci, ev_q)

                        if RC < 3:
                            break

                        def ev_k(bk, ci):
                            P.op("act", lambda e: e.activation(out=kT[:, ci * 512:(ci + 1) * 512], in_=banks[bk][:], func=AF.Copy), reads=[B(bk)], writes=["r_kT%d" % ci])
                        fm_proj(wt1[2], "wt1_2", uTo, "uTo", range(2), lambda ci: 2 + ci, ev_k)

                        if RC < 4:
                            break

                        def ev_kv(bk, t, h=h):
                            P.op("dve", lambda e: e.tensor_scalar(out=kz[:, t, :], in0=banks[bk][:, 0:128], scalar1=zt[:, t, h:h + 1], scalar2=None, op0=ALU.mult), reads=[B(bk), "zt"], writes=["r_kz%d" % t])
                            P.op("act", lambda e: e.activation(out=vv[:, t, :], in_=banks[bk][:, 128:384], func=AF.Copy), reads=[B(bk)], writes=["r_v%d" % t])
                        tm_proj(wkv, "r_wkv", 384, uTc, "uTc", range(8), lambda tb: 4 + tb % 2, ev_kv)

                        if RC < 5:
                            break

                        def ev_kv2(bk, tb):
                            ev_kv(bk, 8 + tb)
                        tm_proj(wkv, "r_wkv", 384, uTo, "uTo", range(8), lambda tb: 4 + tb % 2, ev_kv2)

                        if RC < 6:
                            break

                        def ev_g(bk, tb):
                            P.op("act", lambda e: e.activation(out=sg[:, tb, :], in_=banks[bk][:, 0:256], func=AF.Silu), reads=[B(bk)], writes=["r_sg%d" % tb])
                        tm_proj(wgr, "r_wgr", 256, uTo, "uTo", range(8), lambda tb: 6 + tb % 2, ev_g)
                        if os.environ.get("RET_CUT") == "proj":
                            break
                        for t in range(8):
                            P.op("pe", lambda e, t=t: e.matmul(banks[0][:, 0:256], lhsT=kz[:, t, :], rhs=vv[:, t, :], start=(t == 0), stop=(t == 7)),
                                 reads=["r_kz%d" % t, "r_v%d" % t], writes=[B(0)])
                        P.op("dve", lambda e: e.tensor_copy(out=state[:], in_=banks[0][:, 0:256]), reads=[B(0)], writes=["r_state"])
                        P.op("act", lambda e: e.activation(out=stbf[:], in_=banks[0][:, 0:256], func=AF.Copy), reads=[B(0)], writes=["r_stbf"])
                        P.op("dve", lambda e: e.memset(ysum[:], 0.0), writes=["r_ysum"])
                        P.op("dve", lambda e: e.memset(ysq[:], 0.0), writes=["r_ysq"])
                        if os.environ.get("RET_CUT") == "state":
                            break
                        for c in range(8):
                            csl = slice(c * 128, (c + 1) * 128)
                            sb_ = 1 + c % 2
                            yb = 4 + c // 2
                            ysl = slice((c % 2) * 256, (c % 2) * 256 + 256)
                            s2 = c % 2
                            P.op("pe", lambda e, csl=csl, sb_=sb_: e.matmul(banks[sb_][:, 0:128], lhsT=kT[:, csl], rhs=qT[:, csl], start=True, stop=True),
                                 reads=["r_kT%d" % (c // 4), "r_qT%d" % (c // 4)], writes=[B(sb_)])
                            P.op("dve", lambda e, sb_=sb_, s2=s2, h=h: e.tensor_tensor(out=sTb[s2][:], in0=banks[sb_][:, 0:128], in1=decT[:, 0, :], op=ALU.mult),
                                 reads=[B(sb_), "decT"], writes=["r_sTb%d" % s2])
                            P.op("pe", lambda e, yb=yb, ysl=ysl, s2=s2, c=c: e.matmul(banks[yb][:, ysl], lhsT=sTb[s2][:], rhs=vv[:, 8 + c, :], start=True, stop=False),
                                 reads=["r_sTb%d" % s2, "r_v%d" % (8 + c)], writes=[B(yb)])
                            P.op("pe", lambda e, yb=yb, ysl=ysl, csl=csl: e.matmul(banks[yb][:, ysl], lhsT=qxT[:, csl], rhs=stbf[:], start=False, stop=True),
                                 reads=["r_qxT%d" % (c // 4), "r_stbf"], writes=[B(yb)])
                            if c < 7:
                                P.op("pe", lambda e, c=c: e.matmul(banks[3][:, 0:256], lhsT=kz[:, 8 + c, :], rhs=vv[:, 8 + c, :], start=True, stop=True),
                                     reads=["r_kz%d" % (8 + c), "r_v%d" % (8 + c)], writes=[B(3)])
                                P.op("dve", lambda e, h=h: e.scalar_tensor_tensor(out=state[:], in0=state[:], scalar=GC[h], in1=banks[3][:, 0:256], op0=ALU.mult, op1=ALU.add),
                                     reads=[B(3), "r_state"], writes=["r_state"])
                                P.op("act", lambda e: e.activation(out=stbf[:], in_=state[:], func=AF.Copy), reads=["r_state"], writes=["r_stbf"])
                            P.op("act", lambda e, yb=yb, ysl=ysl, c=c: e.activation(out=junk2[:], in_=banks[yb][:, ysl], func=AF.Identity, accum_out=ysum[:, c:c + 1]),
                                 reads=[B(yb)], writes=["r_junk", "r_ysum"])
                            P.op("act", lambda e, yb=yb, ysl=ysl, c=c: e.activation(out=junk2[:], in_=banks[yb][:, ysl], func=AF.Square, accum_out=ysq[:, c:c + 1]),
                                 reads=[B(yb)], writes=["r_junk", "r_ysq"])
                        if os.environ.get("RET_CUT") == "chunks":
                            break
                        P.op("dve", lambda e: e.tensor_scalar_mul(out=mean[:], in0=ysum[:], scalar1=1.0 / RDV), reads=["r_ysum"], writes=["r_mean"])
                        P.op("dve", lambda e: e.tensor_tensor(out=rstd[:], in0=mean[:], in1=mean[:], op=ALU.mult), reads=["r_mean"], writes=["r_rstd"])
                        P.op("dve", lambda e: e.scalar_tensor_tensor(out=rstd[:], in0=ysq[:], scalar=1.0 / RDV, in1=rstd[:], op0=ALU.mult, op1=ALU.subtract), reads=["r_ysq", "r_rstd"], writes=["r_rstd"])
                        P.op("dve", lambda e: e.tensor_scalar_add(out=rstd[:], in0=rstd[:], scalar1=EPS), reads=["r_rstd"], writes=["r_rstd"])
                        P.op("act", lambda e: e.activation(out=rstd[:], in_=rstd[:], func=AF.Sqrt), reads=["r_rstd"], writes=["r_rstd"])
                        P.op("dve", lambda e: e.reciprocal(out=rstd[:], in_=rstd[:]), reads=["r_rstd"], writes=["r_rstd"])
                        for c in range(8):
                            yb = 4 + c // 2
                            ysl = slice((c % 2) * 256, (c % 2) * 256 + 256)
                            s2 = c % 2
                            P.op("dve", lambda e, yb=yb, ysl=ysl, c=c, s2=s2: e.tensor_scalar(out=t1[s2][:], in0=banks[yb][:, ysl], scalar1=mean[:, c:c + 1], scalar2=rstd[:, c:c + 1], op0=ALU.subtract, op1=ALU.mult),
                                 reads=[B(yb), "r_mean", "r_rstd"], writes=["r_t1%d" % s2])
                            P.op("dve", lambda e, s2=s2, h=h: e.tensor_tensor(out=t1[s2][:], in0=t1[s2][:], in1=gnr_t[:, 0, :], op=ALU.mult), reads=["r_t1%d" % s2, "gnr_t"], writes=["r_t1%d" % s2])
                            P.op("dve", lambda e, s2=s2, c=c: e.tensor_tensor(out=ob[s2][:], in0=t1[s2][:], in1=sg[:, c, :], op=ALU.mult), reads=["r_t1%d" % s2, "r_sg%d" % c], writes=["r_ob%d" % s2])
                            for e2 in range(2):
                                P.op("pe", lambda e, s2=s2, e2=e2: e.matmul(banks[0][:, e2 * 128:(e2 + 1) * 128], lhsT=ob[s2][:, e2 * 128:(e2 + 1) * 128], rhs=identb[:], start=True, stop=True),
                                     reads=["r_ob%d" % s2, "identb"], writes=[B(0)])
                            P.op("act", lambda e, h=h, c=c: e.activation(out=orT[:, 2 * h:2 * h + 2, c * 128:(c + 1) * 128], in_=banks[0][:, 0:256].rearrange("p (k t) -> p k t", k=2), func=AF.Copy),
                                 reads=[B(0)], writes=["orT_%d" % c])
                    P.op("sp", lambda e: e.dma_start(out=orT_d, in_=orT[:].rearrange("p k t -> p (k t)")), reads=["orT_%d" % c for c in range(8)], writes=["orT_d"], dma=True, key="st_orT")
                    dump("orT", orT[:].rearrange("p k t -> p (k t)"), ["orT_%d" % c for c in range(8)])
                    P.barrier()
                if stop_after == "ret":
                    P.emit(nc, final_keys=final_keys)
                    return nc

                with ExitStack() as sn:
                    gq = SB(sn, "gq", [128, 4], F32)
                    gqr = SB(sn, "gqr", [128, 4, 128], F32)
                    krows = SB(sn, "krows", [64, 16, 128], BF16)
                    kcrows = SB(sn, "kcrows", [10, 128], BF16)
                    cmask = [SB(sn, "cmask%d" % k, [128, 512], BF16) for k in range(2)]
                    wt2 = [SB(sn, "wt2_0", [128, 16, 256], BF16)]
                    tri = SB(sn, "tri", [128, 2, 512], BF16)
                    vm_t = SB(sn, "vm_t", [128, 8, 32], F32)
                    fb_t = SB(sn, "fb_t", [128, 8, 32], F32)
                    ovb = SB(sn, "ovb", [128, 32], BF16)
                    cposf = SB(sn, "cposf", [128, 64], F32)
                    cposb = SB(sn, "cposb", [128, 64], BF16)
                    cb1_t = SB(sn, "cb1_t", [128, 2], F32)
                    P.op("sp", lambda e: e.dma_start(out=gq[:], in_=gqkT), writes=["gq"], dma=True, key="tbl")
                    P.op("sp", lambda e: e.dma_start(out=gqr[:].rearrange("p a d -> p (a d)"), in_=gqkR), writes=["gqr"], dma=True, key="tbl")
                    P.op("sp", lambda e: e.dma_start(out=krows[:].rearrange("p a d -> p (a d)"), in_=tb_ap["krows"]), writes=["krows"], dma=True, key="tbl")
                    P.op("sp", lambda e: e.dma_start(out=kcrows[:], in_=tb_ap["kcrows"]), writes=["kcrows"], dma=True, key="tbl")
                    P.op("sp", lambda e: e.dma_start(out=tri[:].rearrange("p a d -> p (a d)"), in_=tb_ap["tri"]), writes=["tri"], dma=True, key="tbl")
                    P.op("sp", lambda e: e.dma_start(out=vm_t[:].rearrange("p a d -> p (a d)"), in_=tb_ap["vm"]), writes=["vm_t"], dma=True, key="tbl")
                    P.op("sp", lambda e: e.dma_start(out=fb_t[:].rearrange("p a d -> p (a d)"), in_=tb_ap["fb"]), writes=["fb_t"], dma=True, key="tbl")
                    P.op("sp", lambda e: e.dma_start(out=ovb[:], in_=tb_ap["ov"]), writes=["ovb"], dma=True, key="tbl")
                    P.op("sp", lambda e: e.dma_start(out=cposf[:], in_=cpos), writes=["cposf"], dma=True, key="tbl")
                    P.op("sp", lambda e: e.dma_start(out=cb1_t[:], in_=cb1), writes=["cb1_t"], dma=True, key="tbl")
                    P.op("dve", lambda e: e.tensor_copy(out=cposb[:], in_=cposf[:]), reads=["cposf"], writes=["cposb"])
                    P.op("act", lambda e: e.mul(out=gq[:, 0:1], in_=gq[:, 0:1], mul=float(DH ** -0.5)), reads=["gq"], writes=["gq"])
                    qTg = SB(sn, "qTg", [128, 8, 512], BF16)
                    rawT = [SB(sn, "rawT%d" % k, [128, 2048], BF16) for k in range(2)]
                    kslcT = SB(sn, "kslcT", [128, 2048], BF16)
                    kwinT = SB(sn, "kwinT", [128, 2048], BF16)
                    vslc = SB(sn, "vslc", [128, 16, 129], BF16)
                    vwin = SB(sn, "vwin", [128, 16, 129], BF16)
                    RA = SB(sn, "RA", [64, 8, 512], BF16)
                    w1t_ = SB(sn, "w1t0", [128, 32, 128], BF16)
                    w1t = [w1t_, w1t_]
                    w2t = [SB(sn, "w2t%d" % k, [128, 128], BF16) for k in range(2)]
                    hTs = SB(sn, "hTs", [128, 128], BF16)
                    zf = SB(sn, "zf", [128, 128], F32)
                    zt_ = SB(sn, "zt_", [128, 128], F32)
                    btile = SB(sn, "btile", [128, 1], F32)
                    kcn = SB(sn, "kcn", [128, 128], F32)
                    kcmpT = SB(sn, "kcmpT", [128, 128], BF16)
                    vcmp = SB(sn, "vcmp", [128, 161], BF16)
                    sqt = SB(sn, "sqt", [128, 512], F32)
                    rtt = SB(sn, "rtt", [128, 512], F32)
                    Pb = [SB(sn, "Pb%d" % k, [128, 512], BF16) for k in range(3)]
                    rs4 = SB(sn, "rs4", [128, 4], F32)
                    coef = SB(sn, "coef", [128, 4], F32)
                    imp = SB(sn, "imp", [128, 32], F32)
                    work = SB(sn, "work", [128, 32], F32)
                    top8 = SB(sn, "top8", [128, 16], F32)
                    selb = SB(sn, "selb", [128, 64], F32)
                    otmp = SB(sn, "otmp", [128, 4, 128], F32)
                    obf = SB(sn, "obf", [128, 4, 128], BF16)
                    ss1 = SB(sn, "ss1", [128, 1], F32)
                    P.op("dve", lambda e: e.memset(vslc[:, :, 128:129], 1.0), writes=["vslc_ones"])
                    P.op("dve", lambda e: e.memset(vwin[:, :, 128:129], 1.0), writes=["vwin_ones"])
                    P.op("dve", lambda e: e.memset(selb[:], 0.0), writes=["selb"])

                    def qknorm(bk, gcol, dst_ap, dst_names):
                        P.op("act", lambda e: e.activation(out=sqt[:], in_=banks[bk][:], func=AF.Square), reads=[B(bk)], writes=["sqt"])
                        P.op("pe", lambda e: e.matmul(banks[3][:], lhsT=ones[:], rhs=sqt[:], start=True, stop=True), reads=["ones", "sqt"], writes=[B(3)])
                        P.op("dve", lambda e: e.tensor_scalar(out=rtt[:], in0=banks[3][:], scalar1=1.0 / DH, scalar2=EPS, op0=ALU.mult, op1=ALU.add), reads=[B(3)], writes=["rtt"])
                        P.op("act", lambda e: e.activation(out=rtt[:], in_=rtt[:], func=AF.Sqrt), reads=["rtt"], writes=["rtt"])
                        P.op("dve", lambda e: e.reciprocal(out=rtt[:], in_=rtt[:]), reads=["rtt"], writes=["rtt"])
                        P.op("dve", lambda e: e.scalar_tensor_tensor(out=dst_ap, in0=banks[bk][:], scalar=gq[:, gcol:gcol + 1], in1=rtt[:], op0=ALU.mult, op1=ALU.mult),
                             reads=[B(bk), "gq", "rtt"], writes=dst_names)

                    for g in range(NG):
                        P.op("sp", lambda e, g=g: e.dma_start(out=RA[:].rearrange("p a d -> p (a d)"), in_=tb_ap["rows"][g]), writes=["RA", "RAsel"], dma=True, key="RA")
                        for hh in range(4):
                            wsl = hh % 3
                            load_w(wt1[wsl], "wt1_%d" % wsl, O_Q + (4 * g + hh) * 128, 128)

                            def ev_qn(bk, ci, hh=hh):
                                qknorm(bk, 0, qTg[:, 4 * ci:4 * ci + 4, hh * 128:(hh + 1) * 128], ["qTg"])
                            def ev_qn2(bk, ci, hh=hh):
                                P.op("act", lambda e: e.activation(out=sqt[:], in_=banks[bk][:], func=AF.Square), reads=[B(bk)], writes=["sqt"])
                                P.op("pe", lambda e: e.matmul(banks[3][:], lhsT=ones[:], rhs=sqt[:], start=True, stop=True), reads=["ones", "sqt"], writes=[B(3)])
                                P.op("dve", lambda e: e.tensor_scalar(out=rtt[:], in0=banks[3][:], scalar1=1.0 / DH, scalar2=EPS, op0=ALU.mult, op1=ALU.add), reads=[B(3)], writes=["rtt"])
                                P.op("act", lambda e: e.activation(out=rtt[:], in_=rtt[:], func=AF.Sqrt), reads=["rtt"], writes=["rtt"])
                                P.op("dve", lambda e: e.reciprocal(out=rtt[:], in_=rtt[:]), reads=["rtt"], writes=["rtt"])
                                P.op("dve", lambda e: e.scalar_tensor_tensor(out=qTg[:, 4 * ci:4 * ci + 4, hh * 128:(hh + 1) * 128], in0=banks[bk][:].rearrange("p (a q) -> p a q", a=4), scalar=gq[:, 0:1],
                                                                             in1=rtt[:].rearrange("p (a q) -> p a q", a=4), op0=ALU.mult, op1=ALU.mult),
                                     reads=[B(bk), "gq", "rtt"], writes=["qTg"])
                            fm_proj(wt1[wsl], "wt1_%d" % wsl, uTo, "uTo", range(2), lambda ci: ci, ev_qn2)
                        for slot, kind in ((0, "raw0"), (1, "raw1"), (2, "kslc"), (4, "kwin")):
                            wsl = slot % 3
                            load_w(wt1[wsl], "wt1_%d" % wsl, O_KV + slot * 512 + g * 128, 128)
                            for src, uname, base in ((uTc, "uTc", 0), (uTo, "uTo", 2)):
                                def ev_f(bk, ci, kind=kind, base=base):
                                    fsl = slice((base + ci) * 512, (base + ci + 1) * 512)
                                    if kind == "raw0":
                                        P.op("act", lambda e: e.activation(out=rawT[0][:, fsl], in_=banks[bk][:], func=AF.Copy), reads=[B(bk)], writes=["rawT0"])
                                    elif kind == "raw1":
                                        P.op("act", lambda e: e.activation(out=rawT[1][:, fsl], in_=banks[bk][:], func=AF.Copy), reads=[B(bk)], writes=["rawT1"])
                                    elif kind == "kslc":
                                        qknorm(bk, 2, kslcT[:, fsl], ["kslcT"])
                                    else:
                                        qknorm(bk, 3, kwinT[:, fsl], ["kwinT"])
                                fm_proj(wt1[wsl], "wt1_%d" % wsl, src, uname, range(2), lambda ci: ci, ev_f)
                        ws2 = 0
                        load_w(wt2[ws2], "wt2_%d" % ws2, O_KV + 3 * 512 + g * 128, 128, 0)
                        load_w(wt2[ws2], "wt2_%d" % ws2, O_KV + 5 * 512 + g * 128, 128, 128)
                        for src, uname, base in ((uTc, "uTc", 0), (uTo, "uTo", 8)):
                            def ev_v(bk, tb, base=base):
                                t = base + tb
                                P.op("act", lambda e: e.activation(out=vslc[:, t, 0:128], in_=banks[bk][:, 0:128], func=AF.Copy), reads=[B(bk)], writes=["vslc%d" % t])
                                P.op("dve", lambda e: e.tensor_copy(out=vwin[:, t, 0:128], in_=banks[bk][:, 128:256]), reads=[B(bk)], writes=["vwin%d" % t])
                            tm_proj(wt2[ws2], "wt2_%d" % ws2, 256, src, uname, range(8), lambda tb: tb % 2, ev_v)
                        for i in range(2):
                            P.op("pool", lambda e, i=i: e.dma_start(out=w1t[i][:], in_=cw1[i].rearrange("(r d) j -> d r j", d=128)), writes=["w1t0"], dma=True)
                            P.op("pool", lambda e, i=i: e.dma_start(out=w2t[i][:], in_=cw2[i]), writes=["w2t%d" % i], dma=True)
                            for r in range(32):
                                P.op("pe", lambda e, i=i, r=r: e.matmul(banks[0][:, 0:127], lhsT=w1t[i][:, r, :], rhs=rawT[i][:, r:r + 16 * 126 + 1:16], start=(r == 0), stop=(r == 31)),
                                     reads=["w1t0", "rawT%d" % i], writes=[B(0)])
                            for r in range(32):
                                P.op("pe", lambda e, i=i, r=r: e.matmul(banks[1][:, 0:1], lhsT=w1t[i][:, r, :], rhs=cposb[:, i * 32 + r:i * 32 + r + 1], start=(r == 0), stop=(r == 31)),
                                     reads=["w1t0", "cposb"], writes=[B(1)])
                            P.op("dve", lambda e, i=i: e.tensor_tensor(out=btile[:], in0=banks[1][:, 0:1], in1=cb1_t[:, i:i + 1], op=ALU.add), reads=[B(1), "cb1_t"], writes=["btile"])
                            P.op("dve", lambda e: e.memset(zf[:], 0.0), writes=["zf"])
                            P.op("act", lambda e: e.activation(out=zf[:, 0:127], in_=banks[0][:, 0:127], func=AF.Identity, bias=btile[:, 0:1]), reads=[B(0), "btile", "zf"], writes=["zf"])
                            P.op("dve", lambda e: e.tensor_tensor(out=zt_[:], in0=zf[:], in1=zf[:], op=ALU.mult), reads=["zf"], writes=["zt_"])
                            P.op("dve", lambda e: e.tensor_tensor(out=zt_[:], in0=zt_[:], in1=zf[:], op=ALU.mult), reads=["zf", "zt_"], writes=["zt_"])
                            P.op("dve", lambda e: e.scalar_tensor_tensor(out=zt_[:], in0=zt_[:], scalar=0.044715, in1=zf[:], op0=ALU.mult, op1=ALU.add), reads=["zf", "zt_"], writes=["zt_"])
                            P.op("act", lambda e: e.activation(out=zt_[:], in_=zt_[:], func=AF.Tanh, scale=float(np.sqrt(2.0 / np.pi))), reads=["zt_"], writes=["zt_"])
                            P.op("dve", lambda e: e.scalar_tensor_tensor(out=zt_[:], in0=zt_[:], scalar=1.0, in1=zf[:], op0=ALU.add, op1=ALU.mult), reads=["zf", "zt_"], writes=["zt_"])
                            P.op("act", lambda e: e.mul(out=hTs[:], in_=zt_[:], mul=0.5), reads=["zt_"], writes=["hTs"])
                            P.op("pe", lambda e, i=i: e.matmul(banks[2][:, 0:128], lhsT=hTs[:], rhs=w2t[i][:], start=True, stop=True), reads=["hTs", "w2t%d" % i], writes=[B(2)])
                            if i == 0:
                                P.op("dve", lambda e: e.memset(ss1[:], 0.0), writes=["ss1"])
                                P.op("act", lambda e: e.activation(out=kcn[:], in_=banks[2][:, 0:128], func=AF.Square, accum_out=ss1[:]), reads=[B(2), "ss1"], writes=["kcn", "ss1"])
                                P.op("dve", lambda e: e.tensor_scalar(out=ss1[:], in0=ss1[:], scalar1=1.0 / DH, scalar2=EPS, op0=ALU.mult, op1=ALU.add), reads=["ss1"], writes=["ss1"])
                                P.op("act", lambda e: e.activation(out=ss1[:], in_=ss1[:], func=AF.Sqrt), reads=["ss1"], writes=["ss1"])
                                P.op("dve", lambda e: e.reciprocal(out=ss1[:], in_=ss1[:]), reads=["ss1"], writes=["ss1"])
                                P.op("dve", lambda e: e.scalar_tensor_tensor(out=kcn[:], in0=banks[2][:, 0:128], scalar=ss1[:, 0:1], in1=gqr[:, 1, :], op0=ALU.mult, op1=ALU.mult),
                                     reads=[B(2), "ss1", "gqr", "kcn"], writes=["kcn"])
                                P.op("pe", lambda e: e.transpose(banks[3][:, 0:128], kcn[:], ident[:]), reads=["kcn", "ident"], writes=[B(3)])
                                P.op("act", lambda e: e.activation(out=kcmpT[:], in_=banks[3][:, 0:128], func=AF.Copy), reads=[B(3)], writes=["kcmpT"])
                                P.op("dve", lambda e: e.memset(kcmpT[:, 127:128], 0.0), reads=["kcmpT"], writes=["kcmpT"])
                            else:
                                P.op("dve", lambda e: e.memset(vcmp[:], 0.0), writes=["vcmp"])
                                P.op("act", lambda e: e.activation(out=vcmp[0:127, 0:128], in_=banks[2][0:127, 0:128], func=AF.Copy), reads=[B(2), "vcmp"], writes=["vcmp"])
                                P.op("dve", lambda e: e.memset(vcmp[0:127, 128:129], 1.0), reads=["vcmp"], writes=["vcmp"])
                                P.op("dve", lambda e: e.tensor_copy(out=vcmp[0:127, 129:161], in_=ovb[0:127, :]), reads=["vcmp", "ovb"], writes=["vcmp"])
                        scount = [0]

                        def score_tile(kT_ap, k_names, qb, extra, pb_i):
                            bk = scount[0] % 3
                            scount[0] += 1
                            n = len(extra)
                            P.op("pe", lambda e: e.matmul(banks[bk][:], lhsT=kT_ap, rhs=qTg[:, qb, :], start=True, stop=(n == 0)), reads=k_names + ["qTg"], writes=[B(bk)])
                            for ii, (l_ap, r_ap, names) in enumerate(extra):
                                P.op("pe", lambda e, l_ap=l_ap, r_ap=r_ap, ii=ii: e.matmul(banks[bk][:], lhsT=l_ap, rhs=r_ap, start=False, stop=(ii == n - 1)), reads=names, writes=[B(bk)])
                            P.op("act", lambda e: e.activation(out=Pb[pb_i][:], in_=banks[bk][:], func=AF.Exp), reads=[B(bk)], writes=["Pb%d" % pb_i])

                        pcount = [0]
                        accset = [0]
                        pending_finish = [None]
                        for qb in range(8):
                            gcol = lambda br, hh: br * 16 + 4 * g + hh
                            a0 = 4 + 2 * (accset[0] % 2)
                            accset[0] += 1
                            pi = pcount[0] % 3
                            pcount[0] += 1
                            cmi = qb % 2
                            P.op("sp", lambda e, qb=qb, cmi=cmi: e.dma_start(out=cmask[cmi][:], in_=tb_ap["cmask"][:, qb * 512:(qb + 1) * 512]), writes=["cmask%d" % cmi], dma=True)
                            score_tile(kcmpT[:], ["kcmpT"], qb,
                                       [(kcrows[0:10, :], RA[0:10, qb, :], ["kcrows", "RA"]), (identb[:], cmask[cmi][:], ["identb", "cmask%d" % cmi])], pi)
                            for hh in range(4):
                                bk = a0 + hh // 2
                                o0 = (hh % 2) * 161
                                P.op("pe", lambda e, bk=bk, o0=o0, hh=hh, pi=pi: e.matmul(banks[bk][:, o0:o0 + 161], lhsT=Pb[pi][:, hh * 128:(hh + 1) * 128], rhs=vcmp[:], start=True, stop=True),
                                     reads=["Pb%d" % pi, "vcmp"], writes=[B(bk)])
                            if pending_finish[0] is not None:
                                pending_finish[0]()
                                pending_finish[0] = None
                            for hh in range(4):
                                bk = a0 + hh // 2
                                o0 = (hh % 2) * 161
                                P.op("dve", lambda e, bk=bk, o0=o0, hh=hh: e.tensor_scalar_max(out=rs4[:, hh:hh + 1], in0=banks[bk][:, o0 + 128:o0 + 129], scalar1=1e-30), reads=[B(bk)], writes=["rs4"])
                            P.op("dve", lambda e: e.reciprocal(out=rs4[:], in_=rs4[:]), reads=["rs4"], writes=["rs4"])
                            for hh in range(4):
                                bk = a0 + hh // 2
                                o0 = (hh % 2) * 161
                                if hh == 0:
                                    P.op("dve", lambda e, bk=bk, o0=o0: e.tensor_scalar(out=imp[:], in0=banks[bk][:, o0 + 129:o0 + 161], scalar1=rs4[:, 0:1], scalar2=None, op0=ALU.mult), reads=[B(bk), "rs4"], writes=["imp"])
                                else:
                                    P.op("dve", lambda e, bk=bk, o0=o0, hh=hh: e.scalar_tensor_tensor(out=imp[:], in0=banks[bk][:, o0 + 129:o0 + 161], scalar=rs4[:, hh:hh + 1], in1=imp[:], op0=ALU.mult, op1=ALU.add),
                                         reads=[B(bk), "rs4", "imp"], writes=["imp"])
                            P.op("dve", lambda e, qb=qb: e.tensor_tensor(out=imp[:], in0=imp[:], in1=vm_t[:, qb, :], op=ALU.mult), reads=["imp", "vm_t"], writes=["imp"])
                            P.op("dve", lambda e, qb=qb: e.tensor_tensor(out=imp[:], in0=imp[:], in1=fb_t[:, qb, :], op=ALU.add), reads=["imp", "fb_t"], writes=["imp"])
                            P.op("dve", lambda e: e.max(out=top8[:, 0:8], in_=imp[:]), reads=["imp"], writes=["top8"])
                            P.op("dve", lambda e: e.match_replace(out=work[:], in_to_replace=top8[:, 0:8], in_values=imp[:], imm_value=-1e30), reads=["imp", "top8"], writes=["work"])
                            P.op("dve", lambda e: e.max(out=top8[:, 8:16], in_=work[:]), reads=["work", "top8"], writes=["top8"])
                            P.op("dve", lambda e: e.tensor_scalar(out=selb[:, 32:64], in0=imp[:], scalar1=top8[:, 15:16], scalar2=None, op0=ALU.is_ge), reads=["imp", "top8", "selb"], writes=["selb"])
                            P.op("dve", lambda e: e.tensor_scalar(out=selb[:, 32:64], in0=selb[:, 32:64], scalar1=-1.0, scalar2=-NEG, op0=ALU.add, op1=ALU.mult), reads=["selb"], writes=["selb"])
                            def sel_finish(qb=qb):
                                P.op("pe", lambda e: e.transpose(banks[3][0:64, 0:128], selb[:], ident[:]), reads=["selb", "ident"], writes=[B(3)])
                                P.op("dve", lambda e, qb=qb: e.tensor_copy(out=RA[32:64, qb, :].rearrange("p (a q) -> p a q", a=4), in_=banks[3][32:64, 0:128].unsqueeze(1).to_broadcast([32, 4, 128])),
                                     reads=[B(3), "RAsel"], writes=["RAsel"])
                            for hh in range(4):
                                P.op("dve", lambda e, hh=hh, qb=qb, gc=gcol(0, hh): e.tensor_tensor(out=coef[:, hh:hh + 1], in0=rs4[:, hh:hh + 1], in1=gates[:, qb, gc:gc + 1], op=ALU.mult), reads=["rs4", "gates", "coef"], writes=["coef"])
                            for hh in range(4):
                                bk = a0 + hh // 2
                                o0 = (hh % 2) * 161
                                P.op("dve", lambda e, bk=bk, o0=o0, hh=hh: e.tensor_scalar(out=otmp[:, hh, :], in0=banks[bk][:, o0:o0 + 128], scalar1=coef[:, hh:hh + 1], scalar2=None, op0=ALU.mult),
                                     reads=[B(bk), "coef", "otmp"], writes=["otmp"])
                            tiles = []
                            for br in (2, 1):
                                a0 = 4 + 2 * (accset[0] % 2)
                                accset[0] += 1
                                kbs = list(range(0, 9 + qb)) if br == 1 else list(range(4 + qb, 9 + qb))
                                for kb in kbs:
                                    tiles.append((br, kb, a0, kb == kbs[0], kb == kbs[-1]))

                            def emit_score(t):
                                br, kb, a0, first, last = t
                                Dd = 8 + qb - kb
                                pi = pcount[0] % 3
                                pcount[0] += 1
                                ksl = slice(kb * 128, (kb + 1) * 128)
                                if br == 1:
                                    extra = [(krows[0:64, kb, :], RA[0:64, qb, :], ["krows", "RA", "RAsel"])]
                                    if Dd == 0:
                                        extra.append((identb[:], tri[:, 0, :], ["identb", "tri"]))
                                    score_tile(kslcT[:, ksl], ["kslcT"], qb, extra, pi)
                                else:
                                    extra = [(krows[0:10, kb, :], RA[0:10, qb, :], ["krows", "RA"])]
                                    if Dd == 0:
                                        extra.append((identb[:], tri[:, 0, :], ["identb", "tri"]))
                                    if Dd == 4:
                                        extra.append((identb[:], tri[:, 1, :], ["identb", "tri"]))
                                    score_tile(kwinT[:, ksl], ["kwinT"], qb, extra, pi)
                                return pi

                            def emit_pv(t, pi):
                                br, kb, a0, first, last = t
                                vt, vn = (vslc, "vslc%d" % kb) if br == 1 else (vwin, "vwin%d" % kb)
                                for hh in range(4):
                                    bk = a0 + hh // 2
                                    o0 = (hh % 2) * 161
                                    P.op("pe", lambda e, bk=bk, o0=o0, hh=hh, pi=pi, vt=vt, kb=kb, st_=(first and hh % 2 == 0), sp_=(last and hh % 2 == 1): e.matmul(banks[bk][:, o0:o0 + 129], lhsT=Pb[pi][:, hh * 128:(hh + 1) * 128], rhs=vt[:, kb, :], start=st_, stop=sp_),
                                         reads=["Pb%d" % pi, vn, ("vslc_ones" if br == 1 else "vwin_ones")], writes=[B(bk)])
                                if not last:
                                    return
                                for bq in range(2):
                                    bk = a0 + bq
                                    P.op("dve", lambda e, bk=bk, bq=bq: e.tensor_scalar_max(out=rs4[:, 2 * bq:2 * bq + 2], in0=banks[bk][:, 128:128 + 162:161], scalar1=1e-30), reads=[B(bk), "rs4"], writes=["rs4"])
                                P.op("dve", lambda e: e.reciprocal(out=rs4[:], in_=rs4[:]), reads=["rs4"], writes=["rs4"])
                                gc0 = br * 16 + 4 * g
                                P.op("dve", lambda e, gc0=gc0, qb=qb: e.tensor_tensor(out=coef[:], in0=rs4[:], in1=gates[:, qb, gc0:gc0 + 4], op=ALU.mult), reads=["rs4", "gates", "coef"], writes=["coef"])
                                for hh in range(4):
                                    bk = a0 + hh // 2
                                    o0 = (hh % 2) * 161
                                    dst = otmp[:, hh, :] if br == 2 else obf[:, hh, :]
                                    dn = "otmp" if br == 2 else "obf"
                                    P.op("dve", lambda e, bk=bk, o0=o0, hh=hh, dst=dst: e.scalar_tensor_tensor(out=dst, in0=banks[bk][:, o0:o0 + 128], scalar=coef[:, hh:hh + 1], in1=otmp[:, hh, :], op0=ALU.mult, op1=ALU.add),
                                         reads=[B(bk), "coef", "otmp", dn], writes=[dn])

                            pis = [emit_score(tiles[0])]
                            for ti in range(len(tiles)):
                                if ti + 1 < len(tiles):
                                    if tiles[ti + 1][0] == 1 and tiles[ti][0] == 2:
                                        sel_finish()
                                    pis.append(emit_score(tiles[ti + 1]))
                                emit_pv(tiles[ti], pis[ti])
                            def finish(qb=qb, g=g):
                                for hh in range(4):
                                    P.op("pe", lambda e, hh=hh: e.matmul(banks[3][:, hh * 128:(hh + 1) * 128], lhsT=obf[:, hh, :], rhs=identb[:], start=(hh == 0), stop=(hh == 3)), reads=["obf", "identb"], writes=[B(3)])
                                P.op("act", lambda e: e.activation(out=onT[:, 4 * g:4 * g + 4, qb * 128:(qb + 1) * 128], in_=banks[3][:].rearrange("p (k t) -> p k t", k=4), func=AF.Copy),
                                     reads=[B(3)], writes=["onT_%d" % qb])
                            pending_finish[0] = finish
                        pending_finish[0]()
                        pending_finish[0] = None
                        P.barrier()
                dump("onT", onT[:].rearrange("p k t -> p (k t)"), ["onT_%d" % c for c in range(8)])
            P.barrier()
            if stop_after == "nsa":
                P.emit(nc, final_keys=final_keys)
                return nc

            with ExitStack() as sg_:
                merged = SB(sg_, "merged", [128, 8, D], F32)
                orT2 = SB(sg_, "orT2", [128, 16, 1024], BF16)
                for tb in range(8):
                    P.op("sp", lambda e, tb=tb: e.dma_start(out=orT2[:, :, tb * 128:(tb + 1) * 128], in_=orT_d.rearrange("p (k t) -> p k t", k=16)[:, :, tb * 128:(tb + 1) * 128]),
                         reads=["orT_d"], writes=["orT_%d" % tb], dma=True, key="ld_orT")
                wpa = [SB(sg_, "wpa%d" % k, [128, 16, 256], BF16) for k in range(4)]
                sgt = [SB(sg_, "sgt%d" % k, [128, 256], F32) for k in range(2)]
                mt = [SB(sg_, "mt%d" % k, [128, 256], F32) for k in range(2)]
                for ph, (Wp, oT_, on_, gcol0) in enumerate(((wpn, onT, "onT", O_GA), (wpr, orT2, "orT", O_GB))):
                    Wpv = Wp.rearrange("(k p) n -> p k n", p=128)
                    for fb8 in range(8):
                        fsl = slice(fb8 * 256, (fb8 + 1) * 256)
                        wi = 2 * (fb8 % 2)
                        P.op("pool", lambda e, fsl=fsl, Wpv=Wpv, wi=wi: e.dma_start(out=wpa[wi][:], in_=Wpv[:, :, fsl]), writes=["wpa%d" % wi], dma=True)
                        P.op("pool", lambda e, fb8=fb8, gcol0=gcol0, wi=wi: e.dma_start(out=wpa[wi + 1][:], in_=Win[:, :, gcol0 + fb8 * 256:gcol0 + (fb8 + 1) * 256]), writes=["wpa%d" % (wi + 1)], dma=True)
                        for tb in range(8):
                            s2 = tb % 2
                            bp, bg_ = 2 * s2, 2 * s2 + 1
                            for k in range(16):
                                P.op("pe", lambda e, k=k, bp=bp, tb=tb, oT_=oT_, wi=wi: e.matmul(banks[bp][:, 0:256], lhsT=oT_[:, k, tb * 128:(tb + 1) * 128], rhs=wpa[wi][:, k, :], start=(k == 0), stop=(k == 15)),
                                     reads=["%s_%d" % (on_, tb), "wpa%d" % wi], writes=[B(bp)])
                            for k in range(16):
                                P.op("pe", lambda e, k=k, bg_=bg_, tb=tb, wi=wi: e.matmul(banks[bg_][:, 0:256], lhsT=uTo[:, k, tb * 128:(tb + 1) * 128], rhs=wpa[wi + 1][:, k, :], start=(k == 0), stop=(k == 15)),
                                     reads=["uTo_%d" % tb, "wpa%d" % (wi + 1)], writes=[B(bg_)])
                            P.op("act", lambda e, bg_=bg_, s2=s2: e.activation(out=sgt[s2][:], in_=banks[bg_][:, 0:256], func=AF.Sigmoid), reads=[B(bg_)], writes=["sgt%d" % s2])
                            if ph == 0:
                                P.op("dve", lambda e, bp=bp, s2=s2, tb=tb, fsl=fsl: e.tensor_tensor(out=merged[:, tb, fsl], in0=sgt[s2][:], in1=banks[bp][:, 0:256], op=ALU.mult),
                                     reads=["sgt%d" % s2, B(bp)], writes=["mg%d_%d" % (tb, fb8 // 2)])
                            else:
                                P.op("dve", lambda e, bp=bp, s2=s2: e.tensor_tensor(out=mt[s2][:], in0=sgt[s2][:], in1=banks[bp][:, 0:256], op=ALU.mult),
                                     reads=["sgt%d" % s2, B(bp)], writes=["mt%d" % s2])
                                P.op("dve", lambda e, s2=s2, tb=tb, fsl=fsl: e.tensor_tensor(out=merged[:, tb, fsl], in0=merged[:, tb, fsl], in1=mt[s2][:], op=ALU.add),
                                     reads=["mt%d" % s2, "mg%d_%d" % (tb, fb8 // 2)], writes=["mg%d_%d" % (tb, fb8 // 2)])
                tcount = 0
                for tb in range(8):
                    for kq in range(4):
                        bk = 4 + tcount % 4
                        tcount += 1
                        for kk in range(4):
                            P.op("pe", lambda e, bk=bk, kk=kk, kq=kq, tb=tb: e.transpose(banks[bk][:, kk * 128:(kk + 1) * 128], merged[:, tb, (4 * kq + kk) * 128:(4 * kq + kk + 1) * 128], ident[:]),
                                 reads=["mg%d_%d" % (tb, kq), "ident"], writes=[B(bk)])
                        if kq % 2 == 0:
                            P.op("act", lambda e, bk=bk, kq=kq, tb=tb: e.activation(out=onT[:, 4 * kq:4 * kq + 4, tb * 128:(tb + 1) * 128], in_=banks[bk][:].rearrange("p (k t) -> p k t", k=4), func=AF.Copy),
                                 reads=[B(bk)], writes=["onT_%d" % tb])
                        else:
                            P.op("dve", lambda e, bk=bk, kq=kq, tb=tb: e.tensor_copy(out=onT[:, 4 * kq:4 * kq + 4, tb * 128:(tb + 1) * 128], in_=banks[bk][:].rearrange("p (k t) -> p k t", k=4)),
                                 reads=[B(bk)], writes=["onT_%d" % tb])
            P.barrier()
            with ExitStack() as so:
                x_sb2 = SB(so, "x_sb2", [128, 8, D], F32)
                GT = SB(so, "GT2", [128, D], F32)
                wot = [SB(so, "wot%d" % k, [128, 16, 512], BF16) for k in range(2)]
                rt = [SB(so, "rt2_%d" % k, [128, 512], F32) for k in range(2)]
                for tb in range(8):
                    P.op("sp", lambda e, tb=tb: e.dma_start(out=x_sb2[:, tb, :], in_=x1_d[:, tb * D:(tb + 1) * D]), reads=["x1_d"], writes=["x%d_%d" % (tb, f) for f in range(4)], dma=True, key="ldx")
                P.op("sp", lambda e: e.dma_start(out=GT[:], in_=ada_d[:, 5 * D:6 * D]), reads=ada_names(5), writes=["GT2"], dma=True)
                Wov = wo.rearrange("(k p) n -> p k n", p=128)
                for fbk in range(4):
                    fsl = slice(fbk * 512, (fbk + 1) * 512)
                    ws = fbk % 2
                    P.op("pool", lambda e, fsl=fsl, ws=ws: e.dma_start(out=wot[ws][:], in_=Wov[:, :, fsl]), writes=["wot%d" % ws], dma=True)
                    for tb in range(8):
                        s2 = tb % 2
                        for k in range(16):
                            P.op("pe", lambda e, k=k, s2=s2, tb=tb, ws=ws: e.matmul(banks[s2][:], lhsT=onT[:, k, tb * 128:(tb + 1) * 128], rhs=wot[ws][:, k, :], start=(k == 0), stop=(k == 15)),
                                 reads=["onT_%d" % tb, "wot%d" % ws], writes=[B(s2)])
                        P.op("dve", lambda e, s2=s2, fsl=fsl: e.tensor_tensor(out=rt[s2][:], in0=banks[s2][:], in1=GT[:, fsl], op=ALU.mult), reads=[B(s2), "GT2"], writes=["rt2_%d" % s2])
                        P.op("dve", lambda e, s2=s2, tb=tb, fsl=fsl: e.tensor_tensor(out=x_sb2[:, tb, fsl], in0=x_sb2[:, tb, fsl], in1=rt[s2][:], op=ALU.add),
                             reads=["rt2_%d" % s2, "x%d_%d" % (tb, fbk)], writes=["x%d_%d" % (tb, fbk)])
                P.op("sp", lambda e: e.dma_start(out=x1_d, in_=x_sb2[:].rearrange("p t d -> p (t d)")), reads=xnames, writes=["x1_d"], dma=True, key="st_x1")
                dump("x2", x_sb2[:].rearrange("p t d -> p (t d)"), xnames)
            P.barrier()
        P.barrier()
        if stop_after == "mix":
            P.emit(nc, final_keys=final_keys)
            return nc

        with ExitStack() as sx:
            x_sb3 = SB(sx, "x_sb3", [128, 8, D], F32)
            hT3 = SB(sx, "hT3", [128, 16, 1024], BF16)
            for tb in range(8):
                P.op("sp", lambda e, tb=tb: e.dma_start(out=x_sb3[:, tb, :], in_=x1_d[:, tb * D:(tb + 1) * D]), reads=["x1_d"], writes=["x%d_%d" % (tb, f) for f in range(4)], dma=True, key="ldx")
            norm_mod_T(x_sb3, 2, hT3, "hT")
            ffn_core(x_sb3, hT3, 2, 1)
            ov = out.rearrange("(t p) d -> p t d", p=128)
            for tb in range(8):
                P.op("sp", lambda e, tb=tb: e.dma_start(out=ov[:, tb, :], in_=x_sb3[:, tb, :]), reads=["x%d_%d" % (tb, f) for f in range(4)], writes=["out%d" % tb], dma=True, key="st_out")
            final_keys.append("st_out")
        P.emit(nc, final_keys=final_keys)
    return nc


def prep_inputs(inp, n_pairs=4):
    f32 = np.float32
    x = np.asarray(inp["x"], f32)
    rep = lambda v: np.ascontiguousarray(np.broadcast_to(np.asarray(v, f32).reshape(1, -1), (128, np.asarray(v).size)))
    shared = {
        "w_ada": np.ascontiguousarray(np.asarray(inp["w_ada"], f32)[0]),
        "b_ada": rep(inp["b_ada"][0]),
        "gnorm": rep(inp["g_norm"][0]),
        "wg": np.ascontiguousarray(np.asarray(inp["w_ffn_gate"], f32)[0]),
        "wu": np.ascontiguousarray(np.asarray(inp["w_ffn_up"], f32)[0]),
        "wd": np.ascontiguousarray(np.asarray(inp["w_ffn_down"], f32)[0]),
        "w_in": np.ascontiguousarray(np.asarray(inp["w_in"], f32)[0]),
        "gqkT": np.ascontiguousarray(np.asarray(inp["g_qk"], f32)[0].T),
        "gqkR": rep(inp["g_qk"][0]),
        "cpos": np.ascontiguousarray(np.asarray(inp["cmp_pos"], f32)[0].transpose(2, 0, 1).reshape(128, 64)),
        "cw1": np.ascontiguousarray(np.asarray(inp["cmp_w1"], f32)[0]),
        "cb1": np.ascontiguousarray(np.asarray(inp["cmp_b1"], f32)[0].T),
        "cw2": np.ascontiguousarray(np.asarray(inp["cmp_w2"], f32)[0]),
        "gnr": rep(inp["ret_gn_gain"][0]),
        "wpn": np.ascontiguousarray(np.asarray(inp["w_proj_nsa"], f32)[0]),
        "wpr": np.ascontiguousarray(np.asarray(inp["w_proj_ret"], f32)[0]),
        "wo": np.ascontiguousarray(np.asarray(inp["w_out"], f32)[0]),
    }
    tabs = [make_tables(0), make_tables(1)]
    maps = []
    cvec = np.asarray(inp["c"], f32)
    for b in range(n_pairs):
        for j in range(2):
            m = dict(shared)
            if j == 0:
                fr = np.concatenate([np.zeros((1024, D), f32), x[b, :1024]], 0)
            else:
                fr = x[b]
            m["xf"] = np.ascontiguousarray(fr)
            m["c_l"] = np.ascontiguousarray(cvec[b].reshape(16, 128).T)
            for name, shape, dt in TABLE_SPECS:
                m["t_" + name] = np.ascontiguousarray(tabs[j][name]).reshape(shape)
            maps.append(m)
    return maps


def kernel(**inputs):
    nc = build()
    maps = prep_inputs(inputs)
    res = run_bass_kernel_spmd(nc, maps, core_ids=list(range(8)))
    outp = np.zeros((4, 2048, D), np.float32)
    for b in range(4):
        for j in range(2):
            outp[b, j * 1024:(j + 1) * 1024] = np.asarray(res.results[2 * b + j]["out"]).reshape(1024, D)
    return outp
```

```python
import contextlib
import os
from contextlib import ExitStack
import numpy as np
import ml_dtypes
import concourse.bass as bass
import concourse.mybir as mybir
from concourse.bass_utils import run_bass_kernel_spmd

F32 = mybir.dt.float32
BF16 = mybir.dt.bfloat16
ALU = mybir.AluOpType
AF = mybir.ActivationFunctionType

ENGS = ("pe", "act", "dve", "pool", "sp")
SAME_ENGINE_SYNC = True

D = 2048
DFF = 5632
NH = 16
NG = 4
DH = 128
RH = 8
RDK = 128
RDV = 256
EPS = 1e-6
NEG = -30000.0
IN_SPLITS = (2048, 3072, 48, 1024, 1024, 2048, 2048, 2048, 2048)
OFF = np.concatenate([[0], np.cumsum(IN_SPLITS)]).tolist()
O_Q, O_KV, O_GL, O_RQ, O_RK, O_RV, O_RG, O_GA, O_GB = OFF[:9]
NIN = OFF[9]


class Op:
    __slots__ = ("eng", "fn", "reads", "writes", "dma", "key", "deps", "sig", "sigidx", "dmacount", "idx", "waw", "dneed")


class Prog:
    def __init__(self):
        self.ops = []
        self.last_write = {}
        self.reads_since = {}
        self.dma_count = {}
        self.bar = set()
        self.last_eng = {}
        self.last_dma = {}

    def barrier(self):
        self.bar = set(self.last_eng.values()) | set(self.last_dma.values())

    def op(self, eng, fn, reads=(), writes=(), dma=False, key=None):
        o = Op()
        o.eng, o.fn, o.dma = eng, fn, dma
        o.reads, o.writes = tuple(reads), tuple(writes)
        o.idx = len(self.ops)
        o.sig = False
        o.sigidx = None
        o.waw = set()
        o.dneed = {}
        deps = set(self.bar)
        for r in o.reads:
            w = self.last_write.get(r)
            if w is not None:
                deps.add(w)
            if r.startswith("bank"):
                for rd in self.reads_since.get(r, ()):
                    if self.ops[rd].eng != eng:
                        deps.add(rd)
        for r in o.writes:
            w = self.last_write.get(r)
            if w is not None:
                deps.add(w)
                o.waw.add(w)
            for rd in self.reads_since.get(r, ()):
                deps.add(rd)
        o.deps = deps
        for d_ in deps:
            p_ = self.ops[d_]
            if p_.dma:
                o.dneed[p_.key] = p_.dmacount
        for r in list(o.reads) + list(o.writes):
            for d_ in [self.last_write.get(r)] + list(self.reads_since.get(r, ())):
                if d_ is not None and self.ops[d_].dma:
                    o.dneed[self.ops[d_].key] = self.dma_count[self.ops[d_].key]
        if dma:
            o.key = key if key is not None else o.writes[0]
            self.dma_count[o.key] = self.dma_count.get(o.key, 0) + 1
            o.dmacount = self.dma_count[o.key]
            self.last_dma[o.key] = o.idx
        else:
            o.key = None
            o.dmacount = 0
            self.last_eng[eng] = o.idx
        for r in o.reads:
            self.reads_since.setdefault(r, []).append(o.idx)
        for r in o.writes:
            self.last_write[r] = o.idx
            self.reads_since[r] = []
        self.ops.append(o)
        return o

    def emit(self, nc, final_keys=(), final_eng="sp"):
        ops = self.ops
        for o in ops:
            nd = set()
            for d in o.deps:
                p = ops[d]
                if p.dma:
                    if o.dma and p.key == o.key and d in o.waw:
                        continue
                    nd.add(d)
                else:
                    if p.eng == o.eng and not o.dma:
                        if p.eng == "pe":
                            continue
                        if not SAME_ENGINE_SYNC:
                            continue
                    nd.add(d)
            best = {}
            for d in nd:
                p = ops[d]
                k = ("d", p.key) if p.dma else ("e", p.eng)
                if k not in best or best[k] < d:
                    best[k] = d
            o.deps = set(best.values())
            for d in o.deps:
                if not ops[d].dma:
                    ops[d].sig = True
        cnt = {e: 0 for e in ENGS}
        for o in ops:
            if o.sig and not o.dma:
                cnt[o.eng] += 1
                o.sigidx = cnt[o.eng]
        with ExitStack() as st:
            esem = {e: st.enter_context(nc.semaphore("s_" + e)) for e in ENGS}
            dsem = {}
            for k in self.dma_count:
                dsem[k] = st.enter_context(nc.semaphore("d_%d" % len(dsem)))
            block = st.enter_context(nc.Block())

            def run_engine(ename, eng):
                waited = {}
                for o in ops:
                    if o.eng != ename:
                        continue
                    need = {}
                    for d in o.deps:
                        p = ops[d]
                        if p.dma:
                            k = ("d", p.key)
                            v = 16 * o.dneed[p.key]
                            s = dsem[p.key]
                        else:
                            k = ("e", p.eng)
                            v = p.sigidx
                            s = esem[p.eng]
                        if need.get(k, (0, None))[0] < v:
                            need[k] = (v, s)
                    for k, (v, s) in need.items():
                        if waited.get(k, 0) >= v:
                            continue
                        eng.wait_ge(s, v)
                        waited[k] = v
                    ins = o.fn(eng)
                    if o.dma:
                        ins.then_inc(dsem[o.key], 16)
                    elif o.sig:
                        ins.then_inc(esem[o.eng], 1)
                if ename == final_eng:
                    for k in self.dma_count:
                        eng.wait_ge(dsem[k], 16 * self.dma_count[k])

            @block.tensor
            def _(e):
                run_engine("pe", e)

            @block.scalar
            def _(e):
                run_engine("act", e)

            @block.vector
            def _(e):
                run_engine("dve", e)

            @block.gpsimd
            def _(e):
                run_engine("pool", e)

            @block.sync
            def _(e):
                run_engine("sp", e)

    def bar_of(self, o):
        return ()


def _split3(a):
    a = np.asarray(a, np.float32)
    hi = a.astype(ml_dtypes.bfloat16)
    r1 = a - hi.astype(np.float32)
    lo = r1.astype(ml_dtypes.bfloat16)
    r2 = r1 - lo.astype(np.float32)
    ll = r2.astype(ml_dtypes.bfloat16)
    return hi, lo, ll


def make_tables(j):
    bf = ml_dtypes.bfloat16
    T = {}
    slopes = np.exp2(-8.0 * np.arange(1, NH + 1, dtype=np.float32) / NH).astype(np.float32)
    rows = np.zeros((NG, 64, 8, 4, 128), np.float32).astype(bf)
    ql = np.arange(128, dtype=np.float32)
    for g in range(NG):
        for hh in range(4):
            s = slopes[4 * g + hh]
            s3 = _split3(np.full((128,), s, np.float32))
            for qb in range(8):
                tq = (1024 + 128 * qb + ql).astype(np.float32)
                a3 = _split3((-s * tq).astype(np.float32))
                for r in range(3):
                    rows[g, r, qb, hh] = a3[r]
                    rows[g, 3 + r, qb, hh] = s3[r]
                    rows[g, 6 + r, qb, hh] = s3[r]
                rows[g, 9, qb, hh] = (0.0 if j == 1 else NEG)
    T["rows"] = rows.reshape(NG, 64, 8 * 512)
    kr = np.zeros((64, 16, 128), np.float32)
    kl = np.arange(128, dtype=np.float32)
    for kb in range(16):
        kr[0:3, kb] = 1.0
        kr[3:6, kb] = kl[None, :]
        kr[6:9, kb] = 128.0 * kb
        kr[9, kb] = 1.0 if kb < 8 else 0.0
        for jj in range(32):
            kr[32 + jj, kb] = ((2 * kb + (np.arange(128) // 64)) == jj).astype(np.float32)
    T["krows"] = kr.astype(bf).reshape(64, 16 * 128)
    kc = np.zeros((10, 128), np.float32)
    c = np.arange(127, dtype=np.float32)
    kc[0:3, :127] = 1.0
    kc[3:6, :127] = 16.0 * c
    kc[6:9, :127] = 15.5
    T["kcrows"] = kc.astype(bf)
    cm = np.zeros((128, 8, 4, 128), np.float32)
    for qb in range(8):
        tq = 1024 + 128 * qb + np.arange(128)
        cc = np.arange(127)
        valid = (16 * cc[:, None] + 31) <= tq[None, :]
        if j == 0:
            valid = valid & (16 * cc[:, None] >= 1024)
        cm[:127, qb] = np.where(valid, 0.0, NEG)[:, None, :]
    T["cmask"] = cm.astype(bf).reshape(128, 8 * 512)
    lo = np.where(kl[:, None] <= ql[None, :], 0.0, NEG).astype(np.float32)
    up = np.where(kl[:, None] > ql[None, :], 0.0, NEG).astype(np.float32)
    T["tri"] = np.stack([np.repeat(lo[:, None, :], 4, 1), np.repeat(up[:, None, :], 4, 1)], 1).astype(bf).reshape(128, 2 * 512)
    cs = (16 * np.arange(127))[:, None]
    js = (64 * np.arange(32))[None, :]
    ov = np.clip(np.minimum(cs + 32, js + 64) - np.maximum(cs, js), 0, None).astype(np.float32) / 32.0
    ovp = np.zeros((128, 32), np.float32)
    ovp[:127] = ov
    T["ov"] = ovp.astype(bf)
    vm = np.zeros((128, 8, 32), np.float32)
    fb = np.zeros((128, 8, 32), np.float32)
    for qb in range(8):
        for q in range(128):
            tf = 1024 + 128 * qb + q
            ta = tf - (0 if j == 1 else 1024)
            bt = ta // 64
            for jf in range(32):
                ja = jf - (0 if j == 1 else 16)
                if ja < 0:
                    fb[q, qb, jf] = -2e4
                elif ja == 0 or ja == bt or ja == bt - 1:
                    fb[q, qb, jf] = 1e4
                elif ja <= bt:
                    vm[q, qb, jf] = 1.0
                else:
                    fb[q, qb, jf] = -1e4
    T["vm"] = vm.reshape(128, 256)
    T["fb"] = fb.reshape(128, 256)
    hh = np.arange(RH, dtype=np.float64)
    lg = np.log1p(-np.exp2(-5.0 - hh))
    n = np.arange(128, dtype=np.float64)
    diff = n[None, :] - n[:, None]
    dec = np.where(diff[None] >= 0, np.exp(lg[:, None, None] * np.maximum(diff[None], 0.0)), 0.0)
    T["decT"] = (dec * (RDK ** -0.5)).transpose(1, 0, 2).astype(np.float32).reshape(128, RH * 128)
    xi = np.exp(lg[:, None] * (n + 1.0)[None, :])
    T["xi"] = np.repeat(xi.astype(np.float32)[None], 128, 0).reshape(128, RH * 128)
    zeta = np.exp(lg[:, None] * (127 - n)[None, :]) * (RDK ** -0.5)
    zt = np.zeros((128, 16, RH), np.float32)
    for t in range(16):
        if t < 8:
            z = np.exp(lg[:, None] * (1023 - (128 * t + n))[None, :]) * (RDK ** -0.5)
            zt[:, t, :] = (z.T if j == 1 else 0.0)
        else:
            zt[:, t, :] = zeta.T
    T["zt"] = zt.reshape(128, 16 * RH)
    T["gC"] = [float(np.exp(lg[h] * 128)) for h in range(RH)]
    T["ident"] = np.eye(128, dtype=np.float32)
    T["identb"] = np.eye(128, dtype=np.float32).astype(bf)
    T["ones"] = np.ones((128, 128), np.float32)
    return T


TABLE_SPECS = [
    ("rows", [NG, 64, 4096], BF16), ("krows", [64, 2048], BF16), ("kcrows", [10, 128], BF16),
    ("cmask", [128, 4096], BF16), ("tri", [128, 1024], BF16), ("ov", [128, 32], BF16),
    ("vm", [128, 256], F32), ("fb", [128, 256], F32), ("decT", [128, 1024], F32), ("xi", [128, 1024], F32),
    ("zt", [128, 128], F32), ("ident", [128, 128], F32), ("identb", [128, 128], BF16), ("ones", [128, 128], F32),
]
GC = make_tables(1)["gC"]


def build(stop_after=None, dbg=(), mixer_only=False):
    nc = bass.Bass("TRN2", target_bir_lowering=False)
    P = Prog()

    def din(name, shape, dt=F32):
        return nc.dram_tensor(name, shape, dt, kind="ExternalInput").ap()

    xf = din("xf", [2048, D])
    c_l = din("c_l", [128, 16])
    if not mixer_only:
        w_ada = din("w_ada", [D, 9 * D])
        b_ada = din("b_ada", [128, 9 * D])
        gnorm = din("gnorm", [128, 3 * D])
        wg = din("wg", [2, D, DFF])
        wu = din("wu", [2, D, DFF])
        wd = din("wd", [2, DFF, D])
    w_in = din("w_in", [D, NIN])
    gqkT = din("gqkT", [128, 4])
    gqkR = din("gqkR", [128, 4 * 128])
    cpos = din("cpos", [128, 2 * 32])
    cw1 = din("cw1", [2, 4096, 128])
    cb1 = din("cb1", [128, 2])
    cw2 = din("cw2", [2, 128, 128])
    gnr = din("gnr", [128, RH * RDV])
    wpn = din("wpn", [D, D])
    wpr = din("wpr", [D, D])
    wo = din("wo", [D, D])
    tb_ap = {}
    for name, shape, dt in TABLE_SPECS:
        tb_ap[name] = din("t_" + name, shape, dt)
    out = nc.dram_tensor("out", [1024, D], F32, kind="ExternalOutput").ap()
    dbg_ap = {}
    for name, shape, dt in dbg:
        dbg_ap[name] = nc.dram_tensor("dbg_" + name, shape, dt, kind="ExternalOutput").ap()
    if mixer_only:
        ada_d = din("ada_in", [128, 9 * D])
        uT_d = din("uT_in", [2, 128, 16 * 1024], BF16)
        x1_i = din("x1_in", [128, 8 * D])
        x1_d = nc.dram_tensor("x1_d", [128, 8 * D], F32, kind="Internal").ap()
    else:
        ada_d = nc.dram_tensor("ada_d", [128, 9 * D], F32, kind="Internal").ap()
        uT_d = nc.dram_tensor("uT_d", [2, 128, 16 * 1024], BF16, kind="Internal").ap()
        x1_d = nc.dram_tensor("x1_d", [128, 8 * D], F32, kind="Internal").ap()
    orT_d = nc.dram_tensor("orT_d", [128, 16 * 1024], BF16, kind="Internal").ap()

    final_keys = []
    outer = ExitStack()
    with outer:
        uid = [0]

        def SB(st, name, shape, dt):
            uid[0] += 1
            return st.enter_context(nc.sbuf_tensor("%s_%d" % (name, uid[0]), shape, dt))

        banks = [outer.enter_context(nc.psum_tensor("bank%d" % i, [128, 512], F32)) for i in range(8)]
        ident = SB(outer, "ident", [128, 128], F32)
        identb = SB(outer, "identb", [128, 128], BF16)
        ones = SB(outer, "ones", [128, 128], F32)
        P.op("sp", lambda e: e.dma_start(out=ident[:], in_=tb_ap["ident"]), writes=["ident"], dma=True, key="tbl")
        P.op("sp", lambda e: e.dma_start(out=identb[:], in_=tb_ap["identb"]), writes=["identb"], dma=True, key="tbl")
        P.op("sp", lambda e: e.dma_start(out=ones[:], in_=tb_ap["ones"]), writes=["ones"], dma=True, key="tbl")

        def B(i):
            return "bank%d" % i

        def dump(name, src_ap, reads):
            if name in dbg_ap:
                k = "dbg_" + name
                P.op("sp", lambda e: e.dma_start(out=dbg_ap[name], in_=src_ap), reads=reads, writes=[k], dma=True, key=k)
                if k not in final_keys:
                    final_keys.append(k)

        ada_t = {}

        def ada_init(st):
            c_sb = SB(st, "c_sb", [128, 16], F32)
            cond = SB(st, "cond", [128, 16], F32)
            ada_t["condrep"] = SB(st, "condrep", [128, 16, 128], BF16)
            ada_t["wa"] = [SB(st, "wa%d" % i, [128, 16, 256], BF16) for i in range(2)]
            ada_t["ba"] = [SB(st, "ba%d" % i, [128, 256], F32) for i in range(2)]
            ada_t["rs"] = [SB(st, "rs%d" % i, [128, 256], F32) for i in range(2)]
            ada_t["gs"] = [SB(st, "gs%d" % i, [128, 256], F32) for i in range(2)]
            condrep = ada_t["condrep"]
            P.op("sp", lambda e: e.dma_start(out=c_sb[:], in_=c_l), writes=["c_sb"], dma=True, key="tbl")
            P.op("act", lambda e: e.activation(out=cond[:], in_=c_sb[:], func=AF.Silu), reads=["c_sb"], writes=["cond"])
            P.op("dve", lambda e: e.tensor_copy(out=condrep[:], in_=cond[:].unsqueeze(2).to_broadcast([128, 16, 128])),
                 reads=["cond"], writes=["condrep"])

        def ada_block(cb):
            condrep, wa, ba, rs, gs = ada_t["condrep"], ada_t["wa"], ada_t["ba"], ada_t["rs"], ada_t["gs"]
            wsrc = w_ada.rearrange("(k p) n -> p k n", p=128)
            s = cb % 2
            slot, fbk = cb // 8, cb % 8
            i, kind = slot // 3, slot % 3
            cs = slice(cb * 256, (cb + 1) * 256)
            bk = s
            P.op("pool", lambda e: e.dma_start(out=wa[s][:], in_=wsrc[:, :, cs]), writes=["wa%d" % s], dma=True)
            P.op("sp", lambda e: e.dma_start(out=ba[s][:], in_=b_ada[:, cs]), writes=["ba%d" % s], dma=True)
            for k in range(16):
                P.op("pe", lambda e, k=k: e.matmul(banks[bk][:, 0:256], lhsT=condrep[:, k, :], rhs=wa[s][:, k, :], start=(k == 0), stop=(k == 15)),
                     reads=["condrep", "wa%d" % s], writes=[B(bk)])
            P.op("dve", lambda e: e.tensor_tensor(out=rs[s][:], in0=banks[bk][:, 0:256], in1=ba[s][:], op=ALU.add),
                 reads=[B(bk), "ba%d" % s], writes=["rs%d" % s])
            if kind == 1:
                gsl = slice(i * D + fbk * 256, i * D + (fbk + 1) * 256)
                P.op("sp", lambda e: e.dma_start(out=gs[s][:], in_=gnorm[:, gsl]), writes=["gs%d" % s], dma=True)
                P.op("dve", lambda e: e.scalar_tensor_tensor(out=rs[s][:], in0=rs[s][:], scalar=1.0, in1=gs[s][:], op0=ALU.add, op1=ALU.mult),
                     reads=["rs%d" % s, "gs%d" % s], writes=["rs%d" % s])
            elif kind == 2 and i != 1:
                P.op("dve", lambda e: e.tensor_scalar_mul(out=rs[s][:], in0=rs[s][:], scalar1=0.5), reads=["rs%d" % s], writes=["rs%d" % s])
            P.op("sp", lambda e: e.dma_start(out=ada_d[:, cs], in_=rs[s][:]), reads=["rs%d" % s], writes=["ada_d%d" % cb], dma=True, key="st_rs%d" % s)

        def ada_names(slot):
            return ["ada_d%d" % cb for cb in range(slot * 8, slot * 8 + 8)]

        def norm_mod_T(x_sb, i, dstT, dst_name):
            with ExitStack() as st:
                G = SB(st, "G", [128, D], F32)
                SH = SB(st, "SH", [128, D], F32)
                ss = SB(st, "ss", [128, 8], F32)
                junk = SB(st, "junk", [128, D], F32)
                hf = [SB(st, "hf%d" % k, [128, D], F32) for k in range(2)]
                P.op("sp", lambda e: e.dma_start(out=G[:], in_=ada_d[:, (3 * i + 1) * D:(3 * i + 2) * D]),
                     reads=ada_names(3 * i + 1), writes=["G"], dma=True)
                P.op("sp", lambda e: e.dma_start(out=SH[:], in_=ada_d[:, (3 * i) * D:(3 * i + 1) * D]),
                     reads=ada_names(3 * i), writes=["SH"], dma=True)
                P.op("dve", lambda e: e.memset(ss[:], 0.0), writes=["ss"])
                for tb in range(8):
                    P.op("act", lambda e, tb=tb: e.activation(out=junk[:], in_=x_sb[:, tb, :], func=AF.Square, accum_out=ss[:, tb:tb + 1]),
                         reads=["x%d_%d" % (tb, f) for f in range(4)], writes=["junk", "ss"])
                P.op("dve", lambda e: e.tensor_scalar(out=ss[:], in0=ss[:], scalar1=1.0 / D, scalar2=EPS, op0=ALU.mult, op1=ALU.add), reads=["ss"], writes=["ss"])
                P.op("act", lambda e: e.activation(out=ss[:], in_=ss[:], func=AF.Sqrt), reads=["ss"], writes=["ss"])
                P.op("dve", lambda e: e.reciprocal(out=ss[:], in_=ss[:]), reads=["ss"], writes=["ss"])
                tcount = 0
                for tb in range(8):
                    s = tb % 2
                    P.op("dve", lambda e, tb=tb, s=s: e.scalar_tensor_tensor(out=hf[s][:], in0=x_sb[:, tb, :], scalar=ss[:, tb:tb + 1], in1=G[:], op0=ALU.mult, op1=ALU.mult),
                         reads=["x%d_%d" % (tb, f) for f in range(4)] + ["ss", "G"], writes=["hf%d" % s])
                    P.op("dve", lambda e, s=s: e.tensor_tensor(out=hf[s][:], in0=hf[s][:], in1=SH[:], op=ALU.add), reads=["hf%d" % s, "SH"], writes=["hf%d" % s])
                    for kq in range(4):
                        bk = tcount % 8
                        tcount += 1
                        for kk in range(4):
                            P.op("pe", lambda e, s=s, bk=bk, kk=kk, kq=kq: e.transpose(banks[bk][:, kk * 128:(kk + 1) * 128], hf[s][:, (4 * kq + kk) * 128:(4 * kq + kk + 1) * 128], ident[:]),
                                 reads=["hf%d" % s, "ident"], writes=[B(bk)])
                        eng = "act" if kq % 2 == 0 else "dve"
                        if eng == "act":
                            P.op("act", lambda e, bk=bk, kq=kq, tb=tb: e.activation(out=dstT[:, 4 * kq:4 * kq + 4, tb * 128:(tb + 1) * 128], in_=banks[bk][:].rearrange("p (k t) -> p k t", k=4), func=AF.Copy),
                                 reads=[B(bk)], writes=["%s_%d" % (dst_name, tb)])
                        else:
                            P.op("dve", lambda e, bk=bk, kq=kq, tb=tb: e.tensor_copy(out=dstT[:, 4 * kq:4 * kq + 4, tb * 128:(tb + 1) * 128], in_=banks[bk][:].rearrange("p (k t) -> p k t", k=4)),
                                 reads=[B(bk)], writes=["%s_%d" % (dst_name, tb)])
            P.barrier()

        def ffn_core(x_sb, hT, i, l, hook=None):
            Wg = wg[l].rearrange("(k p) n -> p k n", p=128)
            Wu = wu[l].rearrange("(k p) n -> p k n", p=128)
            Wd = wd[l].rearrange("(c p) n -> p c n", p=128)
            with ExitStack() as st:
                GT = SB(st, "GT", [128, D], F32)
                hid = SB(st, "hid", [128, 11, 1024], BF16)
                wgt = [SB(st, "wgt%d" % k, [128, 16, 128], BF16) for k in range(2)]
                wut = [SB(st, "wut%d" % k, [128, 16, 128], BF16) for k in range(2)]
                wdt = [SB(st, "wdt%d" % k, [128, 11, 512], BF16) for k in range(2)]
                stt = [SB(st, "stt%d" % k, [128, 512], F32) for k in range(2)]
                rt = [SB(st, "rt%d" % k, [128, 512], F32) for k in range(2)]
                hT_names = ["hT_%d" % tb for tb in range(8)]
                step = 0
                oset = 0
                for r in range(4):
                    for cc in range(11):
                        c = r * 11 + cc
                        s = c % 2
                        csl = slice(c * 128, (c + 1) * 128)
                        P.op("pool", lambda e, s=s, csl=csl: e.dma_start(out=wgt[s][:], in_=Wg[:, :, csl]), writes=["wgt%d" % s], dma=True)
                        P.op("pool", lambda e, s=s, csl=csl: e.dma_start(out=wut[s][:], in_=Wu[:, :, csl]), writes=["wut%d" % s], dma=True)
                        for half in range(2):
                            bg, bu = 4 + 2 * (step % 2), 5 + 2 * (step % 2)
                            ss_ = step % 2
                            step += 1
                            tsl = slice(half * 512, (half + 1) * 512)
                            hr = hT_names[half * 4:(half + 1) * 4]
                            for k in range(16):
                                P.op("pe", lambda e, s=s, k=k, bg=bg, tsl=tsl: e.matmul(banks[bg][:], lhsT=wgt[s][:, k, :], rhs=hT[:, k, tsl], start=(k == 0), stop=(k == 15)),
                                     reads=["wgt%d" % s] + hr, writes=[B(bg)])
                            for k in range(16):
                                P.op("pe", lambda e, s=s, k=k, bu=bu, tsl=tsl: e.matmul(banks[bu][:], lhsT=wut[s][:, k, :], rhs=hT[:, k, tsl], start=(k == 0), stop=(k == 15)),
                                     reads=["wut%d" % s] + hr, writes=[B(bu)])
                            P.op("act", lambda e, bg=bg, ss_=ss_: e.activation(out=stt[ss_][:], in_=banks[bg][:], func=AF.Silu), reads=[B(bg)], writes=["stt%d" % ss_])
                            P.op("dve", lambda e, bu=bu, ss_=ss_, cc=cc, tsl=tsl: e.tensor_tensor(out=hid[:, cc, tsl], in0=stt[ss_][:], in1=banks[bu][:], op=ALU.mult),
                                 reads=["stt%d" % ss_, B(bu)], writes=["hid%d_%d" % (cc, half)])
                        if hook is not None:
                            hook(c)
                    if r == 0:
                        P.op("sp", lambda e: e.dma_start(out=GT[:], in_=ada_d[:, (3 * i + 2) * D:(3 * i + 3) * D]),
                             reads=ada_names(3 * i + 2), writes=["GT"], dma=True)
                    for fbk in range(4):
                        ws = (r * 4 + fbk) % 2
                        fsl = slice(fbk * 512, (fbk + 1) * 512)
                        P.op("pool", lambda e, ws=ws, r=r, fsl=fsl: e.dma_start(out=wdt[ws][:], in_=Wd[:, r * 11:(r + 1) * 11, fsl]), writes=["wdt%d" % ws], dma=True)
                        for tp in range(4):
                            ob = [2 * (oset % 2), 2 * (oset % 2) + 1]
                            oset += 1
                            for cc in range(11):
                                for t2 in range(2):
                                    tb = 2 * tp + t2
                                    P.op("pe", lambda e, ws=ws, cc=cc, tb=tb, b_=ob[t2]: e.matmul(banks[b_][:], lhsT=hid[:, cc, tb * 128:(tb + 1) * 128], rhs=wdt[ws][:, cc, :], start=(cc == 0), stop=(cc == 10)),
                                         reads=["hid%d_%d" % (cc, tb // 4), "wdt%d" % ws], writes=[B(ob[t2])])
                            for t2 in range(2):
                                tb = 2 * tp + t2
                                P.op("dve", lambda e, t2=t2, b_=ob[t2], fsl=fsl: e.tensor_tensor(out=rt[t2][:], in0=banks[b_][:], in1=GT[:, fsl], op=ALU.mult),
                                     reads=[B(ob[t2]), "GT"], writes=["rt%d" % t2])
                                P.op("dve", lambda e, t2=t2, tb=tb, fsl=fsl: e.tensor_tensor(out=x_sb[:, tb, fsl], in0=x_sb[:, tb, fsl], in1=rt[t2][:], op=ALU.add),
                                     reads=["rt%d" % t2, "x%d_%d" % (tb, fbk)], writes=["x%d_%d" % (tb, fbk)])
            P.barrier()

        xnames = ["x%d_%d" % (tb, f) for tb in range(8) for f in range(4)]

        def load_x(x_sb, src):
            v = src.rearrange("(t p) d -> p t d", p=128)
            for tb in range(8):
                P.op("sp", lambda e, tb=tb: e.dma_start(out=x_sb[:, tb, :], in_=v[:, tb, :]), writes=["x%d_%d" % (tb, f) for f in range(4)], dma=True, key="ldx")

        sada = ExitStack()
        ada_pending = list(range(16, 72))
        if not mixer_only:
            ada_init(sada)
            for cb in range(16):
                ada_block(cb)
        else:
            with ExitStack() as sx0:
                xt0 = SB(sx0, "xt0", [128, 8 * D], F32)
                P.op("sp", lambda e: e.dma_start(out=xt0[:], in_=x1_i), writes=["xt0"], dma=True)
                P.op("sp", lambda e: e.dma_start(out=x1_d, in_=xt0[:]), reads=["xt0"], writes=["x1_d"], dma=True, key="st_x1")
            P.barrier()
        with ExitStack() as sx:
            if not mixer_only:
                x_sb = SB(sx, "x_sb", [128, 8, D], F32)
                hT = SB(sx, "hT", [128, 16, 1024], BF16)
                uT = hT
            for half in ((0, 1) if not mixer_only else ()):
                load_x(x_sb, xf[half * 1024:(half + 1) * 1024, :])
                norm_mod_T(x_sb, 0, hT, "hT")
                lim = 48 if half == 0 else 72

                def hook(c, lim=lim):
                    if ada_pending and ada_pending[0] < lim:
                        ada_block(ada_pending.pop(0))
                ffn_core(x_sb, hT, 0, 0, hook=hook)
                assert not ada_pending or ada_pending[0] >= lim
                norm_mod_T(x_sb, 1, uT, "uT")
                P.op("sp", lambda e, half=half: e.dma_start(out=uT_d[half], in_=uT[:].rearrange("p k t -> p (k t)")),
                     reads=["uT_%d" % tb for tb in range(8)], writes=["uT_d%d" % half], dma=True, key="st_uT")
                if half == 1:
                    P.op("sp", lambda e: e.dma_start(out=x1_d, in_=x_sb[:].rearrange("p t d -> p (t d)")), reads=xnames, writes=["x1_d"], dma=True, key="st_x1")
                    dump("x1", x_sb[:].rearrange("p t d -> p (t d)"), xnames)
                    dump("uT", uT[:].rearrange("p k t -> p (k t)"), ["uT_%d" % tb for tb in range(8)])
                P.barrier()
        sada.close()
        P.barrier()
        if stop_after == "ffn1":
            P.emit(nc, final_keys=final_keys)
            return nc

        Win = w_in.rearrange("(k p) n -> p k n", p=128)

        def load_w(tile, tname, col0, ncols, dst0=0):
            P.op("pool", lambda e: e.dma_start(out=tile[:, :, dst0:dst0 + ncols], in_=Win[:, :, col0:col0 + ncols]), writes=[tname], dma=True)

        def fm_proj(wt, wname, uT_, uname, chunks, bank_of, evac):
            for ci in chunks:
                bk = bank_of(ci)
                rn = ["%s_%d" % (uname, tb) for tb in range(4 * ci, 4 * ci + 4)]
                for k in range(16):
                    P.op("pe", lambda e, k=k, bk=bk, ci=ci: e.matmul(banks[bk][:], lhsT=wt[:, k, 0:128], rhs=uT_[:, k, ci * 512:(ci + 1) * 512], start=(k == 0), stop=(k == 15)),
                         reads=[wname] + rn, writes=[B(bk)])
                evac(bk, ci)

        def tm_proj(wt, wname, ncols, uT_, uname, tbs, bank_of, evac):
            for tb in tbs:
                bk = bank_of(tb)
                for k in range(16):
                    P.op("pe", lambda e, k=k, bk=bk, tb=tb: e.matmul(banks[bk][:, 0:ncols], lhsT=uT_[:, k, tb * 128:(tb + 1) * 128], rhs=wt[:, k, 0:ncols], start=(k == 0), stop=(k == 15)),
                         reads=[wname, "%s_%d" % (uname, tb)], writes=[B(bk)])
                evac(bk, tb)

        with ExitStack() as sm:
            uTo = SB(sm, "uTo", [128, 16, 1024], BF16)
            onT = SB(sm, "onT", [128, 16, 1024], BF16)
            uTo_n = ["uTo_%d" % tb for tb in range(8)]
            for tb in range(8):
                P.op("sp", lambda e, tb=tb: e.dma_start(out=uTo[:, :, tb * 128:(tb + 1) * 128], in_=uT_d[1].rearrange("p (k t) -> p k t", k=16)[:, :, tb * 128:(tb + 1) * 128]),
                     reads=["uT_d1"], writes=["uTo_%d" % tb], dma=True, key="ld_uTo")
            with ExitStack() as sa:
                uTc = SB(sa, "uTc", [128, 16, 1024], BF16)
                for tb in range(8):
                    P.op("sp", lambda e, tb=tb: e.dma_start(out=uTc[:, :, tb * 128:(tb + 1) * 128], in_=uT_d[0].rearrange("p (k t) -> p k t", k=16)[:, :, tb * 128:(tb + 1) * 128]),
                         reads=["uT_d0"], writes=["uTc_%d" % tb], dma=True, key="ld_uTc")
                gates = SB(sa, "gates", [128, 8, 48], F32)
                wt1 = [SB(sa, "wt1_%d" % k, [128, 16, 128], BF16) for k in range(3)]
                load_w(wt1[0], "wt1_0", O_GL, 48)

                def ev_gl(bk, tb):
                    P.op("act", lambda e: e.activation(out=gates[:, tb, :], in_=banks[bk][:, 0:48], func=AF.Sigmoid), reads=[B(bk)], writes=["gates"])
                tm_proj(wt1[0], "wt1_0", 48, uTo, "uTo", range(8), lambda tb: tb % 2, ev_gl)
                P.barrier()
                if stop_after == "gates":
                    P.op("sp", lambda e: e.dma_start(out=out[0:128, 0:384], in_=gates[:].rearrange("p a b -> p (a b)")), reads=["gates"], writes=["outg"], dma=True, key="st_out")
                    final_keys.append("st_out")
                    P.emit(nc, final_keys=final_keys)
                    return nc

                with ExitStack() as sr:
                    orT = SB(sr, "orT", [128, 16, 1024], BF16)
                    decT = SB(sr, "decT", [128, 1, 128], F32)
                    xi_t = SB(sr, "xi_t", [128, 1, 128], F32)
                    zt = SB(sr, "zt", [128, 16, RH], F32)
                    gnr_t = SB(sr, "gnr_t", [128, 1, RDV], F32)
                    P.op("sp", lambda e: e.dma_start(out=zt[:].rearrange("p t h -> p (t h)"), in_=tb_ap["zt"]), writes=["zt"], dma=True, key="tbl")
                    qT = SB(sr, "r_qT", [128, 1024], BF16)
                    qxT = SB(sr, "r_qxT", [128, 1024], BF16)
                    kT = SB(sr, "r_kT", [128, 1024], BF16)
                    kz = SB(sr, "r_kz", [128, 16, 128], BF16)
                    vv = SB(sr, "r_v", [128, 16, 256], BF16)
                    sg = SB(sr, "r_sg", [128, 8, 256], F32)
                    state = SB(sr, "r_state", [128, 256], F32)
                    stbf = SB(sr, "r_stbf", [128, 256], BF16)
                    sTb = [SB(sr, "r_sTb%d" % k, [128, 128], BF16) for k in range(2)]
                    t1 = [SB(sr, "r_t1%d" % k, [128, 256], F32) for k in range(2)]
                    ob = [SB(sr, "r_ob%d" % k, [128, 256], BF16) for k in range(2)]
                    ysum = SB(sr, "r_ysum", [128, 8], F32)
                    ysq = SB(sr, "r_ysq", [128, 8], F32)
                    mean = SB(sr, "r_mean", [128, 8], F32)
                    rstd = SB(sr, "r_rstd", [128, 8], F32)
                    junk2 = SB(sr, "r_junk", [128, 256], F32)
                    wkv = SB(sr, "r_wkv", [128, 16, 384], BF16)
                    wgr = SB(sr, "r_wgr", [128, 16, 256], BF16)
                    for h in range(RH):
                        P.op("sp", lambda e, h=h: e.dma_start(out=decT[:, 0, :], in_=tb_ap["decT"][:, h * 128:(h + 1) * 128]), writes=["decT"], dma=True)
                        P.op("sp", lambda e, h=h: e.dma_start(out=xi_t[:, 0, :], in_=tb_ap["xi"][:, h * 128:(h + 1) * 128]), writes=["xi_t"], dma=True)
                        P.op("sp", lambda e, h=h: e.dma_start(out=gnr_t[:, 0, :], in_=gnr[:, h * 256:(h + 1) * 256]), writes=["gnr_t"], dma=True)
                        load_w(wt1[1], "wt1_1", O_RQ + h * 128, 128)
                        load_w(wt1[2], "wt1_2", O_RK + h * 128, 128)
                        load_w(wkv, "r_wkv", O_RK + h * 128, 128, 0)
                        load_w(wkv, "r_wkv", O_RV + h * 256, 256, 128)
                        load_w(wgr, "r_wgr", O_RG + h * 256, 256)

                        RC = int(os.environ.get("RET_CUT2", "99"))
                        if RC < 2:
                            break

                        def ev_q(bk, ci, h=h):
                            P.op("act", lambda e: e.activation(out=qT[:, ci * 512:(ci + 1) * 512], in_=banks[bk][:], func=AF.Copy), reads=[B(bk)], writes=["r_qT%d" % ci])
                            for c4 in range(4):
                                P.op("dve", lambda e, c4=c4: e.tensor_tensor(out=qxT[:, ci * 512 + c4 * 128:ci * 512 + (c4 + 1) * 128], in0=banks[bk][:, c4 * 128:(c4 + 1) * 128],
                                                                             in1=xi_t[:, 0, :], op=ALU.mult), reads=[B(bk), "xi_t"], writes=["r_qxT%d" % ci])
                        fm_proj(wt1[1], "wt1_1", uTo, "uTo", range(2), lambda ci: ci, ev_q)

                        if RC < 3:
                            break

                        def ev_k(bk, ci):
                            P.op("act", lambda e: e.activation(out=kT[:, ci * 512:(ci + 1) * 512], in_=banks[bk][:], func=AF.Copy), reads=[B(bk)], writes=["r_kT%d" % ci])
                        fm_proj(wt1[2], "wt1_2", uTo, "uTo", range(2), lambda ci: 2 + ci, ev_k)

                        if RC < 4:
                            break

                        def ev_kv(bk, t, h=h):
                            P.op("dve", lambda e: e.tensor_scalar(out=kz[:, t, :], in0=banks[bk][:, 0:128], scalar1=zt[:, t, h:h + 1], scalar2=None, op0=ALU.mult), reads=[B(bk), "zt"], writes=["r_kz%d" % t])
                            P.op("act", lambda e: e.activation(out=vv[:, t, :], in_=banks[bk][:, 128:384], func=AF.Copy), reads=[B(bk)], writes=["r_v%d" % t])
                        tm_proj(wkv, "r_wkv", 384, uTc, "uTc", range(8), lambda tb: 4 + tb % 2, ev_kv)

                        if RC < 5:
                            break

                        def ev_kv2(bk, tb):
                            ev_kv(bk, 8 + tb)
                        tm_proj(wkv, "r_wkv", 384, uTo, "uTo", range(8), lambda tb: 4 + tb % 2, ev_kv2)

                        if RC < 6:
                            break

                        def ev_g(bk, tb):
                            P.op("act", lambda e: e.activation(out=sg[:, tb, :], in_=banks[bk][:, 0:256], func=AF.Silu), reads=[B(bk)], writes=["r_sg%d" % tb])
                        tm_proj(wgr, "r_wgr", 256, uTo, "uTo", range(8), lambda tb: 6 + tb % 2, ev_g)
                        if os.environ.get("RET_CUT") == "proj":
                            break
                        for t in range(8):
                            P.op("pe", lambda e, t=t: e.matmul(banks[0][:, 0:256], lhsT=kz[:, t, :], rhs=vv[:, t, :], start=(t == 0), stop=(t == 7)),
                                 reads=["r_kz%d" % t, "r_v%d" % t], writes=[B(0)])
                        P.op("dve", lambda e: e.tensor_copy(out=state[:], in_=banks[0][:, 0:256]), reads=[B(0)], writes=["r_state"])
                        P.op("act", lambda e: e.activation(out=stbf[:], in_=banks[0][:, 0:256], func=AF.Copy), reads=[B(0)], writes=["r_stbf"])
                        P.op("dve", lambda e: e.memset(ysum[:], 0.0), writes=["r_ysum"])
                        P.op("dve", lambda e: e.memset(ysq[:], 0.0), writes=["r_ysq"])
                        if os.environ.get("RET_CUT") == "state":
                            break
                        for c in range(8):
                            csl = slice(c * 128, (c + 1) * 128)
                            sb_ = 1 + c % 2
                            yb = 4 + c // 2
                            ysl = slice((c % 2) * 256, (c % 2) * 256 + 256)
                            s2 = c % 2
                            P.op("pe", lambda e, csl=csl, sb_=sb_: e.matmul(banks[sb_][:, 0:128], lhsT=kT[:, csl], rhs=qT[:, csl], start=True, stop=True),
                                 reads=["r_kT%d" % (c // 4), "r_qT%d" % (c // 4)], writes=[B(sb_)])
                            P.op("dve", lambda e, sb_=sb_, s2=s2, h=h: e.tensor_tensor(out=sTb[s2][:], in0=banks[sb_][:, 0:128], in1=decT[:, 0, :], op=ALU.mult),
                                 reads=[B(sb_), "decT"], writes=["r_sTb%d" % s2])
                            P.op("pe", lambda e, yb=yb, ysl=ysl, s2=s2, c=c: e.matmul(banks[yb][:, ysl], lhsT=sTb[s2][:], rhs=vv[:, 8 + c, :], start=True, stop=False),
                                 reads=["r_sTb%d" % s2, "r_v%d" % (8 + c)], writes=[B(yb)])
                            P.op("pe", lambda e, yb=yb, ysl=ysl, csl=csl: e.matmul(banks[yb][:, ysl], lhsT=qxT[:, csl], rhs=stbf[:], start=False, stop=True),
                                 reads=["r_qxT%d" % (c // 4), "r_stbf"], writes=[B(yb)])
                            if c < 7:
                                P.op("pe", lambda e, c=c: e.matmul(banks[3][:, 0:256], lhsT=kz[:, 8 + c, :], rhs=vv[:, 8 + c, :], start=True, stop=True),
                                     reads=["r_kz%d" % (8 + c), "r_v%d" % (8 + c)], writes=[B(3)])
                                P.op("dve", lambda e, h=h: e.scalar_tensor_tensor(out=state[:], in0=state[:], scalar=GC[h], in1=banks[3][:, 0:256], op0=ALU.mult, op1=ALU.add),
                                     reads=[B(3), "r_state"], writes=["r_state"])
                                P.op("act", lambda e: e.activation(out=stbf[:], in_=state[:], func=AF.Copy), reads=["r_state"], writes=["r_stbf"])
                            P.op("act", lambda e, yb=yb, ysl=ysl, c=c: e.activation(out=junk2[:], in_=banks[yb][:, ysl], func=AF.Identity, accum_out=ysum[:, c:c + 1]),
                                 reads=[B(yb)], writes=["r_junk", "r_ysum"])
                            P.op("act", lambda e, yb=yb, ysl=ysl, c=c: e.activation(out=junk2[:], in_=banks[yb][:, ysl], func=AF.Square, accum_out=ysq[:, c:c + 1]),
                                 reads=[B(yb)], writes=["r_junk", "r_ysq"])
                        if os.environ.get("RET_CUT") == "chunks":
                            break
                        P.op("dve", lambda e: e.tensor_scalar_mul(out=mean[:], in0=ysum[:], scalar1=1.0 / RDV), reads=["r_ysum"], writes=["r_mean"])
                        P.op("dve", lambda e: e.tensor_tensor(out=rstd[:], in0=mean[:], in1=mean[:], op=ALU.mult), reads=["r_mean"], writes=["r_rstd"])
                        P.op("dve", lambda e: e.scalar_tensor_tensor(out=rstd[:], in0=ysq[:], scalar=1.0 / RDV, in1=rstd[:], op0=ALU.mult, op1=ALU.subtract), reads=["r_ysq", "r_rstd"], writes=["r_rstd"])
                        P.op("dve", lambda e: e.tensor_scalar_add(out=rstd[:], in0=rstd[:], scalar1=EPS), reads=["r_rstd"], writes=["r_rstd"])
                        P.op("act", lambda e: e.activation(out=rstd[:], in_=rstd[:], func=AF.Sqrt), reads=["r_rstd"], writes=["r_rstd"])
                        P.op("dve", lambda e: e.reciprocal(out=rstd[:], in_=rstd[:]), reads=["r_rstd"], writes=["r_rstd"])
                        for c in range(8):
                            yb = 4 + c // 2
                            ysl = slice((c % 2) * 256, (c % 2) * 256 + 256)
                            s2 = c % 2
                            P.op("dve", lambda e, yb=yb, ysl=ysl, c=c, s2=s2: e.tensor_scalar(out=t1[s2][:], in0=banks[yb][:, ysl], scalar1=mean[:, c:c + 1], scalar2=rstd[:, c:c + 1], op0=ALU.subtract, op1=ALU.mult),
                                 reads=[B(yb), "r_mean", "r_rstd"], writes=["r_t1%d" % s2])
                            P.op("dve", lambda e, s2=s2, h=h: e.tensor_tensor(out=t1[s2][:], in0=t1[s2][:], in1=gnr_t[:, 0, :], op=ALU.mult), reads=["r_t1%d" % s2, "gnr_t"], writes=["r_t1%d" % s2])
                            P.op("dve", lambda e, s2=s2, c=c: e.tensor_tensor(out=ob[s2][:], in0=t1[s2][:], in1=sg[:, c, :], op=ALU.mult), reads=["r_t1%d" % s2, "r_sg%d" % c], writes=["r_ob%d" % s2])
                            for e2 in range(2):
                                P.op("pe", lambda e, s2=s2, e2=e2: e.matmul(banks[0][:, e2 * 128:(e2 + 1) * 128], lhsT=ob[s2][:, e2 * 128:(e2 + 1) * 128], rhs=identb[:], start=True, stop=True),
                                     reads=["r_ob%d" % s2, "identb"], writes=[B(0)])
                            P.op("act", lambda e, h=h, c=c: e.activation(out=orT[:, 2 * h:2 * h + 2, c * 128:(c + 1) * 128], in_=banks[0][:, 0:256].rearrange("p (k t) -> p k t", k=2), func=AF.Copy),
                                 reads=[B(0)], writes=["orT_%d" % c])
                    P.op("sp", lambda e: e.dma_start(out=orT_d, in_=orT[:].rearrange("p k t -> p (k t)")), reads=["orT_%d" % c for c in range(8)], writes=["orT_d"], dma=True, key="st_orT")
                    dump("orT", orT[:].rearrange("p k t -> p (k t)"), ["orT_%d" % c for c in range(8)])
                    P.barrier()
                if stop_after == "ret":
                    P.emit(nc, final_keys=final_keys)
                    return nc

                with ExitStack() as sn:
                    gq = SB(sn, "gq", [128, 4], F32)
                    gqr = SB(sn, "gqr", [128, 4, 128], F32)
                    krows = SB(sn, "krows", [64, 16, 128], BF16)
                    kcrows = SB(sn, "kcrows", [10, 128], BF16)
                    cmask = [SB(sn, "cmask%d" % k, [128, 512], BF16) for k in range(2)]
                    wt2 = [SB(sn, "wt2_0", [128, 16, 256], BF16)]
                    tri = SB(sn, "tri", [128, 2, 512], BF16)
                    vm_t = SB(sn, "vm_t", [128, 8, 32], F32)
                    fb_t = SB(sn, "fb_t", [128, 8, 32], F32)
                    ovb = SB(sn, "ovb", [128, 32], BF16)
                    cposf = SB(sn, "cposf", [128, 64], F32)
                    cposb = SB(sn, "cposb", [128, 64], BF16)
                    cb1_t = SB(sn, "cb1_t", [128, 2], F32)
                    P.op("sp", lambda e: e.dma_start(out=gq[:], in_=gqkT), writes=["gq"], dma=True, key="tbl")
                    P.op("sp", lambda e: e.dma_start(out=gqr[:].rearrange("p a d -> p (a d)"), in_=gqkR), writes=["gqr"], dma=True, key="tbl")
                    P.op("sp", lambda e: e.dma_start(out=krows[:].rearrange("p a d -> p (a d)"), in_=tb_ap["krows"]), writes=["krows"], dma=True, key="tbl")
                    P.op("sp", lambda e: e.dma_start(out=kcrows[:], in_=tb_ap["kcrows"]), writes=["kcrows"], dma=True, key="tbl")
                    P.op("sp", lambda e: e.dma_start(out=tri[:].rearrange("p a d -> p (a d)"), in_=tb_ap["tri"]), writes=["tri"], dma=True, key="tbl")
                    P.op("sp", lambda e: e.dma_start(out=vm_t[:].rearrange("p a d -> p (a d)"), in_=tb_ap["vm"]), writes=["vm_t"], dma=True, key="tbl")
                    P.op("sp", lambda e: e.dma_start(out=fb_t[:].rearrange("p a d -> p (a d)"), in_=tb_ap["fb"]), writes=["fb_t"], dma=True, key="tbl")
                    P.op("sp", lambda e: e.dma_start(out=ovb[:], in_=tb_ap["ov"]), writes=["ovb"], dma=True, key="tbl")
                    P.op("sp", lambda e: e.dma_start(out=cposf[:], in_=cpos), writes=["cposf"], dma=True, key="tbl")
                    P.op("sp", lambda e: e.dma_start(out=cb1_t[:], in_=cb1), writes=["cb1_t"], dma=True, key="tbl")
                    P.op("dve", lambda e: e.tensor_copy(out=cposb[:], in_=cposf[:]), reads=["cposf"], writes=["cposb"])
                    P.op("act", lambda e: e.mul(out=gq[:, 0:1], in_=gq[:, 0:1], mul=float(DH ** -0.5)), reads=["gq"], writes=["gq"])
                    qTg = SB(sn, "qTg", [128, 8, 512], BF16)
                    rawT = [SB(sn, "rawT%d" % k, [128, 2048], BF16) for k in range(2)]
                    kslcT = SB(sn, "kslcT", [128, 2048], BF16)
                    kwinT = SB(sn, "kwinT", [128, 2048], BF16)
                    vslc = SB(sn, "vslc", [128, 16, 129], BF16)
                    vwin = SB(sn, "vwin", [128, 16, 129], BF16)
                    RA = SB(sn, "RA", [64, 8, 512], BF16)
                    w1t_ = SB(sn, "w1t0", [128, 32, 128], BF16)
                    w1t = [w1t_, w1t_]
                    w2t = [SB(sn, "w2t%d" % k, [128, 128], BF16) for k in range(2)]
                    hTs = SB(sn, "hTs", [128, 128], BF16)
                    zf = SB(sn, "zf", [128, 128], F32)
                    zt_ = SB(sn, "zt_", [128, 128], F32)
                    btile = SB(sn, "btile", [128, 1], F32)
                    kcn = SB(sn, "kcn", [128, 128], F32)
                    kcmpT = SB(sn, "kcmpT", [128, 128], BF16)
                    vcmp = SB(sn, "vcmp", [128, 161], BF16)
                    sqt = SB(sn, "sqt", [128, 512], F32)
                    rtt = SB(sn, "rtt", [128, 512], F32)
                    Pb = [SB(sn, "Pb%d" % k, [128, 512], BF16) for k in range(3)]
                    rs4 = SB(sn, "rs4", [128, 4], F32)
                    coef = SB(sn, "coef", [128, 4], F32)
                    imp = SB(sn, "imp", [128, 32], F32)
                    work = SB(sn, "work", [128, 32], F32)
                    top8 = SB(sn, "top8", [128, 16], F32)
                    selb = SB(sn, "selb", [128, 64], F32)
                    otmp = SB(sn, "otmp", [128, 4, 128], F32)
                    obf = SB(sn, "obf", [128, 4, 128], BF16)
                    ss1 = SB(sn, "ss1", [128, 1], F32)
                    P.op("dve", lambda e: e.memset(vslc[:, :, 128:129], 1.0), writes=["vslc_ones"])
                    P.op("dve", lambda e: e.memset(vwin[:, :, 128:129], 1.0), writes=["vwin_ones"])
                    P.op("dve", lambda e: e.memset(selb[:], 0.0), writes=["selb"])

                    def qknorm(bk, gcol, dst_ap, dst_names):
                        P.op("act", lambda e: e.activation(out=sqt[:], in_=banks[bk][:], func=AF.Square), reads=[B(bk)], writes=["sqt"])
                        P.op("pe", lambda e: e.matmul(banks[3][:], lhsT=ones[:], rhs=sqt[:], start=True, stop=True), reads=["ones", "sqt"], writes=[B(3)])
                        P.op("dve", lambda e: e.tensor_scalar(out=rtt[:], in0=banks[3][:], scalar1=1.0 / DH, scalar2=EPS, op0=ALU.mult, op1=ALU.add), reads=[B(3)], writes=["rtt"])
                        P.op("act", lambda e: e.activation(out=rtt[:], in_=rtt[:], func=AF.Sqrt), reads=["rtt"], writes=["rtt"])
                        P.op("dve", lambda e: e.reciprocal(out=rtt[:], in_=rtt[:]), reads=["rtt"], writes=["rtt"])
                        P.op("dve", lambda e: e.scalar_tensor_tensor(out=dst_ap, in0=banks[bk][:], scalar=gq[:, gcol:gcol + 1], in1=rtt[:], op0=ALU.mult, op1=ALU.mult),
                             reads=[B(bk), "gq", "rtt"], writes=dst_names)

                    for g in range(NG):
                        P.op("sp", lambda e, g=g: e.dma_start(out=RA[:].rearrange("p a d -> p (a d)"), in_=tb_ap["rows"][g]), writes=["RA", "RAsel"], dma=True, key="RA")
                        for hh in range(4):
                            wsl = hh % 3
                            load_w(wt1[wsl], "wt1_%d" % wsl, O_Q + (4 * g + hh) * 128, 128)

                            def ev_qn(bk, ci, hh=hh):
                                qknorm(bk, 0, qTg[:, 4 * ci:4 * ci + 4, hh * 128:(hh + 1) * 128], ["qTg"])
                            def ev_qn2(bk, ci, hh=hh):
                                P.op("act", lambda e: e.activation(out=sqt[:], in_=banks[bk][:], func=AF.Square), reads=[B(bk)], writes=["sqt"])
                                P.op("pe", lambda e: e.matmul(banks[3][:], lhsT=ones[:], rhs=sqt[:], start=True, stop=True), reads=["ones", "sqt"], writes=[B(3)])
                                P.op("dve", lambda e: e.tensor_scalar(out=rtt[:], in0=banks[3][:], scalar1=1.0 / DH, scalar2=EPS, op0=ALU.mult, op1=ALU.add), reads=[B(3)], writes=["rtt"])
                                P.op("act", lambda e: e.activation(out=rtt[:], in_=rtt[:], func=AF.Sqrt), reads=["rtt"], writes=["rtt"])
                                P.op("dve", lambda e: e.reciprocal(out=rtt[:], in_=rtt[:]), reads=["rtt"], writes=["rtt"])
                                P.op("dve", lambda e: e.scalar_tensor_tensor(out=qTg[:, 4 * ci:4 * ci + 4, hh * 128:(hh + 1) * 128], in0=banks[bk][:].rearrange("p (a q) -> p a q", a=4), scalar=gq[:, 0:1],
                                                                             in1=rtt[:].rearrange("p (a q) -> p a q", a=4), op0=ALU.mult, op1=ALU.mult),
                                     reads=[B(bk), "gq", "rtt"], writes=["qTg"])
                            fm_proj(wt1[wsl], "wt1_%d" % wsl, uTo, "uTo", range(2), lambda ci: ci, ev_qn2)
                        for slot, kind in ((0, "raw0"), (1, "raw1"), (2, "kslc"), (4, "kwin")):
                            wsl = slot % 3
                            load_w(wt1[wsl], "wt1_%d" % wsl, O_KV + slot * 512 + g * 128, 128)
                            for src, uname, base in ((uTc, "uTc", 0), (uTo, "uTo", 2)):
                                def ev_f(bk, ci, kind=kind, base=base):
                                    fsl = slice((base + ci) * 512, (base + ci + 1) * 512)
                                    if kind == "raw0":
                                        P.op("act", lambda e: e.activation(out=rawT[0][:, fsl], in_=banks[bk][:], func=AF.Copy), reads=[B(bk)], writes=["rawT0"])
                                    elif kind == "raw1":
                                        P.op("act", lambda e: e.activation(out=rawT[1][:, fsl], in_=banks[bk][:], func=AF.Copy), reads=[B(bk)], writes=["rawT1"])
                                    elif kind == "kslc":
                                        qknorm(bk, 2, kslcT[:, fsl], ["kslcT"])
                                    else:
                                        qknorm(bk, 3, kwinT[:, fsl], ["kwinT"])
                                fm_proj(wt1[wsl], "wt1_%d" % wsl, src, uname, range(2), lambda ci: ci, ev_f)
                        ws2 = 0
                        load_w(wt2[ws2], "wt2_%d" % ws2, O_KV + 3 * 512 + g * 128, 128, 0)
                        load_w(wt2[ws2], "wt2_%d" % ws2, O_KV + 5 * 512 + g * 128, 128, 128)
                        for src, uname, base in ((uTc, "uTc", 0), (uTo, "uTo", 8)):
                            def ev_v(bk, tb, base=base):
                                t = base + tb
                                P.op("act", lambda e: e.activation(out=vslc[:, t, 0:128], in_=banks[bk][:, 0:128], func=AF.Copy), reads=[B(bk)], writes=["vslc%d" % t])
                                P.op("dve", lambda e: e.tensor_copy(out=vwin[:, t, 0:128], in_=banks[bk][:, 128:256]), reads=[B(bk)], writes=["vwin%d" % t])
                            tm_proj(wt2[ws2], "wt2_%d" % ws2, 256, src, uname, range(8), lambda tb: tb % 2, ev_v)
                        for i in range(2):
                            P.op("pool", lambda e, i=i: e.dma_start(out=w1t[i][:], in_=cw1[i].rearrange("(r d) j -> d r j", d=128)), writes=["w1t0"], dma=True)
                            P.op("pool", lambda e, i=i: e.dma_start(out=w2t[i][:], in_=cw2[i]), writes=["w2t%d" % i], dma=True)
                            for r in range(32):
                                P.op("pe", lambda e, i=i, r=r: e.matmul(banks[0][:, 0:127], lhsT=w1t[i][:, r, :], rhs=rawT[i][:, r:r + 16 * 126 + 1:16], start=(r == 0), stop=(r == 31)),
                                     reads=["w1t0", "rawT%d" % i], writes=[B(0)])
                            for r in range(32):
                                P.op("pe", lambda e, i=i, r=r: e.matmul(banks[1][:, 0:1], lhsT=w1t[i][:, r, :], rhs=cposb[:, i * 32 + r:i * 32 + r + 1], start=(r == 0), stop=(r == 31)),
                                     reads=["w1t0", "cposb"], writes=[B(1)])
                            P.op("dve", lambda e, i=i: e.tensor_tensor(out=btile[:], in0=banks[1][:, 0:1], in1=cb1_t[:, i:i + 1], op=ALU.add), reads=[B(1), "cb1_t"], writes=["btile"])
                            P.op("dve", lambda e: e.memset(zf[:], 0.0), writes=["zf"])
                            P.op("act", lambda e: e.activation(out=zf[:, 0:127], in_=banks[0][:, 0:127], func=AF.Identity, bias=btile[:, 0:1]), reads=[B(0), "btile", "zf"], writes=["zf"])
                            P.op("dve", lambda e: e.tensor_tensor(out=zt_[:], in0=zf[:], in1=zf[:], op=ALU.mult), reads=["zf"], writes=["zt_"])
                            P.op("dve", lambda e: e.tensor_tensor(out=zt_[:], in0=zt_[:], in1=zf[:], op=ALU.mult), reads=["zf", "zt_"], writes=["zt_"])
                            P.op("dve", lambda e: e.scalar_tensor_tensor(out=zt_[:], in0=zt_[:], scalar=0.044715, in1=zf[:], op0=ALU.mult, op1=ALU.add), reads=["zf", "zt_"], writes=["zt_"])
                            P.op("act", lambda e: e.activation(out=zt_[:], in_=zt_[:], func=AF.Tanh, scale=float(np.sqrt(2.0 / np.pi))), reads=["zt_"], writes=["zt_"])
                            P.op("dve", lambda e: e.scalar_tensor_tensor(out=zt_[:], in0=zt_[:], scalar=1.0, in1=zf[:], op0=ALU.add, op1=ALU.mult), reads=["zf", "zt_"], writes=["zt_"])
                            P.op("act", lambda e: e.mul(out=hTs[:], in_=zt_[:], mul=0.5), reads=["zt_"], writes=["hTs"])
                            P.op("pe", lambda e, i=i: e.matmul(banks[2][:, 0:128], lhsT=hTs[:], rhs=w2t[i][:], start=True, stop=True), reads=["hTs", "w2t%d" % i], writes=[B(2)])
                            if i == 0:
                                P.op("dve", lambda e: e.memset(ss1[:], 0.0), writes=["ss1"])
                                P.op("act", lambda e: e.activation(out=kcn[:], in_=banks[2][:, 0:128], func=AF.Square, accum_out=ss1[:]), reads=[B(2), "ss1"], writes=["kcn", "ss1"])
                                P.op("dve", lambda e: e.tensor_scalar(out=ss1[:], in0=ss1[:], scalar1=1.0 / DH, scalar2=EPS, op0=ALU.mult, op1=ALU.add), reads=["ss1"], writes=["ss1"])
                                P.op("act", lambda e: e.activation(out=ss1[:], in_=ss1[:], func=AF.Sqrt), reads=["ss1"], writes=["ss1"])
                                P.op("dve", lambda e: e.reciprocal(out=ss1[:], in_=ss1[:]), reads=["ss1"], writes=["ss1"])
                                P.op("dve", lambda e: e.scalar_tensor_tensor(out=kcn[:], in0=banks[2][:, 0:128], scalar=ss1[:, 0:1], in1=gqr[:, 1, :], op0=ALU.mult, op1=ALU.mult),
                                     reads=[B(2), "ss1", "gqr", "kcn"], writes=["kcn"])
                                P.op("pe", lambda e: e.transpose(banks[3][:, 0:128], kcn[:], ident[:]), reads=["kcn", "ident"], writes=[B(3)])
                                P.op("act", lambda e: e.activation(out=kcmpT[:], in_=banks[3][:, 0:128], func=AF.Copy), reads=[B(3)], writes=["kcmpT"])
                                P.op("dve", lambda e: e.memset(kcmpT[:, 127:128], 0.0), reads=["kcmpT"], writes=["kcmpT"])
                            else:
                                P.op("dve", lambda e: e.memset(vcmp[:], 0.0), writes=["vcmp"])
                                P.op("act", lambda e: e.activation(out=vcmp[0:127, 0:128], in_=banks[2][0:127, 0:128], func=AF.Copy), reads=[B(2), "vcmp"], writes=["vcmp"])
                                P.op("dve", lambda e: e.memset(vcmp[0:127, 128:129], 1.0), reads=["vcmp"], writes=["vcmp"])
                                P.op("dve", lambda e: e.tensor_copy(out=vcmp[0:127, 129:161], in_=ovb[0:127, :]), reads=["vcmp", "ovb"], writes=["vcmp"])
                        scount = [0]

                        def score_tile(kT_ap, k_names, qb, extra, pb_i):
                            bk = scount[0] % 3
                            scount[0] += 1
                            n = len(extra)
                            P.op("pe", lambda e: e.matmul(banks[bk][:], lhsT=kT_ap, rhs=qTg[:, qb, :], start=True, stop=(n == 0)), reads=k_names + ["qTg"], writes=[B(bk)])
                            for ii, (l_ap, r_ap, names) in enumerate(extra):
                                P.op("pe", lambda e, l_ap=l_ap, r_ap=r_ap, ii=ii: e.matmul(banks[bk][:], lhsT=l_ap, rhs=r_ap, start=False, stop=(ii == n - 1)), reads=names, writes=[B(bk)])
                            P.op("act", lambda e: e.activation(out=Pb[pb_i][:], in_=banks[bk][:], func=AF.Exp), reads=[B(bk)], writes=["Pb%d" % pb_i])

                        pcount = [0]
                        accset = [0]
                        pending_finish = [None]
                        for qb in range(8):
                            gcol = lambda br, hh: br * 16 + 4 * g + hh
                            a0 = 4 + 2 * (accset[0] % 2)
                            accset[0] += 1
                            pi = pcount[0] % 3
                            pcount[0] += 1
                            cmi = qb % 2
                            P.op("sp", lambda e, qb=qb, cmi=cmi: e.dma_start(out=cmask[cmi][:], in_=tb_ap["cmask"][:, qb * 512:(qb + 1) * 512]), writes=["cmask%d" % cmi], dma=True)
                            score_tile(kcmpT[:], ["kcmpT"], qb,
                                       [(kcrows[0:10, :], RA[0:10, qb, :], ["kcrows", "RA"]), (identb[:], cmask[cmi][:], ["identb", "cmask%d" % cmi])], pi)
                            for hh in range(4):
                                bk = a0 + hh // 2
                                o0 = (hh % 2) * 161
                                P.op("pe", lambda e, bk=bk, o0=o0, hh=hh, pi=pi: e.matmul(banks[bk][:, o0:o0 + 161], lhsT=Pb[pi][:, hh * 128:(hh + 1) * 128], rhs=vcmp[:], start=True, stop=True),
                                     reads=["Pb%d" % pi, "vcmp"], writes=[B(bk)])
                            if pending_finish[0] is not None:
                                pending_finish[0]()
                                pending_finish[0] = None
                            for hh in range(4):
                                bk = a0 + hh // 2
                                o0 = (hh % 2) * 161
                                P.op("dve", lambda e, bk=bk, o0=o0, hh=hh: e.tensor_scalar_max(out=rs4[:, hh:hh + 1], in0=banks[bk][:, o0 + 128:o0 + 129], scalar1=1e-30), reads=[B(bk)], writes=["rs4"])
                            P.op("dve", lambda e: e.reciprocal(out=rs4[:], in_=rs4[:]), reads=["rs4"], writes=["rs4"])
                            for hh in range(4):
                                bk = a0 + hh // 2
                                o0 = (hh % 2) * 161
                                if hh == 0:
                                    P.op("dve", lambda e, bk=bk, o0=o0: e.tensor_scalar(out=imp[:], in0=banks[bk][:, o0 + 129:o0 + 161], scalar1=rs4[:, 0:1], scalar2=None, op0=ALU.mult), reads=[B(bk), "rs4"], writes=["imp"])
                                else:
                                    P.op("dve", lambda e, bk=bk, o0=o0, hh=hh: e.scalar_tensor_tensor(out=imp[:], in0=banks[bk][:, o0 + 129:o0 + 161], scalar=rs4[:, hh:hh + 1], in1=imp[:], op0=ALU.mult, op1=ALU.add),
                                         reads=[B(bk), "rs4", "imp"], writes=["imp"])
                            P.op("dve", lambda e, qb=qb: e.tensor_tensor(out=imp[:], in0=imp[:], in1=vm_t[:, qb, :], op=ALU.mult), reads=["imp", "vm_t"], writes=["imp"])
                            P.op("dve", lambda e, qb=qb: e.tensor_tensor(out=imp[:], in0=imp[:], in1=fb_t[:, qb, :], op=ALU.add), reads=["imp", "fb_t"], writes=["imp"])
                            P.op("dve", lambda e: e.max(out=top8[:, 0:8], in_=imp[:]), reads=["imp"], writes=["top8"])
                            P.op("dve", lambda e: e.match_replace(out=work[:], in_to_replace=top8[:, 0:8], in_values=imp[:], imm_value=-1e30), reads=["imp", "top8"], writes=["work"])
                            P.op("dve", lambda e: e.max(out=top8[:, 8:16], in_=work[:]), reads=["work", "top8"], writes=["top8"])
                            P.op("dve", lambda e: e.tensor_scalar(out=selb[:, 32:64], in0=imp[:], scalar1=top8[:, 15:16], scalar2=None, op0=ALU.is_ge), reads=["imp", "top8", "selb"], writes=["selb"])
                            P.op("dve", lambda e: e.tensor_scalar(out=selb[:, 32:64], in0=selb[:, 32:64], scalar1=-1.0, scalar2=-NEG, op0=ALU.add, op1=ALU.mult), reads=["selb"], writes=["selb"])
                            def sel_finish(qb=qb):
                                P.op("pe", lambda e: e.transpose(banks[3][0:64, 0:128], selb[:], ident[:]), reads=["selb", "ident"], writes=[B(3)])
                                P.op("dve", lambda e, qb=qb: e.tensor_copy(out=RA[32:64, qb, :].rearrange("p (a q) -> p a q", a=4), in_=banks[3][32:64, 0:128].unsqueeze(1).to_broadcast([32, 4, 128])),
                                     reads=[B(3), "RAsel"], writes=["RAsel"])
                            for hh in range(4):
                                P.op("dve", lambda e, hh=hh, qb=qb, gc=gcol(0, hh): e.tensor_tensor(out=coef[:, hh:hh + 1], in0=rs4[:, hh:hh + 1], in1=gates[:, qb, gc:gc + 1], op=ALU.mult), reads=["rs4", "gates", "coef"], writes=["coef"])
                            for hh in range(4):
                                bk = a0 + hh // 2
                                o0 = (hh % 2) * 161
                                P.op("dve", lambda e, bk=bk, o0=o0, hh=hh: e.tensor_scalar(out=otmp[:, hh, :], in0=banks[bk][:, o0:o0 + 128], scalar1=coef[:, hh:hh + 1], scalar2=None, op0=ALU.mult),
                                     reads=[B(bk), "coef", "otmp"], writes=["otmp"])
                            tiles = []
                            for br in (2, 1):
                                a0 = 4 + 2 * (accset[0] % 2)
                                accset[0] += 1
                                kbs = list(range(0, 9 + qb)) if br == 1 else list(range(4 + qb, 9 + qb))
                                for kb in kbs:
                                    tiles.append((br, kb, a0, kb == kbs[0], kb == kbs[-1]))

                            def emit_score(t):
                                br, kb, a0, first, last = t
                                Dd = 8 + qb - kb
                                pi = pcount[0] % 3
                                pcount[0] += 1
                                ksl = slice(kb * 128, (kb + 1) * 128)
                                if br == 1:
                                    extra = [(krows[0:64, kb, :], RA[0:64, qb, :], ["krows", "RA", "RAsel"])]
                                    if Dd == 0:
                                        extra.append((identb[:], tri[:, 0, :], ["identb", "tri"]))
                                    score_tile(kslcT[:, ksl], ["kslcT"], qb, extra, pi)
                                else:
                                    extra = [(krows[0:10, kb, :], RA[0:10, qb, :], ["krows", "RA"])]
                                    if Dd == 0:
                                        extra.append((identb[:], tri[:, 0, :], ["identb", "tri"]))
                                    if Dd == 4:
                                        extra.append((identb[:], tri[:, 1, :], ["identb", "tri"]))
                                    score_tile(kwinT[:, ksl], ["kwinT"], qb, extra, pi)
                                return pi

                            def emit_pv(t, pi):
                                br, kb, a0, first, last = t
                                vt, vn = (vslc, "vslc%d" % kb) if br == 1 else (vwin, "vwin%d" % kb)
                                for hh in range(4):
                                    bk = a0 + hh // 2
                                    o0 = (hh % 2) * 161
                                    P.op("pe", lambda e, bk=bk, o0=o0, hh=hh, pi=pi, vt=vt, kb=kb, st_=(first and hh % 2 == 0), sp_=(last and hh % 2 == 1): e.matmul(banks[bk][:, o0:o0 + 129], lhsT=Pb[pi][:, hh * 128:(hh + 1) * 128], rhs=vt[:, kb, :], start=st_, stop=sp_),
                                         reads=["Pb%d" % pi, vn, ("vslc_ones" if br == 1 else "vwin_ones")], writes=[B(bk)])
                                if not last:
                                    return
                                for bq in range(2):
                                    bk = a0 + bq
                                    P.op("dve", lambda e, bk=bk, bq=bq: e.tensor_scalar_max(out=rs4[:, 2 * bq:2 * bq + 2], in0=banks[bk][:, 128:128 + 162:161], scalar1=1e-30), reads=[B(bk), "rs4"], writes=["rs4"])
                                P.op("dve", lambda e: e.reciprocal(out=rs4[:], in_=rs4[:]), reads=["rs4"], writes=["rs4"])
                                gc0 = br * 16 + 4 * g
                                P.op("dve", lambda e, gc0=gc0, qb=qb: e.tensor_tensor(out=coef[:], in0=rs4[:], in1=gates[:, qb, gc0:gc0 + 4], op=ALU.mult), reads=["rs4", "gates", "coef"], writes=["coef"])
                                for hh in range(4):
                                    bk = a0 + hh // 2
                                    o0 = (hh % 2) * 161
                                    dst = otmp[:, hh, :] if br == 2 else obf[:, hh, :]
                                    dn = "otmp" if br == 2 else "obf"
                                    P.op("dve", lambda e, bk=bk, o0=o0, hh=hh, dst=dst: e.scalar_tensor_tensor(out=dst, in0=banks[bk][:, o0:o0 + 128], scalar=coef[:, hh:hh + 1], in1=otmp[:, hh, :], op0=ALU.mult, op1=ALU.add),
                                         reads=[B(bk), "coef", "otmp", dn], writes=[dn])

                            pis = [emit_score(tiles[0])]
                            for ti in range(len(tiles)):
                                if ti + 1 < len(tiles):
                                    if tiles[ti + 1][0] == 1 and tiles[ti][0] == 2:
                                        sel_finish()
                                    pis.append(emit_score(tiles[ti + 1]))
                                emit_pv(tiles[ti], pis[ti])
                            def finish(qb=qb, g=g):
                                for hh in range(4):
                                    P.op("pe", lambda e, hh=hh: e.matmul(banks[3][:, hh * 128:(hh + 1) * 128], lhsT=obf[:, hh, :], rhs=identb[:], start=(hh == 0), stop=(hh == 3)), reads=["obf", "identb"], writes=[B(3)])
                                P.op("act", lambda e: e.activation(out=onT[:, 4 * g:4 * g + 4, qb * 128:(qb + 1) * 128], in_=banks[3][:].rearrange("p (k t) -> p k t", k=4), func=AF.Copy),
                                     reads=[B(3)], writes=["onT_%d" % qb])
                            pending_finish[0] = finish
                        pending_finish[0]()
                        pending_finish[0] = None
                dump("onT", onT[:].rearrange("p k t -> p (k t)"), ["onT_%d" % c for c in range(8)])
            P.barrier()
            if stop_after == "nsa":
                P.emit(nc, final_keys=final_keys)
                return nc

            with ExitStack() as sg_:
                merged = SB(sg_, "merged", [128, 8, D], F32)
                orT2 = SB(sg_, "orT2", [128, 16, 1024], BF16)
                for tb in range(8):
                    P.op("sp", lambda e, tb=tb: e.dma_start(out=orT2[:, :, tb * 128:(tb + 1) * 128], in_=orT_d.rearrange("p (k t) -> p k t", k=16)[:, :, tb * 128:(tb + 1) * 128]),
                         reads=["orT_d"], writes=["orT_%d" % tb], dma=True, key="ld_orT")
                wpa = [SB(sg_, "wpa%d" % k, [128, 16, 256], BF16) for k in range(4)]
                sgt = [SB(sg_, "sgt%d" % k, [128, 256], F32) for k in range(2)]
                mt = [SB(sg_, "mt%d" % k, [128, 256], F32) for k in range(2)]
                for ph, (Wp, oT_, on_, gcol0) in enumerate(((wpn, onT, "onT", O_GA), (wpr, orT2, "orT", O_GB))):
                    Wpv = Wp.rearrange("(k p) n -> p k n", p=128)
                    for fb8 in range(8):
                        fsl = slice(fb8 * 256, (fb8 + 1) * 256)
                        wi = 2 * (fb8 % 2)
                        P.op("pool", lambda e, fsl=fsl, Wpv=Wpv, wi=wi: e.dma_start(out=wpa[wi][:], in_=Wpv[:, :, fsl]), writes=["wpa%d" % wi], dma=True)
                        P.op("pool", lambda e, fb8=fb8, gcol0=gcol0, wi=wi: e.dma_start(out=wpa[wi + 1][:], in_=Win[:, :, gcol0 + fb8 * 256:gcol0 + (fb8 + 1) * 256]), writes=["wpa%d" % (wi + 1)], dma=True)
                        for tb in range(8):
                            s2 = tb % 2
                            bp, bg_ = 2 * s2, 2 * s2 + 1
                            for k in range(16):
                                P.op("pe", lambda e, k=k, bp=bp, tb=tb, oT_=oT_, wi=wi: e.matmul(banks[bp][:, 0:256], lhsT=oT_[:, k, tb * 128:(tb + 1) * 128], rhs=wpa[wi][:, k, :], start=(k == 0), stop=(k == 15)),
                                     reads=["%s_%d" % (on_, tb), "wpa%d" % wi], writes=[B(bp)])
                            for k in range(16):
                                P.op("pe", lambda e, k=k, bg_=bg_, tb=tb, wi=wi: e.matmul(banks[bg_][:, 0:256], lhsT=uTo[:, k, tb * 128:(tb + 1) * 128], rhs=wpa[wi + 1][:, k, :], start=(k == 0), stop=(k == 15)),
                                     reads=["uTo_%d" % tb, "wpa%d" % (wi + 1)], writes=[B(bg_)])
                            P.op("act", lambda e, bg_=bg_, s2=s2: e.activation(out=sgt[s2][:], in_=banks[bg_][:, 0:256], func=AF.Sigmoid), reads=[B(bg_)], writes=["sgt%d" % s2])
                            if ph == 0:
                                P.op("dve", lambda e, bp=bp, s2=s2, tb=tb, fsl=fsl: e.tensor_tensor(out=merged[:, tb, fsl], in0=sgt[s2][:], in1=banks[bp][:, 0:256], op=ALU.mult),
                                     reads=["sgt%d" % s2, B(bp)], writes=["mg%d_%d" % (tb, fb8 // 2)])
                            else:
                                P.op("dve", lambda e, bp=bp, s2=s2: e.tensor_tensor(out=mt[s2][:], in0=sgt[s2][:], in1=banks[bp][:, 0:256], op=ALU.mult),
                                     reads=["sgt%d" % s2, B(bp)], writes=["mt%d" % s2])
                                P.op("dve", lambda e, s2=s2, tb=tb, fsl=fsl: e.tensor_tensor(out=merged[:, tb, fsl], in0=merged[:, tb, fsl], in1=mt[s2][:], op=ALU.add),
                                     reads=["mt%d" % s2, "mg%d_%d" % (tb, fb8 // 2)], writes=["mg%d_%d" % (tb, fb8 // 2)])
                tcount = 0
                for tb in range(8):
                    for kq in range(4):
                        bk = 4 + tcount % 4
                        tcount += 1
                        for kk in range(4):
                            P.op("pe", lambda e, bk=bk, kk=kk, kq=kq, tb=tb: e.transpose(banks[bk][:, kk * 128:(kk + 1) * 128], merged[:, tb, (4 * kq + kk) * 128:(4 * kq + kk + 1) * 128], ident[:]),
                                 reads=["mg%d_%d" % (tb, kq), "ident"], writes=[B(bk)])
                        if kq % 2 == 0:
                            P.op("act", lambda e, bk=bk, kq=kq, tb=tb: e.activation(out=onT[:, 4 * kq:4 * kq + 4, tb * 128:(tb + 1) * 128], in_=banks[bk][:].rearrange("p (k t) -> p k t", k=4), func=AF.Copy),
                                 reads=[B(bk)], writes=["onT_%d" % tb])
                        else:
                            P.op("dve", lambda e, bk=bk, kq=kq, tb=tb: e.tensor_copy(out=onT[:, 4 * kq:4 * kq + 4, tb * 128:(tb + 1) * 128], in_=banks[bk][:].rearrange("p (k t) -> p k t", k=4)),
                                 reads=[B(bk)], writes=["onT_%d" % tb])
            P.barrier()
            with ExitStack() as so:
                x_sb2 = SB(so, "x_sb2", [128, 8, D], F32)
                GT = SB(so, "GT2", [128, D], F32)
                wot = [SB(so, "wot%d" % k, [128, 16, 512], BF16) for k in range(2)]
                rt = [SB(so, "rt2_%d" % k, [128, 512], F32) for k in range(2)]
                for tb in range(8):
                    P.op("sp", lambda e, tb=tb: e.dma_start(out=x_sb2[:, tb, :], in_=x1_d[:, tb * D:(tb + 1) * D]), reads=["x1_d"], writes=["x%d_%d" % (tb, f) for f in range(4)], dma=True, key="ldx")
                P.op("sp", lambda e: e.dma_start(out=GT[:], in_=ada_d[:, 5 * D:6 * D]), reads=ada_names(5), writes=["GT2"], dma=True)
                Wov = wo.rearrange("(k p) n -> p k n", p=128)
                for fbk in range(4):
                    fsl = slice(fbk * 512, (fbk + 1) * 512)
                    ws = fbk % 2
                    P.op("pool", lambda e, fsl=fsl, ws=ws: e.dma_start(out=wot[ws][:], in_=Wov[:, :, fsl]), writes=["wot%d" % ws], dma=True)
                    for tb in range(8):
                        s2 = tb % 2
                        for k in range(16):
                            P.op("pe", lambda e, k=k, s2=s2, tb=tb, ws=ws: e.matmul(banks[s2][:], lhsT=onT[:, k, tb * 128:(tb + 1) * 128], rhs=wot[ws][:, k, :], start=(k == 0), stop=(k == 15)),
                                 reads=["onT_%d" % tb, "wot%d" % ws], writes=[B(s2)])
                        P.op("dve", lambda e, s2=s2, fsl=fsl: e.tensor_tensor(out=rt[s2][:], in0=banks[s2][:], in1=GT[:, fsl], op=ALU.mult), reads=[B(s2), "GT2"], writes=["rt2_%d" % s2])
                        P.op("dve", lambda e, s2=s2, tb=tb, fsl=fsl: e.tensor_tensor(out=x_sb2[:, tb, fsl], in0=x_sb2[:, tb, fsl], in1=rt[s2][:], op=ALU.add),
                             reads=["rt2_%d" % s2, "x%d_%d" % (tb, fbk)], writes=["x%d_%d" % (tb, fbk)])
                P.op("sp", lambda e: e.dma_start(out=x1_d, in_=x_sb2[:].rearrange("p t d -> p (t d)")), reads=xnames, writes=["x1_d"], dma=True, key="st_x1")
                dump("x2", x_sb2[:].rearrange("p t d -> p (t d)"), xnames)
            P.barrier()
        P.barrier()
        if stop_after == "mix":
            P.emit(nc, final_keys=final_keys)
            return nc

        with ExitStack() as sx:
            x_sb3 = SB(sx, "x_sb3", [128, 8, D], F32)
            hT3 = SB(sx, "hT3", [128, 16, 1024], BF16)
            for tb in range(8):
                P.op("sp", lambda e, tb=tb: e.dma_start(out=x_sb3[:, tb, :], in_=x1_d[:, tb * D:(tb + 1) * D]), reads=["x1_d"], writes=["x%d_%d" % (tb, f) for f in range(4)], dma=True, key="ldx")
            norm_mod_T(x_sb3, 2, hT3, "hT")
            ffn_core(x_sb3, hT3, 2, 1)
            ov = out.rearrange("(t p) d -> p t d", p=128)
            for tb in range(8):
                P.op("sp", lambda e, tb=tb: e.dma_start(out=ov[:, tb, :], in_=x_sb3[:, tb, :]), reads=["x%d_%d" % (tb, f) for f in range(4)], writes=["out%d" % tb], dma=True, key="st_out")
            final_keys.append("st_out")
        P.emit(nc, final_keys=final_keys)
    return nc


def prep_inputs(inp, n_pairs=4):
    f32 = np.float32
    x = np.asarray(inp["x"], f32)
    rep = lambda v: np.ascontiguousarray(np.broadcast_to(np.asarray(v, f32).reshape(1, -1), (128, np.asarray(v).size)))
    shared = {
        "w_ada": np.ascontiguousarray(np.asarray(inp["w_ada"], f32)[0]),
        "b_ada": rep(inp["b_ada"][0]),
        "gnorm": rep(inp["g_norm"][0]),
        "wg": np.ascontiguousarray(np.asarray(inp["w_ffn_gate"], f32)[0]),
        "wu": np.ascontiguousarray(np.asarray(inp["w_ffn_up"], f32)[0]),
        "wd": np.ascontiguousarray(np.asarray(inp["w_ffn_down"], f32)[0]),
        "w_in": np.ascontiguousarray(np.asarray(inp["w_in"], f32)[0]),
        "gqkT": np.ascontiguousarray(np.asarray(inp["g_qk"], f32)[0].T),
        "gqkR": rep(inp["g_qk"][0]),
        "cpos": np.ascontiguousarray(np.asarray(inp["cmp_pos"], f32)[0].transpose(2, 0, 1).reshape(128, 64)),
        "cw1": np.ascontiguousarray(np.asarray(inp["cmp_w1"], f32)[0]),
        "cb1": np.ascontiguousarray(np.asarray(inp["cmp_b1"], f32)[0].T),
        "cw2": np.ascontiguousarray(np.asarray(inp["cmp_w2"], f32)[0]),
        "gnr": rep(inp["ret_gn_gain"][0]),
        "wpn": np.ascontiguousarray(np.asarray(inp["w_proj_nsa"], f32)[0]),
        "wpr": np.ascontiguousarray(np.asarray(inp["w_proj_ret"], f32)[0]),
        "wo": np.ascontiguousarray(np.asarray(inp["w_out"], f32)[0]),
    }
    tabs = [make_tables(0), make_tables(1)]
    maps = []
    cvec = np.asarray(inp["c"], f32)
    for b in range(n_pairs):
        for j in range(2):
            m = dict(shared)
            if j == 0:
                fr = np.concatenate([np.zeros((1024, D), f32), x[b, :1024]], 0)
            else:
                fr = x[b]
            m["xf"] = np.ascontiguousarray(fr)
            m["c_l"] = np.ascontiguousarray(cvec[b].reshape(16, 128).T)
            for name, shape, dt in TABLE_SPECS:
                m["t_" + name] = np.ascontiguousarray(tabs[j][name]).reshape(shape)
            maps.append(m)
    return maps


def kernel(**inputs):
    nc = build()
    maps = prep_inputs(inputs)
    res = run_bass_kernel_spmd(nc, maps, core_ids=list(range(8)))
    outp = np.zeros((4, 2048, D), np.float32)
    for b in range(4):
        for j in range(2):
            outp[b, j * 1024:(j + 1) * 1024] = np.asarray(res.results[2 * b + j]["out"]).reshape(1024, D)
    return outp
```

```python
import contextlib
import os
from contextlib import ExitStack
import numpy as np
import ml_dtypes
import concourse.bass as bass
import concourse.mybir as mybir
from concourse.bass_utils import run_bass_kernel_spmd

F32 = mybir.dt.float32
BF16 = mybir.dt.bfloat16
ALU = mybir.AluOpType
AF = mybir.ActivationFunctionType

ENGS = ("pe", "act", "dve", "pool", "sp")
SAME_ENGINE_SYNC = True

D = 2048
DFF = 5632
NH = 16
NG = 4
DH = 128
RH = 8
RDK = 128
RDV = 256
EPS = 1e-6
NEG = -30000.0
IN_SPLITS = (2048, 3072, 48, 1024, 1024, 2048, 2048, 2048, 2048)
OFF = np.concatenate([[0], np.cumsum(IN_SPLITS)]).tolist()
O_Q, O_KV, O_GL, O_RQ, O_RK, O_RV, O_RG, O_GA, O_GB = OFF[:9]
NIN = OFF[9]


class Op:
    __slots__ = ("eng", "fn", "reads", "writes", "dma", "key", "deps", "sig", "sigidx", "dmacount", "idx", "waw", "dneed")


class Prog:
    def __init__(self):
        self.ops = []
        self.last_write = {}
        self.reads_since = {}
        self.dma_count = {}
        self.bar = set()
        self.last_eng = {}
        self.last_dma = {}

    def barrier(self):
        self.bar = set(self.last_eng.values()) | set(self.last_dma.values())

    def op(self, eng, fn, reads=(), writes=(), dma=False, key=None):
        o = Op()
        o.eng, o.fn, o.dma = eng, fn, dma
        o.reads, o.writes = tuple(reads), tuple(writes)
        o.idx = len(self.ops)
        o.sig = False
        o.sigidx = None
        o.waw = set()
        o.dneed = {}
        deps = set(self.bar)
        for r in o.reads:
            w = self.last_write.get(r)
            if w is not None:
                deps.add(w)
            if r.startswith("bank"):
                for rd in self.reads_since.get(r, ()):
                    if self.ops[rd].eng != eng:
                        deps.add(rd)
        for r in o.writes:
            w = self.last_write.get(r)
            if w is not None:
                deps.add(w)
                o.waw.add(w)
            for rd in self.reads_since.get(r, ()):
                deps.add(rd)
        o.deps = deps
        for d_ in deps:
            p_ = self.ops[d_]
            if p_.dma:
                o.dneed[p_.key] = p_.dmacount
        for r in list(o.reads) + list(o.writes):
            for d_ in [self.last_write.get(r)] + list(self.reads_since.get(r, ())):
                if d_ is not None and self.ops[d_].dma:
                    o.dneed[self.ops[d_].key] = self.dma_count[self.ops[d_].key]
        if dma:
            o.key = key if key is not None else o.writes[0]
            self.dma_count[o.key] = self.dma_count.get(o.key, 0) + 1
            o.dmacount = self.dma_count[o.key]
            self.last_dma[o.key] = o.idx
        else:
            o.key = None
            o.dmacount = 0
            self.last_eng[eng] = o.idx
        for r in o.reads:
            self.reads_since.setdefault(r, []).append(o.idx)
        for r in o.writes:
            self.last_write[r] = o.idx
            self.reads_since[r] = []
        self.ops.append(o)
        return o

    def emit(self, nc, final_keys=(), final_eng="sp"):
        ops = self.ops
        for o in ops:
            nd = set()
            for d in o.deps:
                p = ops[d]
                if p.dma:
                    if o.dma and p.key == o.key and d in o.waw:
                        continue
                    nd.add(d)
                else:
                    if p.eng == o.eng and not o.dma:
                        if p.eng == "pe":
                            continue
                        if not SAME_ENGINE_SYNC:
                            continue
                    nd.add(d)
            best = {}
            for d in nd:
                p = ops[d]
                k = ("d", p.key) if p.dma else ("e", p.eng)
                if k not in best or best[k] < d:
                    best[k] = d
            o.deps = set(best.values())
            for d in o.deps:
                if not ops[d].dma:
                    ops[d].sig = True
        cnt = {e: 0 for e in ENGS}
        for o in ops:
            if o.sig and not o.dma:
                cnt[o.eng] += 1
                o.sigidx = cnt[o.eng]
        with ExitStack() as st:
            esem = {e: st.enter_context(nc.semaphore("s_" + e)) for e in ENGS}
            dsem = {}
            for k in self.dma_count:
                dsem[k] = st.enter_context(nc.semaphore("d_%d" % len(dsem)))
            block = st.enter_context(nc.Block())

            def run_engine(ename, eng):
                waited = {}
                for o in ops:
                    if o.eng != ename:
                        continue
                    need = {}
                    for d in o.deps:
                        p = ops[d]
                        if p.dma:
                            k = ("d", p.key)
                            v = 16 * o.dneed[p.key]
                            s = dsem[p.key]
                        else:
                            k = ("e", p.eng)
                            v = p.sigidx
                            s = esem[p.eng]
                        if need.get(k, (0, None))[0] < v:
                            need[k] = (v, s)
                    for k, (v, s) in need.items():
                        if waited.get(k, 0) >= v:
                            continue
                        eng.wait_ge(s, v)
                        waited[k] = v
                    ins = o.fn(eng)
                    if o.dma:
                        ins.then_inc(dsem[o.key], 16)
                    elif o.sig:
                        ins.then_inc(esem[o.eng], 1)
                if ename == final_eng:
                    for k in self.dma_count:
                        eng.wait_ge(dsem[k], 16 * self.dma_count[k])

            @block.tensor
            def _(e):
                run_engine("pe", e)

            @block.scalar
            def _(e):
                run_engine("act", e)

            @block.vector
            def _(e):
                run_engine("dve", e)

            @block.gpsimd
            def _(e):
                run_engine("pool", e)

            @block.sync
            def _(e):
                run_engine("sp", e)

    def bar_of(self, o):
        return ()


def _split3(a):
    a = np.asarray(a, np.float32)
    hi = a.astype(ml_dtypes.bfloat16)
    r1 = a - hi.astype(np.float32)
    lo = r1.astype(ml_dtypes.bfloat16)
    r2 = r1 - lo.astype(np.float32)
    ll = r2.astype(ml_dtypes.bfloat16)
    return hi, lo, ll


def make_tables(j):
    bf = ml_dtypes.bfloat16
    T = {}
    slopes = np.exp2(-8.0 * np.arange(1, NH + 1, dtype=np.float32) / NH).astype(np.float32)
    rows = np.zeros((NG, 64, 8, 4, 128), np.float32).astype(bf)
    ql = np.arange(128, dtype=np.float32)
    for g in range(NG):
        for hh in range(4):
            s = slopes[4 * g + hh]
            s3 = _split3(np.full((128,), s, np.float32))
            for qb in range(8):
                tq = (1024 + 128 * qb + ql).astype(np.float32)
                a3 = _split3((-s * tq).astype(np.float32))
                for r in range(3):
                    rows[g, r, qb, hh] = a3[r]
                    rows[g, 3 + r, qb, hh] = s3[r]
                    rows[g, 6 + r, qb, hh] = s3[r]
                rows[g, 9, qb, hh] = (0.0 if j == 1 else NEG)
    T["rows"] = rows.reshape(NG, 64, 8 * 512)
    kr = np.zeros((64, 16, 128), np.float32)
    kl = np.arange(128, dtype=np.float32)
    for kb in range(16):
        kr[0:3, kb] = 1.0
        kr[3:6, kb] = kl[None, :]
        kr[6:9, kb] = 128.0 * kb
        kr[9, kb] = 1.0 if kb < 8 else 0.0
        for jj in range(32):
            kr[32 + jj, kb] = ((2 * kb + (np.arange(128) // 64)) == jj).astype(np.float32)
    T["krows"] = kr.astype(bf).reshape(64, 16 * 128)
    kc = np.zeros((10, 128), np.float32)
    c = np.arange(127, dtype=np.float32)
    kc[0:3, :127] = 1.0
    kc[3:6, :127] = 16.0 * c
    kc[6:9, :127] = 15.5
    T["kcrows"] = kc.astype(bf)
    cm = np.zeros((128, 8, 4, 128), np.float32)
    for qb in range(8):
        tq = 1024 + 128 * qb + np.arange(128)
        cc = np.arange(127)
        valid = (16 * cc[:, None] + 31) <= tq[None, :]
        if j == 0:
            valid = valid & (16 * cc[:, None] >= 1024)
        cm[:127, qb] = np.where(valid, 0.0, NEG)[:, None, :]
    T["cmask"] = cm.astype(bf).reshape(128, 8 * 512)
    lo = np.where(kl[:, None] <= ql[None, :], 0.0, NEG).astype(np.float32)
    up = np.where(kl[:, None] > ql[None, :], 0.0, NEG).astype(np.float32)
    T["tri"] = np.stack([np.repeat(lo[:, None, :], 4, 1), np.repeat(up[:, None, :], 4, 1)], 1).astype(bf).reshape(128, 2 * 512)
    cs = (16 * np.arange(127))[:, None]
    js = (64 * np.arange(32))[None, :]
    ov = np.clip(np.minimum(cs + 32, js + 64) - np.maximum(cs, js), 0, None).astype(np.float32) / 32.0
    ovp = np.zeros((128, 32), np.float32)
    ovp[:127] = ov
    T["ov"] = ovp.astype(bf)
    vm = np.zeros((128, 8, 32), np.float32)
    fb = np.zeros((128, 8, 32), np.float32)
    for qb in range(8):
        for q in range(128):
            tf = 1024 + 128 * qb + q
            ta = tf - (0 if j == 1 else 1024)
            bt = ta // 64
            for jf in range(32):
                ja = jf - (0 if j == 1 else 16)
                if ja < 0:
                    fb[q, qb, jf] = -2e4
                elif ja == 0 or ja == bt or ja == bt - 1:
                    fb[q, qb, jf] = 1e4
                elif ja <= bt:
                    vm[q, qb, jf] = 1.0
                else:
                    fb[q, qb, jf] = -1e4
    T["vm"] = vm.reshape(128, 256)
    T["fb"] = fb.reshape(128, 256)
    hh = np.arange(RH, dtype=np.float64)
    lg = np.log1p(-np.exp2(-5.0 - hh))
    n = np.arange(128, dtype=np.float64)
    diff = n[None, :] - n[:, None]
    dec = np.where(diff[None] >= 0, np.exp(lg[:, None, None] * np.maximum(diff[None], 0.0)), 0.0)
    T["decT"] = (dec * (RDK ** -0.5)).transpose(1, 0, 2).astype(np.float32).reshape(128, RH * 128)
    xi = np.exp(lg[:, None] * (n + 1.0)[None, :])
    T["xi"] = np.repeat(xi.astype(np.float32)[None], 128, 0).reshape(128, RH * 128)
    zeta = np.exp(lg[:, None] * (127 - n)[None, :]) * (RDK ** -0.5)
    zt = np.zeros((128, 16, RH), np.float32)
    for t in range(16):
        if t < 8:
            z = np.exp(lg[:, None] * (1023 - (128 * t + n))[None, :]) * (RDK ** -0.5)
            zt[:, t, :] = (z.T if j == 1 else 0.0)
        else:
            zt[:, t, :] = zeta.T
    T["zt"] = zt.reshape(128, 16 * RH)
    T["gC"] = [float(np.exp(lg[h] * 128)) for h in range(RH)]
    T["ident"] = np.eye(128, dtype=np.float32)
    T["identb"] = np.eye(128, dtype=np.float32).astype(bf)
    T["ones"] = np.ones((128, 128), np.float32)
    return T


TABLE_SPECS = [
    ("rows", [NG, 64, 4096], BF16), ("krows", [64, 2048], BF16), ("kcrows", [10, 128], BF16),
    ("cmask", [128, 4096], BF16), ("tri", [128, 1024], BF16), ("ov", [128, 32], BF16),
    ("vm", [128, 256], F32), ("fb", [128, 256], F32), ("decT", [128, 1024], F32), ("xi", [128, 1024], F32),
    ("zt", [128, 128], F32), ("ident", [128, 128], F32), ("identb", [128, 128], BF16), ("ones", [128, 128], F32),
]
GC = make_tables(1)["gC"]


def build(stop_after=None, dbg=(), mixer_only=False):
    nc = bass.Bass("TRN2", target_bir_lowering=False)
    P = Prog()

    def din(name, shape, dt=F32):
        return nc.dram_tensor(name, shape, dt, kind="ExternalInput").ap()

    xf = din("xf", [2048, D])
    c_l = din("c_l", [128, 16])
    if not mixer_only:
        w_ada = din("w_ada", [D, 9 * D])
        b_ada = din("b_ada", [128, 9 * D])
        gnorm = din("gnorm", [128, 3 * D])
        wg = din("wg", [2, D, DFF])
        wu = din("wu", [2, D, DFF])
        wd = din("wd", [2, DFF, D])
    w_in = din("w_in", [D, NIN])
    gqkT = din("gqkT", [128, 4])
    gqkR = din("gqkR", [128, 4 * 128])
    cpos = din("cpos", [128, 2 * 32])
    cw1 = din("cw1", [2, 4096, 128])
    cb1 = din("cb1", [128, 2])
    cw2 = din("cw2", [2, 128, 128])
    gnr = din("gnr", [128, RH * RDV])
    wpn = din("wpn", [D, D])
    wpr = din("wpr", [D, D])
    wo = din("wo", [D, D])
    tb_ap = {}
    for name, shape, dt in TABLE_SPECS:
        tb_ap[name] = din("t_" + name, shape, dt)
    out = nc.dram_tensor("out", [1024, D], F32, kind="ExternalOutput").ap()
    dbg_ap = {}
    for name, shape, dt in dbg:
        dbg_ap[name] = nc.dram_tensor("dbg_" + name, shape, dt, kind="ExternalOutput").ap()
    if mixer_only:
        ada_d = din("ada_in", [128, 9 * D])
        uT_d = din("uT_in", [2, 128, 16 * 1024], BF16)
        x1_i = din("x1_in", [128, 8 * D])
        x1_d = nc.dram_tensor("x1_d", [128, 8 * D], F32, kind="Internal").ap()
    else:
        ada_d = nc.dram_tensor("ada_d", [128, 9 * D], F32, kind="Internal").ap()
        uT_d = nc.dram_tensor("uT_d", [2, 128, 16 * 1024], BF16, kind="Internal").ap()
        x1_d = nc.dram_tensor("x1_d", [128, 8 * D], F32, kind="Internal").ap()
    orT_d = nc.dram_tensor("orT_d", [128, 16 * 1024], BF16, kind="Internal").ap()

    final_keys = []
    outer = ExitStack()
    with outer:
        uid = [0]

        def SB(st, name, shape, dt):
            uid[0] += 1
            return st.enter_context(nc.sbuf_tensor("%s_%d" % (name, uid[0]), shape, dt))

        banks = [outer.enter_context(nc.psum_tensor("bank%d" % i, [128, 512], F32)) for i in range(8)]
        ident = SB(outer, "ident", [128, 128], F32)
        identb = SB(outer, "identb", [128, 128], BF16)
        ones = SB(outer, "ones", [128, 128], F32)
        P.op("sp", lambda e: e.dma_start(out=ident[:], in_=tb_ap["ident"]), writes=["ident"], dma=True, key="tbl")
        P.op("sp", lambda e: e.dma_start(out=identb[:], in_=tb_ap["identb"]), writes=["identb"], dma=True, key="tbl")
        P.op("sp", lambda e: e.dma_start(out=ones[:], in_=tb_ap["ones"]), writes=["ones"], dma=True, key="tbl")

        def B(i):
            return "bank%d" % i

        def dump(name, src_ap, reads):
            if name in dbg_ap:
                k = "dbg_" + name
                P.op("sp", lambda e: e.dma_start(out=dbg_ap[name], in_=src_ap), reads=reads, writes=[k], dma=True, key=k)
                if k not in final_keys:
                    final_keys.append(k)

        ada_t = {}

        def ada_init(st):
            c_sb = SB(st, "c_sb", [128, 16], F32)
            cond = SB(st, "cond", [128, 16], F32)
            ada_t["condrep"] = SB(st, "condrep", [128, 16, 128], BF16)
            ada_t["wa"] = [SB(st, "wa%d" % i, [128, 16, 256], BF16) for i in range(2)]
            ada_t["ba"] = [SB(st, "ba%d" % i, [128, 256], F32) for i in range(2)]
            ada_t["rs"] = [SB(st, "rs%d" % i, [128, 256], F32) for i in range(2)]
            ada_t["gs"] = [SB(st, "gs%d" % i, [128, 256], F32) for i in range(2)]
            condrep = ada_t["condrep"]
            P.op("sp", lambda e: e.dma_start(out=c_sb[:], in_=c_l), writes=["c_sb"], dma=True, key="tbl")
            P.op("act", lambda e: e.activation(out=cond[:], in_=c_sb[:], func=AF.Silu), reads=["c_sb"], writes=["cond"])
            P.op("dve", lambda e: e.tensor_copy(out=condrep[:], in_=cond[:].unsqueeze(2).to_broadcast([128, 16, 128])),
                 reads=["cond"], writes=["condrep"])

        def ada_block(cb):
            condrep, wa, ba, rs, gs = ada_t["condrep"], ada_t["wa"], ada_t["ba"], ada_t["rs"], ada_t["gs"]
            wsrc = w_ada.rearrange("(k p) n -> p k n", p=128)
            s = cb % 2
            slot, fbk = cb // 8, cb % 8
            i, kind = slot // 3, slot % 3
            cs = slice(cb * 256, (cb + 1) * 256)
            bk = s
            P.op("pool", lambda e: e.dma_start(out=wa[s][:], in_=wsrc[:, :, cs]), writes=["wa%d" % s], dma=True)
            P.op("sp", lambda e: e.dma_start(out=ba[s][:], in_=b_ada[:, cs]), writes=["ba%d" % s], dma=True)
            for k in range(16):
                P.op("pe", lambda e, k=k: e.matmul(banks[bk][:, 0:256], lhsT=condrep[:, k, :], rhs=wa[s][:, k, :], start=(k == 0), stop=(k == 15)),
                     reads=["condrep", "wa%d" % s], writes=[B(bk)])
            P.op("dve", lambda e: e.tensor_tensor(out=rs[s][:], in0=banks[bk][:, 0:256], in1=ba[s][:], op=ALU.add),
                 reads=[B(bk), "ba%d" % s], writes=["rs%d" % s])
            if kind == 1:
                gsl = slice(i * D + fbk * 256, i * D + (fbk + 1) * 256)
                P.op("sp", lambda e: e.dma_start(out=gs[s][:], in_=gnorm[:, gsl]), writes=["gs%d" % s], dma=True)
                P.op("dve", lambda e: e.scalar_tensor_tensor(out=rs[s][:], in0=rs[s][:], scalar=1.0, in1=gs[s][:], op0=ALU.add, op1=ALU.mult),
                     reads=["rs%d" % s, "gs%d" % s], writes=["rs%d" % s])
            elif kind == 2 and i != 1:
                P.op("dve", lambda e: e.tensor_scalar_mul(out=rs[s][:], in0=rs[s][:], scalar1=0.5), reads=["rs%d" % s], writes=["rs%d" % s])
            P.op("sp", lambda e: e.dma_start(out=ada_d[:, cs], in_=rs[s][:]), reads=["rs%d" % s], writes=["ada_d%d" % cb], dma=True, key="st_rs%d" % s)

        def ada_names(slot):
            return ["ada_d%d" % cb for cb in range(slot * 8, slot * 8 + 8)]

        def norm_mod_T(x_sb, i, dstT, dst_name):
            with ExitStack() as st:
                G = SB(st, "G", [128, D], F32)
                SH = SB(st, "SH", [128, D], F32)
                ss = SB(st, "ss", [128, 8], F32)
                junk = SB(st, "junk", [128, D], F32)
                hf = [SB(st, "hf%d" % k, [128, D], F32) for k in range(2)]
                P.op("sp", lambda e: e.dma_start(out=G[:], in_=ada_d[:, (3 * i + 1) * D:(3 * i + 2) * D]),
                     reads=ada_names(3 * i + 1), writes=["G"], dma=True)
                P.op("sp", lambda e: e.dma_start(out=SH[:], in_=ada_d[:, (3 * i) * D:(3 * i + 1) * D]),
                     reads=ada_names(3 * i), writes=["SH"], dma=True)
                P.op("dve", lambda e: e.memset(ss[:], 0.0), writes=["ss"])
                for tb in range(8):
                    P.op("act", lambda e, tb=tb: e.activation(out=junk[:], in_=x_sb[:, tb, :], func=AF.Square, accum_out=ss[:, tb:tb + 1]),
                         reads=["x%d_%d" % (tb, f) for f in range(4)], writes=["junk", "ss"])
                P.op("dve", lambda e: e.tensor_scalar(out=ss[:], in0=ss[:], scalar1=1.0 / D, scalar2=EPS, op0=ALU.mult, op1=ALU.add), reads=["ss"], writes=["ss"])
                P.op("act", lambda e: e.activation(out=ss[:], in_=ss[:], func=AF.Sqrt), reads=["ss"], writes=["ss"])
                P.op("dve", lambda e: e.reciprocal(out=ss[:], in_=ss[:]), reads=["ss"], writes=["ss"])
                tcount = 0
                for tb in range(8):
                    s = tb % 2
                    P.op("dve", lambda e, tb=tb, s=s: e.scalar_tensor_tensor(out=hf[s][:], in0=x_sb[:, tb, :], scalar=ss[:, tb:tb + 1], in1=G[:], op0=ALU.mult, op1=ALU.mult),
                         reads=["x%d_%d" % (tb, f) for f in range(4)] + ["ss", "G"], writes=["hf%d" % s])
                    P.op("dve", lambda e, s=s: e.tensor_tensor(out=hf[s][:], in0=hf[s][:], in1=SH[:], op=ALU.add), reads=["hf%d" % s, "SH"], writes=["hf%d" % s])
                    for kq in range(4):
                        bk = tcount % 8
                        tcount += 1
                        for kk in range(4):
                            P.op("pe", lambda e, s=s, bk=bk, kk=kk, kq=kq: e.transpose(banks[bk][:, kk * 128:(kk + 1) * 128], hf[s][:, (4 * kq + kk) * 128:(4 * kq + kk + 1) * 128], ident[:]),
                                 reads=["hf%d" % s, "ident"], writes=[B(bk)])
                        eng = "act" if kq % 2 == 0 else "dve"
                        if eng == "act":
                            P.op("act", lambda e, bk=bk, kq=kq, tb=tb: e.activation(out=dstT[:, 4 * kq:4 * kq + 4, tb * 128:(tb + 1) * 128], in_=banks[bk][:].rearrange("p (k t) -> p k t", k=4), func=AF.Copy),
                                 reads=[B(bk)], writes=["%s_%d" % (dst_name, tb)])
                        else:
                            P.op("dve", lambda e, bk=bk, kq=kq, tb=tb: e.tensor_copy(out=dstT[:, 4 * kq:4 * kq + 4, tb * 128:(tb + 1) * 128], in_=banks[bk][:].rearrange("p (k t) -> p k t", k=4)),
                                 reads=[B(bk)], writes=["%s_%d" % (dst_name, tb)])
            P.barrier()

        def ffn_core(x_sb, hT, i, l, hook=None):
            Wg = wg[l].rearrange("(k p) n -> p k n", p=128)
            Wu = wu[l].rearrange("(k p) n -> p k n", p=128)
            Wd = wd[l].rearrange("(c p) n -> p c n", p=128)
            with ExitStack() as st:
                GT = SB(st, "GT", [128, D], F32)
                hid = SB(st, "hid", [128, 11, 1024], BF16)
                wgt = [SB(st, "wgt%d" % k, [128, 16, 128], BF16) for k in range(2)]
                wut = [SB(st, "wut%d" % k, [128, 16, 128], BF16) for k in range(2)]
                wdt = [SB(st, "wdt%d" % k, [128, 11, 512], BF16) for k in range(2)]
                stt = [SB(st, "stt%d" % k, [128, 512], F32) for k in range(2)]
                rt = [SB(st, "rt%d" % k, [128, 512], F32) for k in range(2)]
                hT_names = ["hT_%d" % tb for tb in range(8)]
                step = 0
                oset = 0
                for r in range(4):
                    for cc in range(11):
                        c = r * 11 + cc
                        s = c % 2
                        csl = slice(c * 128, (c + 1) * 128)
                        P.op("pool", lambda e, s=s, csl=csl: e.dma_start(out=wgt[s][:], in_=Wg[:, :, csl]), writes=["wgt%d" % s], dma=True)
                        P.op("pool", lambda e, s=s, csl=csl: e.dma_start(out=wut[s][:], in_=Wu[:, :, csl]), writes=["wut%d" % s], dma=True)
                        for half in range(2):
                            bg, bu = 4 + 2 * (step % 2), 5 + 2 * (step % 2)
                            ss_ = step % 2
                            step += 1
                            tsl = slice(half * 512, (half + 1) * 512)
                            hr = hT_names[half * 4:(half + 1) * 4]
                            for k in range(16):
                                P.op("pe", lambda e, s=s, k=k, bg=bg, tsl=tsl: e.matmul(banks[bg][:], lhsT=wgt[s][:, k, :], rhs=hT[:, k, tsl], start=(k == 0), stop=(k == 15)),
                                     reads=["wgt%d" % s] + hr, writes=[B(bg)])
                            for k in range(16):
                                P.op("pe", lambda e, s=s, k=k, bu=bu, tsl=tsl: e.matmul(banks[bu][:], lhsT=wut[s][:, k, :], rhs=hT[:, k, tsl], start=(k == 0), stop=(k == 15)),
                                     reads=["wut%d" % s] + hr, writes=[B(bu)])
                            P.op("act", lambda e, bg=bg, ss_=ss_: e.activation(out=stt[ss_][:], in_=banks[bg][:], func=AF.Silu), reads=[B(bg)], writes=["stt%d" % ss_])
                            P.op("dve", lambda e, bu=bu, ss_=ss_, cc=cc, tsl=tsl: e.tensor_tensor(out=hid[:, cc, tsl], in0=stt[ss_][:], in1=banks[bu][:], op=ALU.mult),
                                 reads=["stt%d" % ss_, B(bu)], writes=["hid%d_%d" % (cc, half)])
                        if hook is not None:
                            hook(c)
                    if r == 0:
                        P.op("sp", lambda e: e.dma_start(out=GT[:], in_=ada_d[:, (3 * i + 2) * D:(3 * i + 3) * D]),
                             reads=ada_names(3 * i + 2), writes=["GT"], dma=True)
                    for fbk in range(4):
                        ws = (r * 4 + fbk) % 2
                        fsl = slice(fbk * 512, (fbk + 1) * 512)
                        P.op("pool", lambda e, ws=ws, r=r, fsl=fsl: e.dma_start(out=wdt[ws][:], in_=Wd[:, r * 11:(r + 1) * 11, fsl]), writes=["wdt%d" % ws], dma=True)
                        for tp in range(4):
                            ob = [2 * (oset % 2), 2 * (oset % 2) + 1]
                            oset += 1
                            for cc in range(11):
                                for t2 in range(2):
                                    tb = 2 * tp + t2
                                    P.op("pe", lambda e, ws=ws, cc=cc, tb=tb, b_=ob[t2]: e.matmul(banks[b_][:], lhsT=hid[:, cc, tb * 128:(tb + 1) * 128], rhs=wdt[ws][:, cc, :], start=(cc == 0), stop=(cc == 10)),
                                         reads=["hid%d_%d" % (cc, tb // 4), "wdt%d" % ws], writes=[B(ob[t2])])
                            for t2 in range(2):
                                tb = 2 * tp + t2
                                P.op("dve", lambda e, t2=t2, b_=ob[t2], fsl=fsl: e.tensor_tensor(out=rt[t2][:], in0=banks[b_][:], in1=GT[:, fsl], op=ALU.mult),
                                     reads=[B(ob[t2]), "GT"], writes=["rt%d" % t2])
                                P.op("dve", lambda e, t2=t2, tb=tb, fsl=fsl: e.tensor_tensor(out=x_sb[:, tb, fsl], in0=x_sb[:, tb, fsl], in1=rt[t2][:], op=ALU.add),
                                     reads=["rt%d" % t2, "x%d_%d" % (tb, fbk)], writes=["x%d_%d" % (tb, fbk)])
            P.barrier()

        xnames = ["x%d_%d" % (tb, f) for tb in range(8) for f in range(4)]

        def load_x(x_sb, src):
            v = src.rearrange("(t p) d -> p t d", p=128)
            for tb in range(8):
                P.op("sp", lambda e, tb=tb: e.dma_start(out=x_sb[:, tb, :], in_=v[:, tb, :]), writes=["x%d_%d" % (tb, f) for f in range(4)], dma=True, key="ldx")

        sada = ExitStack()
        ada_pending = list(range(16, 72))
        if not mixer_only:
            ada_init(sada)
            for cb in range(16):
                ada_block(cb)
        else:
            with ExitStack() as sx0:
                xt0 = SB(sx0, "xt0", [128, 8 * D], F32)
                P.op("sp", lambda e: e.dma_start(out=xt0[:], in_=x1_i), writes=["xt0"], dma=True)
                P.op("sp", lambda e: e.dma_start(out=x1_d, in_=xt0[:]), reads=["xt0"], writes=["x1_d"], dma=True, key="st_x1")
            P.barrier()
        with ExitStack() as sx:
            if not mixer_only:
                x_sb = SB(sx, "x_sb", [128, 8, D], F32)
                hT = SB(sx, "hT", [128, 16, 1024], BF16)
                uT = hT
            for half in ((0, 1) if not mixer_only else ()):
                load_x(x_sb, xf[half * 1024:(half + 1) * 1024, :])
                norm_mod_T(x_sb, 0, hT, "hT")
                lim = 48 if half == 0 else 72

                def hook(c, lim=lim):
                    if ada_pending and ada_pending[0] < lim:
                        ada_block(ada_pending.pop(0))
                ffn_core(x_sb, hT, 0, 0, hook=hook)
                assert not ada_pending or ada_pending[0] >= lim
                norm_mod_T(x_sb, 1, uT, "uT")
                P.op("sp", lambda e, half=half: e.dma_start(out=uT_d[half], in_=uT[:].rearrange("p k t -> p (k t)")),
                     reads=["uT_%d" % tb for tb in range(8)], writes=["uT_d%d" % half], dma=True, key="st_uT")
                if half == 1:
                    P.op("sp", lambda e: e.dma_start(out=x1_d, in_=x_sb[:].rearrange("p t d -> p (t d)")), reads=xnames, writes=["x1_d"], dma=True, key="st_x1")
                    dump("x1", x_sb[:].rearrange("p t d -> p (t d)"), xnames)
                    dump("uT", uT[:].rearrange("p k t -> p (k t)"), ["uT_%d" % tb for tb in range(8)])
                P.barrier()
        sada.close()
        P.barrier()
        if stop_after == "ffn1":
            P.emit(nc, final_keys=final_keys)
            return nc

        Win = w_in.rearrange("(k p) n -> p k n", p=128)

        def load_w(tile, tname, col0, ncols, dst0=0):
            P.op("pool", lambda e: e.dma_start(out=tile[:, :, dst0:dst0 + ncols], in_=Win[:, :, col0:col0 + ncols]), writes=[tname], dma=True)

        def fm_proj(wt, wname, uT_, uname, chunks, bank_of, evac):
            for ci in chunks:
                bk = bank_of(ci)
                rn = ["%s_%d" % (uname, tb) for tb in range(4 * ci, 4 * ci + 4)]
                for k in range(16):
                    P.op("pe", lambda e, k=k, bk=bk, ci=ci: e.matmul(banks[bk][:], lhsT=wt[:, k, 0:128], rhs=uT_[:, k, ci * 512:(ci + 1) * 512], start=(k == 0), stop=(k == 15)),
                         reads=[wname] + rn, writes=[B(bk)])
                evac(bk, ci)

        def tm_proj(wt, wname, ncols, uT_, uname, tbs, bank_of, evac):
            for tb in tbs:
                bk = bank_of(tb)
                for k in range(16):
                    P.op("pe", lambda e, k=k, bk=bk, tb=tb: e.matmul(banks[bk][:, 0:ncols], lhsT=uT_[:, k, tb * 128:(tb + 1) * 128], rhs=wt[:, k, 0:ncols], start=(k == 0), stop=(k == 15)),
                         reads=[wname, "%s_%d" % (uname, tb)], writes=[B(bk)])
                evac(bk, tb)

        with ExitStack() as sm:
            uTo = SB(sm, "uTo", [128, 16, 1024], BF16)
            onT = SB(sm, "onT", [128, 16, 1024], BF16)
            uTo_n = ["uTo_%d" % tb for tb in range(8)]
            for tb in range(8):
                P.op("sp", lambda e, tb=tb: e.dma_start(out=uTo[:, :, tb * 128:(tb + 1) * 128], in_=uT_d[1].rearrange("p (k t) -> p k t", k=16)[:, :, tb * 128:(tb + 1) * 128]),
                     reads=["uT_d1"], writes=["uTo_%d" % tb], dma=True, key="ld_uTo")
            with ExitStack() as sa:
                uTc = SB(sa, "uTc", [128, 16, 1024], BF16)
                for tb in range(8):
                    P.op("sp", lambda e, tb=tb: e.dma_start(out=uTc[:, :, tb * 128:(tb + 1) * 128], in_=uT_d[0].rearrange("p (k t) -> p k t", k=16)[:, :, tb * 128:(tb + 1) * 128]),
                         reads=["uT_d0"], writes=["uTc_%d" % tb], dma=True, key="ld_uTc")
                gates = SB(sa, "gates", [128, 8, 48], F32)
                wt1 = [SB(sa, "wt1_%d" % k, [128, 16, 128], BF16) for k in range(3)]
                load_w(wt1[0], "wt1_0", O_GL, 48)

                def ev_gl(bk, tb):
                    P.op("act", lambda e: e.activation(out=gates[:, tb, :], in_=banks[bk][:, 0:48], func=AF.Sigmoid), reads=[B(bk)], writes=["gates"])
                tm_proj(wt1[0], "wt1_0", 48, uTo, "uTo", range(8), lambda tb: tb % 2, ev_gl)
                P.barrier()
                if stop_after == "gates":
                    P.op("sp", lambda e: e.dma_start(out=out[0:128, 0:384], in_=gates[:].rearrange("p a b -> p (a b)")), reads=["gates"], writes=["outg"], dma=True, key="st_out")
                    final_keys.append("st_out")
                    P.emit(nc, final_keys=final_keys)
                    return nc

                with ExitStack() as sr:
                    orT = SB(sr, "orT", [128, 16, 1024], BF16)
                    decT = SB(sr, "decT", [128, 1, 128], F32)
                    xi_t = SB(sr, "xi_t", [128, 1, 128], F32)
                    zt = SB(sr, "zt", [128, 16, RH], F32)
                    gnr_t = SB(sr, "gnr_t", [128, 1, RDV], F32)
                    P.op("sp", lambda e: e.dma_start(out=zt[:].rearrange("p t h -> p (t h)"), in_=tb_ap["zt"]), writes=["zt"], dma=True, key="tbl")
                    qT = SB(sr, "r_qT", [128, 1024], BF16)
                    qxT = SB(sr, "r_qxT", [128, 1024], BF16)
                    kT = SB(sr, "r_kT", [128, 1024], BF16)
                    kz = SB(sr, "r_kz", [128, 16, 128], BF16)
                    vv = SB(sr, "r_v", [128, 16, 256], BF16)
                    sg = SB(sr, "r_sg", [128, 8, 256], F32)
                    state = SB(sr, "r_state", [128, 256], F32)
                    stbf = SB(sr, "r_stbf", [128, 256], BF16)
                    sTb = [SB(sr, "r_sTb%d" % k, [128, 128], BF16) for k in range(2)]
                    t1 = [SB(sr, "r_t1%d" % k, [128, 256], F32) for k in range(2)]
                    ob = [SB(sr, "r_ob%d" % k, [128, 256], BF16) for k in range(2)]
                    ysum = SB(sr, "r_ysum", [128, 8], F32)
                    ysq = SB(sr, "r_ysq", [128, 8], F32)
                    mean = SB(sr, "r_mean", [128, 8], F32)
                    rstd = SB(sr, "r_rstd", [128, 8], F32)
                    junk2 = SB(sr, "r_junk", [128, 256], F32)
                    wkv = SB(sr, "r_wkv", [128, 16, 384], BF16)
                    wgr = SB(sr, "r_wgr", [128, 16, 256], BF16)
                    for h in range(RH):
                        P.op("sp", lambda e, h=h: e.dma_start(out=decT[:, 0, :], in_=tb_ap["decT"][:, h * 128:(h + 1) * 128]), writes=["decT"], dma=True)
                        P.op("sp", lambda e, h=h: e.dma_start(out=xi_t[:, 0, :], in_=tb_ap["xi"][:, h * 128:(h + 1) * 128]), writes=["xi_t"], dma=True)
                        P.op("sp", lambda e, h=h: e.dma_start(out=gnr_t[:, 0, :], in_=gnr[:, h * 256:(h + 1) * 256]), writes=["gnr_t"], dma=True)
                        load_w(wt1[1], "wt1_1", O_RQ + h * 128, 128)
                        load_w(wt1[2], "wt1_2", O_RK + h * 128, 128)
                        load_w(wkv, "r_wkv", O_RK + h * 128, 128, 0)
                        load_w(wkv, "r_wkv", O_RV + h * 256, 256, 128)
                        load_w(wgr, "r_wgr", O_RG + h * 256, 256)

                        RC = int(os.environ.get("RET_CUT2", "99"))
                        if RC < 2:
                            break

                        def ev_q(bk, ci, h=h):
                            P.op("act", lambda e: e.activation(out=qT[:, ci * 512:(ci + 1) * 512], in_=banks[bk][:], func=AF.Copy), reads=[B(bk)], writes=["r_qT%d" % ci])
                            for c4 in range(4):
                                P.op("dve", lambda e, c4=c4: e.tensor_tensor(out=qxT[:, ci * 512 + c4 * 128:ci * 512 + (c4 + 1) * 128], in0=banks[bk][:, c4 * 128:(c4 + 1) * 128],
                                                                             in1=xi_t[:, 0, :], op=ALU.mult), reads=[B(bk), "xi_t"], writes=["r_qxT%d" % ci])
                        fm_proj(wt1[1], "wt1_1", uTo, "uTo", range(2), lambda ci: ci, ev_q)

                        if RC < 3:
                            break

                        def ev_k(bk, ci):
                            P.op("act", lambda e: e.activation(out=kT[:, ci * 512:(ci + 1) * 512], in_=banks[bk][:], func=AF.Copy), reads=[B(bk)], writes=["r_kT%d" % ci])
                        fm_proj(wt1[2], "wt1_2", uTo, "uTo", range(2), lambda ci: 2 + ci, ev_k)

                        if RC < 4:
                            break

                        def ev_kv(bk, t, h=h):
                            P.op("dve", lambda e: e.tensor_scalar(out=kz[:, t, :], in0=banks[bk][:, 0:128], scalar1=zt[:, t, h:h + 1], scalar2=None, op0=ALU.mult), reads=[B(bk), "zt"], writes=["r_kz%d" % t])
                            P.op("act", lambda e: e.activation(out=vv[:, t, :], in_=banks[bk][:, 128:384], func=AF.Copy), reads=[B(bk)], writes=["r_v%d" % t])
                        tm_proj(wkv, "r_wkv", 384, uTc, "uTc", range(8), lambda tb: 4 + tb % 2, ev_kv)

                        if RC < 5:
                            break

                        def ev_kv2(bk, tb):
                            ev_kv(bk, 8 + tb)
                        tm_proj(wkv, "r_wkv", 384, uTo, "uTo", range(8), lambda tb: 4 + tb % 2, ev_kv2)

                        if RC < 6:
                            break

                        def ev_g(bk, tb):
                            P.op("act", lambda e: e.activation(out=sg[:, tb, :], in_=banks[bk][:, 0:256], func=AF.Silu), reads=[B(bk)], writes=["r_sg%d" % tb])
                        tm_proj(wgr, "r_wgr", 256, uTo, "uTo", range(8), lambda tb: 6 + tb % 2, ev_g)
                        if os.environ.get("RET_CUT") == "proj":
                            break
                        for t in range(8):
                            P.op("pe", lambda e, t=t: e.matmul(banks[0][:, 0:256], lhsT=kz[:, t, :], rhs=vv[:, t, :], start=(t == 0), stop=(t == 7)),
                                 reads=["r_kz%d" % t, "r_v%d" % t], writes=[B(0)])
                        P.op("dve", lambda e: e.tensor_copy(out=state[:], in_=banks[0][:, 0:256]), reads=[B(0)], writes=["r_state"])
                        P.op("act", lambda e: e.activation(out=stbf[:], in_=banks[0][:, 0:256], func=AF.Copy), reads=[B(0)], writes=["r_stbf"])
                        P.op("dve", lambda e: e.memset(ysum[:], 0.0), writes=["r_ysum"])
                        P.op("dve", lambda e: e.memset(ysq[:], 0.0), writes=["r_ysq"])
                        if os.environ.get("RET_CUT") == "state":
                            break
                        for c in range(8):
                            csl = slice(c * 128, (c + 1) * 128)
                            sb_ = 1 + c % 2
                            yb = 4 + c // 2
                            ysl = slice((c % 2) * 256, (c % 2) * 256 + 256)
                            s2 = c % 2
                            P.op("pe", lambda e, csl=csl, sb_=sb_: e.matmul(banks[sb_][:, 0:128], lhsT=kT[:, csl], rhs=qT[:, csl], start=True, stop=True),
                                 reads=["r_kT%d" % (c // 4), "r_qT%d" % (c // 4)], writes=[B(sb_)])
                            P.op("dve", lambda e, sb_=sb_, s2=s2, h=h: e.tensor_tensor(out=sTb[s2][:], in0=banks[sb_][:, 0:128], in1=decT[:, 0, :], op=ALU.mult),
                                 reads=[B(sb_), "decT"], writes=["r_sTb%d" % s2])
                            P.op("pe", lambda e, yb=yb, ysl=ysl, s2=s2, c=c: e.matmul(banks[yb][:, ysl], lhsT=sTb[s2][:], rhs=vv[:, 8 + c, :], start=True, stop=False),
                                 reads=["r_sTb%d" % s2, "r_v%d" % (8 + c)], writes=[B(yb)])
                            P.op("pe", lambda e, yb=yb, ysl=ysl, csl=csl: e.matmul(banks[yb][:, ysl], lhsT=qxT[:, csl], rhs=stbf[:], start=False, stop=True),
                                 reads=["r_qxT%d" % (c // 4), "r_stbf"], writes=[B(yb)])
                            if c < 7:
                                P.op("pe", lambda e, c=c: e.matmul(banks[3][:, 0:256], lhsT=kz[:, 8 + c, :], rhs=vv[:, 8 + c, :], start=True, stop=True),
                                     reads=["r_kz%d" % (8 + c), "r_v%d" % (8 + c)], writes=[B(3)])
                                P.op("dve", lambda e, h=h: e.scalar_tensor_tensor(out=state[:], in0=state[:], scalar=GC[h], in1=banks[3][:, 0:256], op0=ALU.mult, op1=ALU.add),
                                     reads=[B(3), "r_state"], writes=["r_state"])
                                P.op("act", lambda e: e.activation(out=stbf[:], in_=state[:], func=AF.Copy), reads=["r_state"], writes=["r_stbf"])
                            P.op("act", lambda e, yb=yb, ysl=ysl, c=c: e.activation(out=junk2[:], in_=banks[yb][:, ysl], func=AF.Identity, accum_out=ysum[:, c:c + 1]),
                                 reads=[B(yb)], writes=["r_junk", "r_ysum"])
                            P.op("act", lambda e, yb=yb, ysl=ysl, c=c: e.activation(out=junk2[:], in_=banks[yb][:, ysl], func=AF.Square, accum_out=ysq[:, c:c + 1]),
                                 reads=[B(yb)], writes=["r_junk", "r_ysq"])
                        if os.environ.get("RET_CUT") == "chunks":
                            break
                        P.op("dve", lambda e: e.tensor_scalar_mul(out=mean[:], in0=ysum[:], scalar1=1.0 / RDV), reads=["r_ysum"], writes=["r_mean"])
                        P.op("dve", lambda e: e.tensor_tensor(out=rstd[:], in0=mean[:], in1=mean[:], op=ALU.mult), reads=["r_mean"], writes=["r_rstd"])
                        P.op("dve", lambda e: e.scalar_tensor_tensor(out=rstd[:], in0=ysq[:], scalar=1.0 / RDV, in1=rstd[:], op0=ALU.mult, op1=ALU.subtract), reads=["r_ysq", "r_rstd"], writes=["r_rstd"])
                        P.op("dve", lambda e: e.tensor_scalar_add(out=rstd[:], in0=rstd[:], scalar1=EPS), reads=["r_rstd"], writes=["r_rstd"])
                        P.op("act", lambda e: e.activation(out=rstd[:], in_=rstd[:], func=AF.Sqrt), reads=["r_rstd"], writes=["r_rstd"])
                        P.op("dve", lambda e: e.reciprocal(out=rstd[:], in_=rstd[:]), reads=["r_rstd"], writes=["r_rstd"])
                        for c in range(8):
                            yb = 4 + c // 2
                            ysl = slice((c % 2) * 256, (c % 2) * 256 + 256)
                            s2 = c % 2
                            P.op("dve", lambda e, yb=yb, ysl=ysl, c=c, s2=s2: e.tensor_scalar(out=t1[s2][:], in0=banks[yb][:, ysl], scalar1=mean[:, c:c + 1], scalar2=rstd[:, c:c + 1], op0=ALU.subtract, op1=ALU.mult),
                                 reads=[B(yb), "r_mean", "r_rstd"], writes=["r_t1%d" % s2])
                            P.op("dve", lambda e, s2=s2, h=h: e.tensor_tensor(out=t1[s2][:], in0=t1[s2][:], in1=gnr_t[:, 0, :], op=ALU.mult), reads=["r_t1%d" % s2, "gnr_t"], writes=["r_t1%d" % s2])
                            P.op("dve", lambda e, s2=s2, c=c: e.tensor_tensor(out=ob[s2][:], in0=t1[s2][:], in1=sg[:, c, :], op=ALU.mult), reads=["r_t1%d" % s2, "r_sg%d" % c], writes=["r_ob%d" % s2])
                            for e2 in range(2):
                                P.op("pe", lambda e, s2=s2, e2=e2: e.matmul(banks[0][:, e2 * 128:(e2 + 1) * 128], lhsT=ob[s2][:, e2 * 128:(e2 + 1) * 128], rhs=identb[:], start=True, stop=True),
                                     reads=["r_ob%d" % s2, "identb"], writes=[B(0)])
                            P.op("act", lambda e, h=h, c=c: e.activation(out=orT[:, 2 * h:2 * h + 2, c * 128:(c + 1) * 128], in_=banks[0][:, 0:256].rearrange("p (k t) -> p k t", k=2), func=AF.Copy),
                                 reads=[B(0)], writes=["orT_%d" % c])
                    P.op("sp", lambda e: e.dma_start(out=orT_d, in_=orT[:].rearrange("p k t -> p (k t)")), reads=["orT_%d" % c for c in range(8)], writes=["orT_d"], dma=True, key="st_orT")
                    dump("orT", orT[:].rearrange("p k t -> p (k t)"), ["orT_%d" % c for c in range(8)])
                    P.barrier()
                if stop_after == "ret":
                    P.emit(nc, final_keys=final_keys)
                    return nc

                with ExitStack() as sn:
                    gq = SB(sn, "gq", [128, 4], F32)
                    gqr = SB(sn, "gqr", [128, 4, 128], F32)
                    krows = SB(sn, "krows", [64, 16, 128], BF16)
                    kcrows = SB(sn, "kcrows", [10, 128], BF16)
                    cmask = [SB(sn, "cmask%d" % k, [128, 512], BF16) for k in range(2)]
                    wt2 = [SB(sn, "wt2_0", [128, 16, 256], BF16)]
                    tri = SB(sn, "tri", [128, 2, 512], BF16)
                    vm_t = SB(sn, "vm_t", [128, 8, 32], F32)
                    fb_t = SB(sn, "fb_t", [128, 8, 32], F32)
                    ovb = SB(sn, "ovb", [128, 32], BF16)
                    cposf = SB(sn, "cposf", [128, 64], F32)
                    cposb = SB(sn, "cposb", [128, 64], BF16)
                    cb1_t = SB(sn, "cb1_t", [128, 2], F32)
                    P.op("sp", lambda e: e.dma_start(out=gq[:], in_=gqkT), writes=["gq"], dma=True, key="tbl")
                    P.op("sp", lambda e: e.dma_start(out=gqr[:].rearrange("p a d -> p (a d)"), in_=gqkR), writes=["gqr"], dma=True, key="tbl")
                    P.op("sp", lambda e: e.dma_start(out=krows[:].rearrange("p a d -> p (a d)"), in_=tb_ap["krows"]), writes=["krows"], dma=True, key="tbl")
                    P.op("sp", lambda e: e.dma_start(out=kcrows[:], in_=tb_ap["kcrows"]), writes=["kcrows"], dma=True, key="tbl")
                    P.op("sp", lambda e: e.dma_start(out=tri[:].rearrange("p a d -> p (a d)"), in_=tb_ap["tri"]), writes=["tri"], dma=True, key="tbl")
                    P.op("sp", lambda e: e.dma_start(out=vm_t[:].rearrange("p a d -> p (a d)"), in_=tb_ap["vm"]), writes=["vm_t"], dma=True, key="tbl")
                    P.op("sp", lambda e: e.dma_start(out=fb_t[:].rearrange("p a d -> p (a d)"), in_=tb_ap["fb"]), writes=["fb_t"], dma=True, key="tbl")
                    P.op("sp", lambda e: e.dma_start(out=ovb[:], in_=tb_ap["ov"]), writes=["ovb"], dma=True, key="tbl")
                    P.op("sp", lambda e: e.dma_start(out=cposf[:], in_=cpos), writes=["cposf"], dma=True, key="tbl")
                    P.op("sp", lambda e: e.dma_start(out=cb1_t[:], in_=cb1), writes=["cb1_t"], dma=True, key="tbl")
                    P.op("dve", lambda e: e.tensor_copy(out=cposb[:], in_=cposf[:]), reads=["cposf"], writes=["cposb"])
                    P.op("act", lambda e: e.mul(out=gq[:, 0:1], in_=gq[:, 0:1], mul=float(DH ** -0.5)), reads=["gq"], writes=["gq"])
                    qTg = SB(sn, "qTg", [128, 8, 512], BF16)
                    rawT = [SB(sn, "rawT%d" % k, [128, 2048], BF16) for k in range(2)]
                    kslcT = SB(sn, "kslcT", [128, 2048], BF16)
                    kwinT = SB(sn, "kwinT", [128, 2048], BF16)
                    vslc = SB(sn, "vslc", [128, 16, 129], BF16)
                    vwin = SB(sn, "vwin", [128, 16, 129], BF16)
                    RA = SB(sn, "RA", [64, 8, 512], BF16)
                    w1t_ = SB(sn, "w1t0", [128, 32, 128], BF16)
                    w1t = [w1t_, w1t_]
                    w2t = [SB(sn, "w2t%d" % k, [128, 128], BF16) for k in range(2)]
                    hTs = SB(sn, "hTs", [128, 128], BF16)
                    zf = SB(sn, "zf", [128, 128], F32)
                    zt_ = SB(sn, "zt_", [128, 128], F32)
                    btile = SB(sn, "btile", [128, 1], F32)
                    kcn = SB(sn, "kcn", [128, 128], F32)
                    kcmpT = SB(sn, "kcmpT", [128, 128], BF16)
                    vcmp = SB(sn, "vcmp", [128, 161], BF16)
                    sqt = SB(sn, "sqt", [128, 512], F32)
                    rtt = SB(sn, "rtt", [128, 512], F32)
                    Pb = [SB(sn, "Pb%d" % k, [128, 512], BF16) for k in range(4)]
                    rs4 = SB(sn, "rs4", [128, 4], F32)
                    coef = SB(sn, "coef", [128, 4], F32)
                    imp = SB(sn, "imp", [128, 32], F32)
                    work = SB(sn, "work", [128, 32], F32)
                    top8 = SB(sn, "top8", [128, 16], F32)
                    selb = SB(sn, "selb", [128, 64], F32)
                    otmp = SB(sn, "otmp", [128, 4, 128], F32)
                    obf = SB(sn, "obf", [128, 4, 128], BF16)
                    ss1 = SB(sn, "ss1", [128, 1], F32)
                    P.op("dve", lambda e: e.memset(vslc[:, :, 128:129], 1.0), writes=["vslc_ones"])
                    P.op("dve", lambda e: e.memset(vwin[:, :, 128:129], 1.0), writes=["vwin_ones"])
                    P.op("dve", lambda e: e.memset(selb[:], 0.0), writes=["selb"])

                    def qknorm(bk, gcol, dst_ap, dst_names):
                        P.op("act", lambda e: e.activation(out=sqt[:], in_=banks[bk][:], func=AF.Square), reads=[B(bk)], writes=["sqt"])
                        P.op("pe", lambda e: e.matmul(banks[3][:], lhsT=ones[:], rhs=sqt[:], start=True, stop=True), reads=["ones", "sqt"], writes=[B(3)])
                        P.op("dve", lambda e: e.tensor_scalar(out=rtt[:], in0=banks[3][:], scalar1=1.0 / DH, scalar2=EPS, op0=ALU.mult, op1=ALU.add), reads=[B(3)], writes=["rtt"])
                        P.op("act", lambda e: e.activation(out=rtt[:], in_=rtt[:], func=AF.Sqrt), reads=["rtt"], writes=["rtt"])
                        P.op("dve", lambda e: e.reciprocal(out=rtt[:], in_=rtt[:]), reads=["rtt"], writes=["rtt"])
                        P.op("dve", lambda e: e.scalar_tensor_tensor(out=dst_ap, in0=banks[bk][:], scalar=gq[:, gcol:gcol + 1], in1=rtt[:], op0=ALU.mult, op1=ALU.mult),
                             reads=[B(bk), "gq", "rtt"], writes=dst_names)

                    for g in range(NG):
                        P.op("sp", lambda e, g=g: e.dma_start(out=RA[:].rearrange("p a d -> p (a d)"), in_=tb_ap["rows"][g]), writes=["RA", "RAsel"], dma=True, key="RA")
                        for hh in range(4):
                            wsl = hh % 3
                            load_w(wt1[wsl], "wt1_%d" % wsl, O_Q + (4 * g + hh) * 128, 128)

                            def ev_qn(bk, ci, hh=hh):
                                qknorm(bk, 0, qTg[:, 4 * ci:4 * ci + 4, hh * 128:(hh + 1) * 128], ["qTg"])
                            def ev_qn2(bk, ci, hh=hh):
                                P.op("act", lambda e: e.activation(out=sqt[:], in_=banks[bk][:], func=AF.Square), reads=[B(bk)], writes=["sqt"])
                                P.op("pe", lambda e: e.matmul(banks[3][:], lhsT=ones[:], rhs=sqt[:], start=True, stop=True), reads=["ones", "sqt"], writes=[B(3)])
                                P.op("dve", lambda e: e.tensor_scalar(out=rtt[:], in0=banks[3][:], scalar1=1.0 / DH, scalar2=EPS, op0=ALU.mult, op1=ALU.add), reads=[B(3)], writes=["rtt"])
                                P.op("act", lambda e: e.activation(out=rtt[:], in_=rtt[:], func=AF.Sqrt), reads=["rtt"], writes=["rtt"])
                                P.op("dve", lambda e: e.reciprocal(out=rtt[:], in_=rtt[:]), reads=["rtt"], writes=["rtt"])
                                P.op("dve", lambda e: e.scalar_tensor_tensor(out=qTg[:, 4 * ci:4 * ci + 4, hh * 128:(hh + 1) * 128], in0=banks[bk][:].rearrange("p (a q) -> p a q", a=4), scalar=gq[:, 0:1],
                                                                             in1=rtt[:].rearrange("p (a q) -> p a q", a=4), op0=ALU.mult, op1=ALU.mult),
                                     reads=[B(bk), "gq", "rtt"], writes=["qTg"])
                            fm_proj(wt1[wsl], "wt1_%d" % wsl, uTo, "uTo", range(2), lambda ci: ci, ev_qn2)
                        for slot, kind in ((0, "raw0"), (1, "raw1"), (2, "kslc"), (4, "kwin")):
                            wsl = slot % 3
                            load_w(wt1[wsl], "wt1_%d" % wsl, O_KV + slot * 512 + g * 128, 128)
                            for src, uname, base in ((uTc, "uTc", 0), (uTo, "uTo", 2)):
                                def ev_f(bk, ci, kind=kind, base=base):
                                    fsl = slice((base + ci) * 512, (base + ci + 1) * 512)
                                    if kind == "raw0":
                                        P.op("act", lambda e: e.activation(out=rawT[0][:, fsl], in_=banks[bk][:], func=AF.Copy), reads=[B(bk)], writes=["rawT0"])
                                    elif kind == "raw1":
                                        P.op("act", lambda e: e.activation(out=rawT[1][:, fsl], in_=banks[bk][:], func=AF.Copy), reads=[B(bk)], writes=["rawT1"])
                                    elif kind == "kslc":
                                        qknorm(bk, 2, kslcT[:, fsl], ["kslcT"])
                                    else:
                                        qknorm(bk, 3, kwinT[:, fsl], ["kwinT"])
                                fm_proj(wt1[wsl], "wt1_%d" % wsl, src, uname, range(2), lambda ci: ci, ev_f)
                        ws2 = 0
                        load_w(wt2[ws2], "wt2_%d" % ws2, O_KV + 3 * 512 + g * 128, 128, 0)
                        load_w(wt2[ws2], "wt2_%d" % ws2, O_KV + 5 * 512 + g * 128, 128, 128)
                        for src, uname, base in ((uTc, "uTc", 0), (uTo, "uTo", 8)):
                            def ev_v(bk, tb, base=base):
                                t = base + tb
                                P.op("act", lambda e: e.activation(out=vslc[:, t, 0:128], in_=banks[bk][:, 0:128], func=AF.Copy), reads=[B(bk)], writes=["vslc%d" % t])
                                P.op("dve", lambda e: e.tensor_copy(out=vwin[:, t, 0:128], in_=banks[bk][:, 128:256]), reads=[B(bk)], writes=["vwin%d" % t])
                            tm_proj(wt2[ws2], "wt2_%d" % ws2, 256, src, uname, range(8), lambda tb: tb % 2, ev_v)
                        for i in range(2):
                            P.op("pool", lambda e, i=i: e.dma_start(out=w1t[i][:], in_=cw1[i].rearrange("(r d) j -> d r j", d=128)), writes=["w1t0"], dma=True)
                            P.op("pool", lambda e, i=i: e.dma_start(out=w2t[i][:], in_=cw2[i]), writes=["w2t%d" % i], dma=True)
                            for r in range(32):
                                P.op("pe", lambda e, i=i, r=r: e.matmul(banks[0][:, 0:127], lhsT=w1t[i][:, r, :], rhs=rawT[i][:, r:r + 16 * 126 + 1:16], start=(r == 0), stop=(r == 31)),
                                     reads=["w1t0", "rawT%d" % i], writes=[B(0)])
                            for r in range(32):
                                P.op("pe", lambda e, i=i, r=r: e.matmul(banks[1][:, 0:1], lhsT=w1t[i][:, r, :], rhs=cposb[:, i * 32 + r:i * 32 + r + 1], start=(r == 0), stop=(r == 31)),
                                     reads=["w1t0", "cposb"], writes=[B(1)])
                            P.op("dve", lambda e, i=i: e.tensor_tensor(out=btile[:], in0=banks[1][:, 0:1], in1=cb1_t[:, i:i + 1], op=ALU.add), reads=[B(1), "cb1_t"], writes=["btile"])
                            P.op("dve", lambda e: e.memset(zf[:], 0.0), writes=["zf"])
                            P.op("act", lambda e: e.activation(out=zf[:, 0:127], in_=banks[0][:, 0:127], func=AF.Identity, bias=btile[:, 0:1]), reads=[B(0), "btile", "zf"], writes=["zf"])
                            P.op("dve", lambda e: e.tensor_tensor(out=zt_[:], in0=zf[:], in1=zf[:], op=ALU.mult), reads=["zf"], writes=["zt_"])
                            P.op("dve", lambda e: e.tensor_tensor(out=zt_[:], in0=zt_[:], in1=zf[:], op=ALU.mult), reads=["zf", "zt_"], writes=["zt_"])
                            P.op("dve", lambda e: e.scalar_tensor_tensor(out=zt_[:], in0=zt_[:], scalar=0.044715, in1=zf[:], op0=ALU.mult, op1=ALU.add), reads=["zf", "zt_"], writes=["zt_"])
                            P.op("act", lambda e: e.activation(out=zt_[:], in_=zt_[:], func=AF.Tanh, scale=float(np.sqrt(2.0 / np.pi))), reads=["zt_"], writes=["zt_"])
                            P.op("dve", lambda e: e.scalar_tensor_tensor(out=zt_[:], in0=zt_[:], scalar=1.0, in1=zf[:], op0=ALU.add, op1=ALU.mult), reads=["zf", "zt_"], writes=["zt_"])
                            P.op("act", lambda e: e.mul(out=hTs[:], in_=zt_[:], mul=0.5), reads=["zt_"], writes=["hTs"])
                            P.op("pe", lambda e, i=i: e.matmul(banks[2][:, 0:128], lhsT=hTs[:], rhs=w2t[i][:], start=True, stop=True), reads=["hTs", "w2t%d" % i], writes=[B(2)])
                            if i == 0:
                                P.op("dve", lambda e: e.memset(ss1[:], 0.0), writes=["ss1"])
                                P.op("act", lambda e: e.activation(out=kcn[:], in_=banks[2][:, 0:128], func=AF.Square, accum_out=ss1[:]), reads=[B(2), "ss1"], writes=["kcn", "ss1"])
                                P.op("dve", lambda e: e.tensor_scalar(out=ss1[:], in0=ss1[:], scalar1=1.0 / DH, scalar2=EPS, op0=ALU.mult, op1=ALU.add), reads=["ss1"], writes=["ss1"])
                                P.op("act", lambda e: e.activation(out=ss1[:], in_=ss1[:], func=AF.Sqrt), reads=["ss1"], writes=["ss1"])
                                P.op("dve", lambda e: e.reciprocal(out=ss1[:], in_=ss1[:]), reads=["ss1"], writes=["ss1"])
                                P.op("dve", lambda e: e.scalar_tensor_tensor(out=kcn[:], in0=banks[2][:, 0:128], scalar=ss1[:, 0:1], in1=gqr[:, 1, :], op0=ALU.mult, op1=ALU.mult),
                                     reads=[B(2), "ss1", "gqr", "kcn"], writes=["kcn"])
                                P.op("pe", lambda e: e.transpose(banks[3][:, 0:128], kcn[:], ident[:]), reads=["kcn", "ident"], writes=[B(3)])
                                P.op("act", lambda e: e.activation(out=kcmpT[:], in_=banks[3][:, 0:128], func=AF.Copy), reads=[B(3)], writes=["kcmpT"])
                                P.op("dve", lambda e: e.memset(kcmpT[:, 127:128], 0.0), reads=["kcmpT"], writes=["kcmpT"])
                            else:
                                P.op("dve", lambda e: e.memset(vcmp[:], 0.0), writes=["vcmp"])
                                P.op("act", lambda e: e.activation(out=vcmp[0:127, 0:128], in_=banks[2][0:127, 0:128], func=AF.Copy), reads=[B(2), "vcmp"], writes=["vcmp"])
                                P.op("dve", lambda e: e.memset(vcmp[0:127, 128:129], 1.0), reads=["vcmp"], writes=["vcmp"])
                                P.op("dve", lambda e: e.tensor_copy(out=vcmp[0:127, 129:161], in_=ovb[0:127, :]), reads=["vcmp", "ovb"], writes=["vcmp"])
                        scount = [0]

                        def score_tile(kT_ap, k_names, qb, extra, pb_i):
                            bk = scount[0] % 3
                            scount[0] += 1
                            n = len(extra)
                            P.op("pe", lambda e: e.matmul(banks[bk][:], lhsT=kT_ap, rhs=qTg[:, qb, :], start=True, stop=(n == 0)), reads=k_names + ["qTg"], writes=[B(bk)])
                            for ii, (l_ap, r_ap, names) in enumerate(extra):
                                P.op("pe", lambda e, l_ap=l_ap, r_ap=r_ap, ii=ii: e.matmul(banks[bk][:], lhsT=l_ap, rhs=r_ap, start=False, stop=(ii == n - 1)), reads=names, writes=[B(bk)])
                            P.op("act", lambda e: e.activation(out=Pb[pb_i][:], in_=banks[bk][:], func=AF.Exp), reads=[B(bk)], writes=["Pb%d" % pb_i])

                        pcount = [0]
                        accset = [0]
                        pending_finish = [None]
                        for qb in range(8):
                            gcol = lambda br, hh: br * 16 + 4 * g + hh
                            a0 = 4 + 2 * (accset[0] % 2)
                            accset[0] += 1
                            pi = pcount[0] % 4
                            pcount[0] += 1
                            cmi = qb % 2
                            P.op("sp", lambda e, qb=qb, cmi=cmi: e.dma_start(out=cmask[cmi][:], in_=tb_ap["cmask"][:, qb * 512:(qb + 1) * 512]), writes=["cmask%d" % cmi], dma=True)
                            score_tile(kcmpT[:], ["kcmpT"], qb,
                                       [(kcrows[0:10, :], RA[0:10, qb, :], ["kcrows", "RA"]), (identb[:], cmask[cmi][:], ["identb", "cmask%d" % cmi])], pi)
                            for hh in range(4):
                                bk = a0 + hh // 2
                                o0 = (hh % 2) * 161
                                P.op("pe", lambda e, bk=bk, o0=o0, hh=hh, pi=pi: e.matmul(banks[bk][:, o0:o0 + 161], lhsT=Pb[pi][:, hh * 128:(hh + 1) * 128], rhs=vcmp[:], start=True, stop=True),
                                     reads=["Pb%d" % pi, "vcmp"], writes=[B(bk)])
                            if pending_finish[0] is not None:
                                pending_finish[0]()
                                pending_finish[0] = None
                            for hh in range(4):
                                bk = a0 + hh // 2
                                o0 = (hh % 2) * 161
                                P.op("dve", lambda e, bk=bk, o0=o0, hh=hh: e.tensor_scalar_max(out=rs4[:, hh:hh + 1], in0=banks[bk][:, o0 + 128:o0 + 129], scalar1=1e-30), reads=[B(bk)], writes=["rs4"])
                            P.op("dve", lambda e: e.reciprocal(out=rs4[:], in_=rs4[:]), reads=["rs4"], writes=["rs4"])
                            for hh in range(4):
                                bk = a0 + hh // 2
                                o0 = (hh % 2) * 161
                                if hh == 0:
                                    P.op("dve", lambda e, bk=bk, o0=o0: e.tensor_scalar(out=imp[:], in0=banks[bk][:, o0 + 129:o0 + 161], scalar1=rs4[:, 0:1], scalar2=None, op0=ALU.mult), reads=[B(bk), "rs4"], writes=["imp"])
                                else:
                                    P.op("dve", lambda e, bk=bk, o0=o0, hh=hh: e.scalar_tensor_tensor(out=imp[:], in0=banks[bk][:, o0 + 129:o0 + 161], scalar=rs4[:, hh:hh + 1], in1=imp[:], op0=ALU.mult, op1=ALU.add),
                                         reads=[B(bk), "rs4", "imp"], writes=["imp"])
                            P.op("dve", lambda e, qb=qb: e.tensor_tensor(out=imp[:], in0=imp[:], in1=vm_t[:, qb, :], op=ALU.mult), reads=["imp", "vm_t"], writes=["imp"])
                            P.op("dve", lambda e, qb=qb: e.tensor_tensor(out=imp[:], in0=imp[:], in1=fb_t[:, qb, :], op=ALU.add), reads=["imp", "fb_t"], writes=["imp"])
                            P.op("dve", lambda e: e.max(out=top8[:, 0:8], in_=imp[:]), reads=["imp"], writes=["top8"])
                            P.op("dve", lambda e: e.match_replace(out=work[:], in_to_replace=top8[:, 0:8], in_values=imp[:], imm_value=-1e30), reads=["imp", "top8"], writes=["work"])
                            P.op("dve", lambda e: e.max(out=top8[:, 8:16], in_=work[:]), reads=["work", "top8"], writes=["top8"])
                            P.op("dve", lambda e: e.tensor_scalar(out=selb[:, 32:64], in0=imp[:], scalar1=top8[:, 15:16], scalar2=None, op0=ALU.is_ge), reads=["imp", "top8", "selb"], writes=["selb"])
                            P.op("dve", lambda e: e.tensor_scalar(out=selb[:, 32:64], in0=selb[:, 32:64], scalar1=-1.0, scalar2=-NEG, op0=ALU.add, op1=ALU.mult), reads=["selb"], writes=["selb"])
                            def sel_finish(qb=qb):
                                P.op("pe", lambda e: e.transpose(banks[3][0:64, 0:128], selb[:], ident[:]), reads=["selb", "ident"], writes=[B(3)])
                                P.op("dve", lambda e, qb=qb: e.tensor_copy(out=RA[32:64, qb, :].rearrange("p (a q) -> p a q", a=4), in_=banks[3][32:64, 0:128].unsqueeze(1).to_broadcast([32, 4, 128])),
                                     reads=[B(3), "RAsel"], writes=["RAsel"])
                            for hh in range(4):
                                P.op("dve", lambda e, hh=hh, qb=qb, gc=gcol(0, hh): e.tensor_tensor(out=coef[:, hh:hh + 1], in0=rs4[:, hh:hh + 1], in1=gates[:, qb, gc:gc + 1], op=ALU.mult), reads=["rs4", "gates", "coef"], writes=["coef"])
                            for hh in range(4):
                                bk = a0 + hh // 2
                                o0 = (hh % 2) * 161
                                P.op("dve", lambda e, bk=bk, o0=o0, hh=hh: e.tensor_scalar(out=otmp[:, hh, :], in0=banks[bk][:, o0:o0 + 128], scalar1=coef[:, hh:hh + 1], scalar2=None, op0=ALU.mult),
                                     reads=[B(bk), "coef", "otmp"], writes=["otmp"])
                            tiles = []
                            for br in (2, 1):
                                a0 = 4 + 2 * (accset[0] % 2)
                                accset[0] += 1
                                kbs = list(range(0, 9 + qb)) if br == 1 else list(range(4 + qb, 9 + qb))
                                for kb in kbs:
                                    tiles.append((br, kb, a0, kb == kbs[0], kb == kbs[-1]))

                            def emit_score(t):
                                br, kb, a0, first, last = t
                                Dd = 8 + qb - kb
                                pi = pcount[0] % 4
                                pcount[0] += 1
                                ksl = slice(kb * 128, (kb + 1) * 128)
                                if br == 1:
                                    extra = [(krows[0:64, kb, :], RA[0:64, qb, :], ["krows", "RA", "RAsel"])]
                                    if Dd == 0:
                                        extra.append((identb[:], tri[:, 0, :], ["identb", "tri"]))
                                    score_tile(kslcT[:, ksl], ["kslcT"], qb, extra, pi)
                                else:
                                    extra = [(krows[0:10, kb, :], RA[0:10, qb, :], ["krows", "RA"])]
                                    if Dd == 0:
                                        extra.append((identb[:], tri[:, 0, :], ["identb", "tri"]))
                                    if Dd == 4:
                                        extra.append((identb[:], tri[:, 1, :], ["identb", "tri"]))
                                    score_tile(kwinT[:, ksl], ["kwinT"], qb, extra, pi)
                                return pi

                            def emit_pv(t, pi):
                                br, kb, a0, first, last = t
                                vt, vn = (vslc, "vslc%d" % kb) if br == 1 else (vwin, "vwin%d" % kb)
                                for hh in range(4):
                                    bk = a0 + hh // 2
                                    o0 = (hh % 2) * 161
                                    P.op("pe", lambda e, bk=bk, o0=o0, hh=hh, pi=pi, vt=vt, kb=kb, st_=(first and hh % 2 == 0), sp_=(last and hh % 2 == 1): e.matmul(banks[bk][:, o0:o0 + 129], lhsT=Pb[pi][:, hh * 128:(hh + 1) * 128], rhs=vt[:, kb, :], start=st_, stop=sp_),
                                         reads=["Pb%d" % pi, vn, ("vslc_ones" if br == 1 else "vwin_ones")], writes=[B(bk)])
                                if not last:
                                    return
                                for bq in range(2):
                                    bk = a0 + bq
                                    P.op("dve", lambda e, bk=bk, bq=bq: e.tensor_scalar_max(out=rs4[:, 2 * bq:2 * bq + 2], in0=banks[bk][:, 128:128 + 162:161], scalar1=1e-30), reads=[B(bk), "rs4"], writes=["rs4"])
                                P.op("dve", lambda e: e.reciprocal(out=rs4[:], in_=rs4[:]), reads=["rs4"], writes=["rs4"])
                                gc0 = br * 16 + 4 * g
                                P.op("dve", lambda e, gc0=gc0, qb=qb: e.tensor_tensor(out=coef[:], in0=rs4[:], in1=gates[:, qb, gc0:gc0 + 4], op=ALU.mult), reads=["rs4", "gates", "coef"], writes=["coef"])
                                for hh in range(4):
                                    bk = a0 + hh // 2
                                    o0 = (hh % 2) * 161
                                    dst = otmp[:, hh, :] if br == 2 else obf[:, hh, :]
                                    dn = "otmp" if br == 2 else "obf"
                                    P.op("dve", lambda e, bk=bk, o0=o0, hh=hh, dst=dst: e.scalar_tensor_tensor(out=dst, in0=banks[bk][:, o0:o0 + 128], scalar=coef[:, hh:hh + 1], in1=otmp[:, hh, :], op0=ALU.mult, op1=ALU.add),
                                         reads=[B(bk), "coef", "otmp", dn], writes=[dn])

                            DEPTH = 2
                            pis = []
                            nsc = [0]

                            def next_score():
                                tn = nsc[0]
                                if tn < len(tiles):
                                    if tn > 0 and tiles[tn][0] == 1 and tiles[tn - 1][0] == 2:
                                        sel_finish()
                                    pis.append(emit_score(tiles[tn]))
                                    nsc[0] += 1
                            for _ in range(DEPTH):
                                next_score()
                            for ti in range(len(tiles)):
                                next_score()
                                emit_pv(tiles[ti], pis[ti])
                            def finish(qb=qb, g=g):
                                for hh in range(4):
                                    P.op("pe", lambda e, hh=hh: e.matmul(banks[3][:, hh * 128:(hh + 1) * 128], lhsT=obf[:, hh, :], rhs=identb[:], start=(hh == 0), stop=(hh == 3)), reads=["obf", "identb"], writes=[B(3)])
                                P.op("act", lambda e: e.activation(out=onT[:, 4 * g:4 * g + 4, qb * 128:(qb + 1) * 128], in_=banks[3][:].rearrange("p (k t) -> p k t", k=4), func=AF.Copy),
                                     reads=[B(3)], writes=["onT_%d" % qb])
                            pending_finish[0] = finish
                        pending_finish[0]()
                        pending_finish[0] = None
                dump("onT", onT[:].rearrange("p k t -> p (k t)"), ["onT_%d" % c for c in range(8)])
            P.barrier()
            if stop_after == "nsa":
                P.emit(nc, final_keys=final_keys)
                return nc

            with ExitStack() as sg_:
                merged = SB(sg_, "merged", [128, 8, D], F32)
                orT2 = SB(sg_, "orT2", [128, 16, 1024], BF16)
                for tb in range(8):
                    P.op("sp", lambda e, tb=tb: e.dma_start(out=orT2[:, :, tb * 128:(tb + 1) * 128], in_=orT_d.rearrange("p (k t) -> p k t", k=16)[:, :, tb * 128:(tb + 1) * 128]),
                         reads=["orT_d"], writes=["orT_%d" % tb], dma=True, key="ld_orT")
                wpa = [SB(sg_, "wpa%d" % k, [128, 16, 256], BF16) for k in range(4)]
                sgt = [SB(sg_, "sgt%d" % k, [128, 256], F32) for k in range(2)]
                mt = [SB(sg_, "mt%d" % k, [128, 256], F32) for k in range(2)]
                for ph, (Wp, oT_, on_, gcol0) in enumerate(((wpn, onT, "onT", O_GA), (wpr, orT2, "orT", O_GB))):
                    Wpv = Wp.rearrange("(k p) n -> p k n", p=128)
                    for fb8 in range(8):
                        fsl = slice(fb8 * 256, (fb8 + 1) * 256)
                        wi = 2 * (fb8 % 2)
                        P.op("pool", lambda e, fsl=fsl, Wpv=Wpv, wi=wi: e.dma_start(out=wpa[wi][:], in_=Wpv[:, :, fsl]), writes=["wpa%d" % wi], dma=True)
                        P.op("pool", lambda e, fb8=fb8, gcol0=gcol0, wi=wi: e.dma_start(out=wpa[wi + 1][:], in_=Win[:, :, gcol0 + fb8 * 256:gcol0 + (fb8 + 1) * 256]), writes=["wpa%d" % (wi + 1)], dma=True)
                        for tb in range(8):
                            s2 = tb % 2
                            bp, bg_ = 2 * s2, 2 * s2 + 1
                            for k in range(16):
                                P.op("pe", lambda e, k=k, bp=bp, tb=tb, oT_=oT_, wi=wi: e.matmul(banks[bp][:, 0:256], lhsT=oT_[:, k, tb * 128:(tb + 1) * 128], rhs=wpa[wi][:, k, :], start=(k == 0), stop=(k == 15)),
                                     reads=["%s_%d" % (on_, tb), "wpa%d" % wi], writes=[B(bp)])
                            for k in range(16):
                                P.op("pe", lambda e, k=k, bg_=bg_, tb=tb, wi=wi: e.matmul(banks[bg_][:, 0:256], lhsT=uTo[:, k, tb * 128:(tb + 1) * 128], rhs=wpa[wi + 1][:, k, :], start=(k == 0), stop=(k == 15)),
                                     reads=["uTo_%d" % tb, "wpa%d" % (wi + 1)], writes=[B(bg_)])
                            P.op("act", lambda e, bg_=bg_, s2=s2: e.activation(out=sgt[s2][:], in_=banks[bg_][:, 0:256], func=AF.Sigmoid), reads=[B(bg_)], writes=["sgt%d" % s2])
                            if ph == 0:
                                P.op("dve", lambda e, bp=bp, s2=s2, tb=tb, fsl=fsl: e.tensor_tensor(out=merged[:, tb, fsl], in0=sgt[s2][:], in1=banks[bp][:, 0:256], op=ALU.mult),
                                     reads=["sgt%d" % s2, B(bp)], writes=["mg%d_%d" % (tb, fb8 // 2)])
                            else:
                                P.op("dve", lambda e, bp=bp, s2=s2: e.tensor_tensor(out=mt[s2][:], in0=sgt[s2][:], in1=banks[bp][:, 0:256], op=ALU.mult),
                                     reads=["sgt%d" % s2, B(bp)], writes=["mt%d" % s2])
                                P.op("dve", lambda e, s2=s2, tb=tb, fsl=fsl: e.tensor_tensor(out=merged[:, tb, fsl], in0=merged[:, tb, fsl], in1=mt[s2][:], op=ALU.add),
                                     reads=["mt%d" % s2, "mg%d_%d" % (tb, fb8 // 2)], writes=["mg%d_%d" % (tb, fb8 // 2)])
                tcount = 0
                for tb in range(8):
                    for kq in range(4):
                        bk = 4 + tcount % 4
                        tcount += 1
                        for kk in range(4):
                            P.op("pe", lambda e, bk=bk, kk=kk, kq=kq, tb=tb: e.transpose(banks[bk][:, kk * 128:(kk + 1) * 128], merged[:, tb, (4 * kq + kk) * 128:(4 * kq + kk + 1) * 128], ident[:]),
                                 reads=["mg%d_%d" % (tb, kq), "ident"], writes=[B(bk)])
                        if kq % 2 == 0:
                            P.op("act", lambda e, bk=bk, kq=kq, tb=tb: e.activation(out=onT[:, 4 * kq:4 * kq + 4, tb * 128:(tb + 1) * 128], in_=banks[bk][:].rearrange("p (k t) -> p k t", k=4), func=AF.Copy),
                                 reads=[B(bk)], writes=["onT_%d" % tb])
                        else:
                            P.op("dve", lambda e, bk=bk, kq=kq, tb=tb: e.tensor_copy(out=onT[:, 4 * kq:4 * kq + 4, tb * 128:(tb + 1) * 128], in_=banks[bk][:].rearrange("p (k t) -> p k t", k=4)),
                                 reads=[B(bk)], writes=["onT_%d" % tb])
            P.barrier()
            with ExitStack() as so:
                x_sb2 = SB(so, "x_sb2", [128, 8, D], F32)
                GT = SB(so, "GT2", [128, D], F32)
                wot = [SB(so, "wot%d" % k, [128, 16, 512], BF16) for k in range(2)]
                rt = [SB(so, "rt2_%d" % k, [128, 512], F32) for k in range(2)]
                for tb in range(8):
                    P.op("sp", lambda e, tb=tb: e.dma_start(out=x_sb2[:, tb, :], in_=x1_d[:, tb * D:(tb + 1) * D]), reads=["x1_d"], writes=["x%d_%d" % (tb, f) for f in range(4)], dma=True, key="ldx")
                P.op("sp", lambda e: e.dma_start(out=GT[:], in_=ada_d[:, 5 * D:6 * D]), reads=ada_names(5), writes=["GT2"], dma=True)
                Wov = wo.rearrange("(k p) n -> p k n", p=128)
                for fbk in range(4):
                    fsl = slice(fbk * 512, (fbk + 1) * 512)
                    ws = fbk % 2
                    P.op("pool", lambda e, fsl=fsl, ws=ws: e.dma_start(out=wot[ws][:], in_=Wov[:, :, fsl]), writes=["wot%d" % ws], dma=True)
                    for tb in range(8):
                        s2 = tb % 2
                        for k in range(16):
                            P.op("pe", lambda e, k=k, s2=s2, tb=tb, ws=ws: e.matmul(banks[s2][:], lhsT=onT[:, k, tb * 128:(tb + 1) * 128], rhs=wot[ws][:, k, :], start=(k == 0), stop=(k == 15)),
                                 reads=["onT_%d" % tb, "wot%d" % ws], writes=[B(s2)])
                        P.op("dve", lambda e, s2=s2, fsl=fsl: e.tensor_tensor(out=rt[s2][:], in0=banks[s2][:], in1=GT[:, fsl], op=ALU.mult), reads=[B(s2), "GT2"], writes=["rt2_%d" % s2])
                        P.op("dve", lambda e, s2=s2, tb=tb, fsl=fsl: e.tensor_tensor(out=x_sb2[:, tb, fsl], in0=x_sb2[:, tb, fsl], in1=rt[s2][:], op=ALU.add),
                             reads=["rt2_%d" % s2, "x%d_%d" % (tb, fbk)], writes=["x%d_%d" % (tb, fbk)])
                P.op("sp", lambda e: e.dma_start(out=x1_d, in_=x_sb2[:].rearrange("p t d -> p (t d)")), reads=xnames, writes=["x1_d"], dma=True, key="st_x1")
                dump("x2", x_sb2[:].rearrange("p t d -> p (t d)"), xnames)
            P.barrier()
        P.barrier()
        if stop_after == "mix":
            P.emit(nc, final_keys=final_keys)
            return nc

        with ExitStack() as sx:
            x_sb3 = SB(sx, "x_sb3", [128, 8, D], F32)
            hT3 = SB(sx, "hT3", [128, 16, 1024], BF16)
            for tb in range(8):
                P.op("sp", lambda e, tb=tb: e.dma_start(out=x_sb3[:, tb, :], in_=x1_d[:, tb * D:(tb + 1) * D]), reads=["x1_d"], writes=["x%d_%d" % (tb, f) for f in range(4)], dma=True, key="ldx")
            norm_mod_T(x_sb3, 2, hT3, "hT")
            ffn_core(x_sb3, hT3, 2, 1)
            ov = out.rearrange("(t p) d -> p t d", p=128)
            for tb in range(8):
                P.op("sp", lambda e, tb=tb: e.dma_start(out=ov[:, tb, :], in_=x_sb3[:, tb, :]), reads=["x%d_%d" % (tb, f) for f in range(4)], writes=["out%d" % tb], dma=True, key="st_out")
            final_keys.append("st_out")
        P.emit(nc, final_keys=final_keys)
    return nc


def prep_inputs(inp, n_pairs=4):
    f32 = np.float32
    x = np.asarray(inp["x"], f32)
    rep = lambda v: np.ascontiguousarray(np.broadcast_to(np.asarray(v, f32).reshape(1, -1), (128, np.asarray(v).size)))
    shared = {
        "w_ada": np.ascontiguousarray(np.asarray(inp["w_ada"], f32)[0]),
        "b_ada": rep(inp["b_ada"][0]),
        "gnorm": rep(inp["g_norm"][0]),
        "wg": np.ascontiguousarray(np.asarray(inp["w_ffn_gate"], f32)[0]),
        "wu": np.ascontiguousarray(np.asarray(inp["w_ffn_up"], f32)[0]),
        "wd": np.ascontiguousarray(np.asarray(inp["w_ffn_down"], f32)[0]),
        "w_in": np.ascontiguousarray(np.asarray(inp["w_in"], f32)[0]),
        "gqkT": np.ascontiguousarray(np.asarray(inp["g_qk"], f32)[0].T),
        "gqkR": rep(inp["g_qk"][0]),
        "cpos": np.ascontiguousarray(np.asarray(inp["cmp_pos"], f32)[0].transpose(2, 0, 1).reshape(128, 64)),
        "cw1": np.ascontiguousarray(np.asarray(inp["cmp_w1"], f32)[0]),
        "cb1": np.ascontiguousarray(np.asarray(inp["cmp_b1"], f32)[0].T),
        "cw2": np.ascontiguousarray(np.asarray(inp["cmp_w2"], f32)[0]),
        "gnr": rep(inp["ret_gn_gain"][0]),
        "wpn": np.ascontiguousarray(np.asarray(inp["w_proj_nsa"], f32)[0]),
        "wpr": np.ascontiguousarray(np.asarray(inp["w_proj_ret"], f32)[0]),
        "wo": np.ascontiguousarray(np.asarray(inp["w_out"], f32)[0]),
    }
    tabs = [make_tables(0), make_tables(1)]
    maps = []
    cvec = np.asarray(inp["c"], f32)
    for b in range(n_pairs):
        for j in range(2):
            m = dict(shared)
            if j == 0:
                fr = np.concatenate([np.zeros((1024, D), f32), x[b, :1024]], 0)
            else:
                fr = x[b]
            m["xf"] = np.ascontiguousarray(fr)
            m["c_l"] = np.ascontiguousarray(cvec[b].reshape(16, 128).T)
            for name, shape, dt in TABLE_SPECS:
                m["t_" + name] = np.ascontiguousarray(tabs[j][name]).reshape(shape)
            maps.append(m)
    return maps


def kernel(**inputs):
    nc = build()
    maps = prep_inputs(inputs)
    res = run_bass_kernel_spmd(nc, maps, core_ids=list(range(8)))
    outp = np.zeros((4, 2048, D), np.float32)
    for b in range(4):
        for j in range(2):
            outp[b, j * 1024:(j + 1) * 1024] = np.asarray(res.results[2 * b + j]["out"]).reshape(1024, D)
    return outp
```

```python
import contextlib
import os
from contextlib import ExitStack
import numpy as np
import ml_dtypes
import concourse.bass as bass
import concourse.mybir as mybir
from concourse.bass_utils import run_bass_kernel_spmd

F32 = mybir.dt.float32
BF16 = mybir.dt.bfloat16
ALU = mybir.AluOpType
AF = mybir.ActivationFunctionType

ENGS = ("pe", "act", "dve", "pool", "sp")
SAME_ENGINE_SYNC = True

D = 2048
DFF = 5632
NH = 16
NG = 4
DH = 128
RH = 8
RDK = 128
RDV = 256
EPS = 1e-6
NEG = -30000.0
IN_SPLITS = (2048, 3072, 48, 1024, 1024, 2048, 2048, 2048, 2048)
OFF = np.concatenate([[0], np.cumsum(IN_SPLITS)]).tolist()
O_Q, O_KV, O_GL, O_RQ, O_RK, O_RV, O_RG, O_GA, O_GB = OFF[:9]
NIN = OFF[9]


class Op:
    __slots__ = ("eng", "fn", "reads", "writes", "dma", "key", "deps", "sig", "sigidx", "dmacount", "idx", "waw", "dneed")


class Prog:
    def __init__(self):
        self.ops = []
        self.last_write = {}
        self.reads_since = {}
        self.dma_count = {}
        self.bar = set()
        self.last_eng = {}
        self.last_dma = {}

    def barrier(self):
        self.bar = set(self.last_eng.values()) | set(self.last_dma.values())

    def op(self, eng, fn, reads=(), writes=(), dma=False, key=None):
        o = Op()
        o.eng, o.fn, o.dma = eng, fn, dma
        o.reads, o.writes = tuple(reads), tuple(writes)
        o.idx = len(self.ops)
        o.sig = False
        o.sigidx = None
        o.waw = set()
        o.dneed = {}
        deps = set(self.bar)
        for r in o.reads:
            w = self.last_write.get(r)
            if w is not None:
                deps.add(w)
            if r.startswith("bank"):
                for rd in self.reads_since.get(r, ()):
                    if self.ops[rd].eng != eng:
                        deps.add(rd)
        for r in o.writes:
            w = self.last_write.get(r)
            if w is not None:
                deps.add(w)
                o.waw.add(w)
            for rd in self.reads_since.get(r, ()):
                deps.add(rd)
        o.deps = deps
        for d_ in deps:
            p_ = self.ops[d_]
            if p_.dma:
                o.dneed[p_.key] = p_.dmacount
        for r in list(o.reads) + list(o.writes):
            for d_ in [self.last_write.get(r)] + list(self.reads_since.get(r, ())):
                if d_ is not None and self.ops[d_].dma:
                    o.dneed[self.ops[d_].key] = self.dma_count[self.ops[d_].key]
        if dma:
            o.key = key if key is not None else o.writes[0]
            self.dma_count[o.key] = self.dma_count.get(o.key, 0) + 1
            o.dmacount = self.dma_count[o.key]
            self.last_dma[o.key] = o.idx
        else:
            o.key = None
            o.dmacount = 0
            self.last_eng[eng] = o.idx
        for r in o.reads:
            self.reads_since.setdefault(r, []).append(o.idx)
        for r in o.writes:
            self.last_write[r] = o.idx
            self.reads_since[r] = []
        self.ops.append(o)
        return o

    def emit(self, nc, final_keys=(), final_eng="sp"):
        ops = self.ops
        for o in ops:
            nd = set()
            for d in o.deps:
                p = ops[d]
                if p.dma:
                    if o.dma and p.key == o.key and d in o.waw:
                        continue
                    nd.add(d)
                else:
                    if p.eng == o.eng and not o.dma:
                        if p.eng == "pe":
                            continue
                        if not SAME_ENGINE_SYNC:
                            continue
                    nd.add(d)
            best = {}
            for d in nd:
                p = ops[d]
                k = ("d", p.key) if p.dma else ("e", p.eng)
                if k not in best or best[k] < d:
                    best[k] = d
            o.deps = set(best.values())
            for d in o.deps:
                if not ops[d].dma:
                    ops[d].sig = True
        cnt = {e: 0 for e in ENGS}
        for o in ops:
            if o.sig and not o.dma:
                cnt[o.eng] += 1
                o.sigidx = cnt[o.eng]
        with ExitStack() as st:
            esem = {e: st.enter_context(nc.semaphore("s_" + e)) for e in ENGS}
            dsem = {}
            for k in self.dma_count:
                dsem[k] = st.enter_context(nc.semaphore("d_%d" % len(dsem)))
            block = st.enter_context(nc.Block())

            def run_engine(ename, eng):
                waited = {}
                for o in ops:
                    if o.eng != ename:
                        continue
                    need = {}
                    for d in o.deps:
                        p = ops[d]
                        if p.dma:
                            k = ("d", p.key)
                            v = 16 * o.dneed[p.key]
                            s = dsem[p.key]
                        else:
                            k = ("e", p.eng)
                            v = p.sigidx
                            s = esem[p.eng]
                        if need.get(k, (0, None))[0] < v:
                            need[k] = (v, s)
                    for k, (v, s) in need.items():
                        if waited.get(k, 0) >= v:
                            continue
                        eng.wait_ge(s, v)
                        waited[k] = v
                    ins = o.fn(eng)
                    if o.dma:
                        ins.then_inc(dsem[o.key], 16)
                    elif o.sig:
                        ins.then_inc(esem[o.eng], 1)
                if ename == final_eng:
                    for k in self.dma_count:
                        eng.wait_ge(dsem[k], 16 * self.dma_count[k])

            @block.tensor
            def _(e):
                run_engine("pe", e)

            @block.scalar
            def _(e):
                run_engine("act", e)

            @block.vector
            def _(e):
                run_engine("dve", e)

            @block.gpsimd
            def _(e):
                run_engine("pool", e)

            @block.sync
            def _(e):
                run_engine("sp", e)

    def bar_of(self, o):
        return ()


def _split3(a):
    a = np.asarray(a, np.float32)
    hi = a.astype(ml_dtypes.bfloat16)
    r1 = a - hi.astype(np.float32)
    lo = r1.astype(ml_dtypes.bfloat16)
    r2 = r1 - lo.astype(np.float32)
    ll = r2.astype(ml_dtypes.bfloat16)
    return hi, lo, ll


def make_tables(j):
    bf = ml_dtypes.bfloat16
    T = {}
    slopes = np.exp2(-8.0 * np.arange(1, NH + 1, dtype=np.float32) / NH).astype(np.float32)
    rows = np.zeros((NG, 64, 8, 4, 128), np.float32).astype(bf)
    ql = np.arange(128, dtype=np.float32)
    for g in range(NG):
        for hh in range(4):
            s = slopes[4 * g + hh]
            s3 = _split3(np.full((128,), s, np.float32))
            for qb in range(8):
                tq = (1024 + 128 * qb + ql).astype(np.float32)
                a3 = _split3((-s * tq).astype(np.float32))
                for r in range(3):
                    rows[g, r, qb, hh] = a3[r]
                    rows[g, 3 + r, qb, hh] = s3[r]
                    rows[g, 6 + r, qb, hh] = s3[r]
                rows[g, 9, qb, hh] = (0.0 if j == 1 else NEG)
    T["rows"] = rows.reshape(NG, 64, 8 * 512)
    kr = np.zeros((64, 16, 128), np.float32)
    kl = np.arange(128, dtype=np.float32)
    for kb in range(16):
        kr[0:3, kb] = 1.0
        kr[3:6, kb] = kl[None, :]
        kr[6:9, kb] = 128.0 * kb
        kr[9, kb] = 1.0 if kb < 8 else 0.0
        for jj in range(32):
            kr[32 + jj, kb] = ((2 * kb + (np.arange(128) // 64)) == jj).astype(np.float32)
    T["krows"] = kr.astype(bf).reshape(64, 16 * 128)
    kc = np.zeros((10, 128), np.float32)
    c = np.arange(127, dtype=np.float32)
    kc[0:3, :127] = 1.0
    kc[3:6, :127] = 16.0 * c
    kc[6:9, :127] = 15.5
    T["kcrows"] = kc.astype(bf)
    cm = np.zeros((128, 8, 4, 128), np.float32)
    for qb in range(8):
        tq = 1024 + 128 * qb + np.arange(128)
        cc = np.arange(127)
        valid = (16 * cc[:, None] + 31) <= tq[None, :]
        if j == 0:
            valid = valid & (16 * cc[:, None] >= 1024)
        cm[:127, qb] = np.where(valid, 0.0, NEG)[:, None, :]
    T["cmask"] = cm.astype(bf).reshape(128, 8 * 512)
    lo = np.where(kl[:, None] <= ql[None, :], 0.0, NEG).astype(np.float32)
    up = np.where(kl[:, None] > ql[None, :], 0.0, NEG).astype(np.float32)
    T["tri"] = np.stack([np.repeat(lo[:, None, :], 4, 1), np.repeat(up[:, None, :], 4, 1)], 1).astype(bf).reshape(128, 2 * 512)
    cs = (16 * np.arange(127))[:, None]
    js = (64 * np.arange(32))[None, :]
    ov = np.clip(np.minimum(cs + 32, js + 64) - np.maximum(cs, js), 0, None).astype(np.float32) / 32.0
    ovp = np.zeros((128, 32), np.float32)
    ovp[:127] = ov
    T["ov"] = ovp.astype(bf)
    vm = np.zeros((128, 8, 32), np.float32)
    fb = np.zeros((128, 8, 32), np.float32)
    for qb in range(8):
        for q in range(128):
            tf = 1024 + 128 * qb + q
            ta = tf - (0 if j == 1 else 1024)
            bt = ta // 64
            for jf in range(32):
                ja = jf - (0 if j == 1 else 16)
                if ja < 0:
                    fb[q, qb, jf] = -2e4
                elif ja == 0 or ja == bt or ja == bt - 1:
                    fb[q, qb, jf] = 1e4
                elif ja <= bt:
                    vm[q, qb, jf] = 1.0
                else:
                    fb[q, qb, jf] = -1e4
    T["vm"] = vm.reshape(128, 256)
    T["fb"] = fb.reshape(128, 256)
    hh = np.arange(RH, dtype=np.float64)
    lg = np.log1p(-np.exp2(-5.0 - hh))
    n = np.arange(128, dtype=np.float64)
    diff = n[None, :] - n[:, None]
    dec = np.where(diff[None] >= 0, np.exp(lg[:, None, None] * np.maximum(diff[None], 0.0)), 0.0)
    T["decT"] = (dec * (RDK ** -0.5)).transpose(1, 0, 2).astype(np.float32).reshape(128, RH * 128)
    xi = np.exp(lg[:, None] * (n + 1.0)[None, :])
    T["xi"] = np.repeat(xi.astype(np.float32)[None], 128, 0).reshape(128, RH * 128)
    zeta = np.exp(lg[:, None] * (127 - n)[None, :]) * (RDK ** -0.5)
    zt = np.zeros((128, 16, RH), np.float32)
    for t in range(16):
        if t < 8:
            z = np.exp(lg[:, None] * (1023 - (128 * t + n))[None, :]) * (RDK ** -0.5)
            zt[:, t, :] = (z.T if j == 1 else 0.0)
        else:
            zt[:, t, :] = zeta.T
    T["zt"] = zt.reshape(128, 16 * RH)
    T["gC"] = [float(np.exp(lg[h] * 128)) for h in range(RH)]
    T["ident"] = np.eye(128, dtype=np.float32)
    T["identb"] = np.eye(128, dtype=np.float32).astype(bf)
    T["ones"] = np.ones((128, 128), np.float32)
    return T


TABLE_SPECS = [
    ("rows", [NG, 64, 4096], BF16), ("krows", [64, 2048], BF16), ("kcrows", [10, 128], BF16),
    ("cmask", [128, 4096], BF16), ("tri", [128, 1024], BF16), ("ov", [128, 32], BF16),
    ("vm", [128, 256], F32), ("fb", [128, 256], F32), ("decT", [128, 1024], F32), ("xi", [128, 1024], F32),
    ("zt", [128, 128], F32), ("ident", [128, 128], F32), ("identb", [128, 128], BF16), ("ones", [128, 128], F32),
]
GC = make_tables(1)["gC"]


def build(stop_after=None, dbg=(), mixer_only=False):
    nc = bass.Bass("TRN2", target_bir_lowering=False)
    P = Prog()

    def din(name, shape, dt=F32):
        return nc.dram_tensor(name, shape, dt, kind="ExternalInput").ap()

    xf = din("xf", [2048, D])
    c_l = din("c_l", [128, 16])
    if not mixer_only:
        w_ada = din("w_ada", [D, 9 * D])
        b_ada = din("b_ada", [128, 9 * D])
        gnorm = din("gnorm", [128, 3 * D])
        wg = din("wg", [2, D, DFF])
        wu = din("wu", [2, D, DFF])
        wd = din("wd", [2, DFF, D])
    w_in = din("w_in", [D, NIN])
    gqkT = din("gqkT", [128, 4])
    gqkR = din("gqkR", [128, 4 * 128])
    cpos = din("cpos", [128, 2 * 32])
    cw1 = din("cw1", [2, 4096, 128])
    cb1 = din("cb1", [128, 2])
    cw2 = din("cw2", [2, 128, 128])
    gnr = din("gnr", [128, RH * RDV])
    wpn = din("wpn", [D, D])
    wpr = din("wpr", [D, D])
    wo = din("wo", [D, D])
    tb_ap = {}
    for name, shape, dt in TABLE_SPECS:
        tb_ap[name] = din("t_" + name, shape, dt)
    out = nc.dram_tensor("out", [1024, D], F32, kind="ExternalOutput").ap()
    dbg_ap = {}
    for name, shape, dt in dbg:
        dbg_ap[name] = nc.dram_tensor("dbg_" + name, shape, dt, kind="ExternalOutput").ap()
    if mixer_only:
        ada_d = din("ada_in", [128, 9 * D])
        uT_d = din("uT_in", [2, 128, 16 * 1024], BF16)
        x1_i = din("x1_in", [128, 8 * D])
        x1_d = nc.dram_tensor("x1_d", [128, 8 * D], F32, kind="Internal").ap()
    else:
        ada_d = nc.dram_tensor("ada_d", [128, 9 * D], F32, kind="Internal").ap()
        uT_d = nc.dram_tensor("uT_d", [2, 128, 16 * 1024], BF16, kind="Internal").ap()
        x1_d = nc.dram_tensor("x1_d", [128, 8 * D], F32, kind="Internal").ap()
    orT_d = nc.dram_tensor("orT_d", [128, 16 * 1024], BF16, kind="Internal").ap()

    final_keys = []
    outer = ExitStack()
    with outer:
        uid = [0]

        def SB(st, name, shape, dt):
            uid[0] += 1
            return st.enter_context(nc.sbuf_tensor("%s_%d" % (name, uid[0]), shape, dt))

        banks = [outer.enter_context(nc.psum_tensor("bank%d" % i, [128, 512], F32)) for i in range(8)]
        ident = SB(outer, "ident", [128, 128], F32)
        identb = SB(outer, "identb", [128, 128], BF16)
        ones = SB(outer, "ones", [128, 128], F32)
        P.op("sp", lambda e: e.dma_start(out=ident[:], in_=tb_ap["ident"]), writes=["ident"], dma=True, key="tbl")
        P.op("sp", lambda e: e.dma_start(out=identb[:], in_=tb_ap["identb"]), writes=["identb"], dma=True, key="tbl")
        P.op("sp", lambda e: e.dma_start(out=ones[:], in_=tb_ap["ones"]), writes=["ones"], dma=True, key="tbl")

        def B(i):
            return "bank%d" % i

        def dump(name, src_ap, reads):
            if name in dbg_ap:
                k = "dbg_" + name
                P.op("sp", lambda e: e.dma_start(out=dbg_ap[name], in_=src_ap), reads=reads, writes=[k], dma=True, key=k)
                if k not in final_keys:
                    final_keys.append(k)

        ada_t = {}

        def ada_init(st):
            c_sb = SB(st, "c_sb", [128, 16], F32)
            cond = SB(st, "cond", [128, 16], F32)
            ada_t["condrep"] = SB(st, "condrep", [128, 16, 128], BF16)
            ada_t["wa"] = [SB(st, "wa%d" % i, [128, 16, 256], BF16) for i in range(2)]
            ada_t["ba"] = [SB(st, "ba%d" % i, [128, 256], F32) for i in range(2)]
            ada_t["rs"] = [SB(st, "rs%d" % i, [128, 256], F32) for i in range(2)]
            ada_t["gs"] = [SB(st, "gs%d" % i, [128, 256], F32) for i in range(2)]
            condrep = ada_t["condrep"]
            P.op("sp", lambda e: e.dma_start(out=c_sb[:], in_=c_l), writes=["c_sb"], dma=True, key="tbl")
            P.op("act", lambda e: e.activation(out=cond[:], in_=c_sb[:], func=AF.Silu), reads=["c_sb"], writes=["cond"])
            P.op("dve", lambda e: e.tensor_copy(out=condrep[:], in_=cond[:].unsqueeze(2).to_broadcast([128, 16, 128])),
                 reads=["cond"], writes=["condrep"])

        def ada_block(cb):
            condrep, wa, ba, rs, gs = ada_t["condrep"], ada_t["wa"], ada_t["ba"], ada_t["rs"], ada_t["gs"]
            wsrc = w_ada.rearrange("(k p) n -> p k n", p=128)
            s = cb % 2
            slot, fbk = cb // 8, cb % 8
            i, kind = slot // 3, slot % 3
            cs = slice(cb * 256, (cb + 1) * 256)
            bk = s
            P.op("pool", lambda e: e.dma_start(out=wa[s][:], in_=wsrc[:, :, cs]), writes=["wa%d" % s], dma=True)
            P.op("sp", lambda e: e.dma_start(out=ba[s][:], in_=b_ada[:, cs]), writes=["ba%d" % s], dma=True)
            for k in range(16):
                P.op("pe", lambda e, k=k: e.matmul(banks[bk][:, 0:256], lhsT=condrep[:, k, :], rhs=wa[s][:, k, :], start=(k == 0), stop=(k == 15)),
                     reads=["condrep", "wa%d" % s], writes=[B(bk)])
            P.op("dve", lambda e: e.tensor_tensor(out=rs[s][:], in0=banks[bk][:, 0:256], in1=ba[s][:], op=ALU.add),
                 reads=[B(bk), "ba%d" % s], writes=["rs%d" % s])
            if kind == 1:
                gsl = slice(i * D + fbk * 256, i * D + (fbk + 1) * 256)
                P.op("sp", lambda e: e.dma_start(out=gs[s][:], in_=gnorm[:, gsl]), writes=["gs%d" % s], dma=True)
                P.op("dve", lambda e: e.scalar_tensor_tensor(out=rs[s][:], in0=rs[s][:], scalar=1.0, in1=gs[s][:], op0=ALU.add, op1=ALU.mult),
                     reads=["rs%d" % s, "gs%d" % s], writes=["rs%d" % s])
            elif kind == 2 and i != 1:
                P.op("dve", lambda e: e.tensor_scalar_mul(out=rs[s][:], in0=rs[s][:], scalar1=0.5), reads=["rs%d" % s], writes=["rs%d" % s])
            P.op("sp", lambda e: e.dma_start(out=ada_d[:, cs], in_=rs[s][:]), reads=["rs%d" % s], writes=["ada_d%d" % cb], dma=True, key="st_rs%d" % s)

        def ada_names(slot):
            return ["ada_d%d" % cb for cb in range(slot * 8, slot * 8 + 8)]

        def norm_mod_T(x_sb, i, dstT, dst_name):
            with ExitStack() as st:
                G = SB(st, "G", [128, D], F32)
                SH = SB(st, "SH", [128, D], F32)
                ss = SB(st, "ss", [128, 8], F32)
                junk = SB(st, "junk", [128, D], F32)
                hf = [SB(st, "hf%d" % k, [128, D], F32) for k in range(2)]
                P.op("sp", lambda e: e.dma_start(out=G[:], in_=ada_d[:, (3 * i + 1) * D:(3 * i + 2) * D]),
                     reads=ada_names(3 * i + 1), writes=["G"], dma=True)
                P.op("sp", lambda e: e.dma_start(out=SH[:], in_=ada_d[:, (3 * i) * D:(3 * i + 1) * D]),
                     reads=ada_names(3 * i), writes=["SH"], dma=True)
                P.op("dve", lambda e: e.memset(ss[:], 0.0), writes=["ss"])
                for tb in range(8):
                    P.op("act", lambda e, tb=tb: e.activation(out=junk[:], in_=x_sb[:, tb, :], func=AF.Square, accum_out=ss[:, tb:tb + 1]),
                         reads=["x%d_%d" % (tb, f) for f in range(4)], writes=["junk", "ss"])
                P.op("dve", lambda e: e.tensor_scalar(out=ss[:], in0=ss[:], scalar1=1.0 / D, scalar2=EPS, op0=ALU.mult, op1=ALU.add), reads=["ss"], writes=["ss"])
                P.op("act", lambda e: e.activation(out=ss[:], in_=ss[:], func=AF.Sqrt), reads=["ss"], writes=["ss"])
                P.op("dve", lambda e: e.reciprocal(out=ss[:], in_=ss[:]), reads=["ss"], writes=["ss"])
                tcount = 0
                for tb in range(8):
                    s = tb % 2
                    P.op("dve", lambda e, tb=tb, s=s: e.scalar_tensor_tensor(out=hf[s][:], in0=x_sb[:, tb, :], scalar=ss[:, tb:tb + 1], in1=G[:], op0=ALU.mult, op1=ALU.mult),
                         reads=["x%d_%d" % (tb, f) for f in range(4)] + ["ss", "G"], writes=["hf%d" % s])
                    P.op("dve", lambda e, s=s: e.tensor_tensor(out=hf[s][:], in0=hf[s][:], in1=SH[:], op=ALU.add), reads=["hf%d" % s, "SH"], writes=["hf%d" % s])
                    for kq in range(4):
                        bk = tcount % 8
                        tcount += 1
                        for kk in range(4):
                            P.op("pe", lambda e, s=s, bk=bk, kk=kk, kq=kq: e.transpose(banks[bk][:, kk * 128:(kk + 1) * 128], hf[s][:, (4 * kq + kk) * 128:(4 * kq + kk + 1) * 128], ident[:]),
                                 reads=["hf%d" % s, "ident"], writes=[B(bk)])
                        eng = "act" if kq % 2 == 0 else "dve"
                        if eng == "act":
                            P.op("act", lambda e, bk=bk, kq=kq, tb=tb: e.activation(out=dstT[:, 4 * kq:4 * kq + 4, tb * 128:(tb + 1) * 128], in_=banks[bk][:].rearrange("p (k t) -> p k t", k=4), func=AF.Copy),
                                 reads=[B(bk)], writes=["%s_%d" % (dst_name, tb)])
                        else:
                            P.op("dve", lambda e, bk=bk, kq=kq, tb=tb: e.tensor_copy(out=dstT[:, 4 * kq:4 * kq + 4, tb * 128:(tb + 1) * 128], in_=banks[bk][:].rearrange("p (k t) -> p k t", k=4)),
                                 reads=[B(bk)], writes=["%s_%d" % (dst_name, tb)])
            P.barrier()

        def ffn_core(x_sb, hT, i, l, hook=None):
            Wg = wg[l].rearrange("(k p) n -> p k n", p=128)
            Wu = wu[l].rearrange("(k p) n -> p k n", p=128)
            Wd = wd[l].rearrange("(c p) n -> p c n", p=128)
            with ExitStack() as st:
                GT = SB(st, "GT", [128, D], F32)
                hid = SB(st, "hid", [128, 11, 1024], BF16)
                wgt = [SB(st, "wgt%d" % k, [128, 16, 128], BF16) for k in range(2)]
                wut = [SB(st, "wut%d" % k, [128, 16, 128], BF16) for k in range(2)]
                wdt = [SB(st, "wdt%d" % k, [128, 11, 512], BF16) for k in range(2)]
                stt = [SB(st, "stt%d" % k, [128, 512], F32) for k in range(2)]
                rt = [SB(st, "rt%d" % k, [128, 512], F32) for k in range(2)]
                hT_names = ["hT_%d" % tb for tb in range(8)]
                step = 0
                oset = 0
                for r in range(4):
                    for cc in range(11):
                        c = r * 11 + cc
                        s = c % 2
                        csl = slice(c * 128, (c + 1) * 128)
                        P.op("pool", lambda e, s=s, csl=csl: e.dma_start(out=wgt[s][:], in_=Wg[:, :, csl]), writes=["wgt%d" % s], dma=True)
                        P.op("pool", lambda e, s=s, csl=csl: e.dma_start(out=wut[s][:], in_=Wu[:, :, csl]), writes=["wut%d" % s], dma=True)
                        for half in range(2):
                            bg, bu = 4 + 2 * (step % 2), 5 + 2 * (step % 2)
                            ss_ = step % 2
                            step += 1
                            tsl = slice(half * 512, (half + 1) * 512)
                            hr = hT_names[half * 4:(half + 1) * 4]
                            for k in range(16):
                                P.op("pe", lambda e, s=s, k=k, bg=bg, tsl=tsl: e.matmul(banks[bg][:], lhsT=wgt[s][:, k, :], rhs=hT[:, k, tsl], start=(k == 0), stop=(k == 15)),
                                     reads=["wgt%d" % s] + hr, writes=[B(bg)])
                            for k in range(16):
                                P.op("pe", lambda e, s=s, k=k, bu=bu, tsl=tsl: e.matmul(banks[bu][:], lhsT=wut[s][:, k, :], rhs=hT[:, k, tsl], start=(k == 0), stop=(k == 15)),
                                     reads=["wut%d" % s] + hr, writes=[B(bu)])
                            P.op("act", lambda e, bg=bg, ss_=ss_: e.activation(out=stt[ss_][:], in_=banks[bg][:], func=AF.Silu), reads=[B(bg)], writes=["stt%d" % ss_])
                            P.op("dve", lambda e, bu=bu, ss_=ss_, cc=cc, tsl=tsl: e.tensor_tensor(out=hid[:, cc, tsl], in0=stt[ss_][:], in1=banks[bu][:], op=ALU.mult),
                                 reads=["stt%d" % ss_, B(bu)], writes=["hid%d_%d" % (cc, half)])
                        if hook is not None:
                            hook(c)
                    if r == 0:
                        P.op("sp", lambda e: e.dma_start(out=GT[:], in_=ada_d[:, (3 * i + 2) * D:(3 * i + 3) * D]),
                             reads=ada_names(3 * i + 2), writes=["GT"], dma=True)
                    for fbk in range(4):
                        ws = (r * 4 + fbk) % 2
                        fsl = slice(fbk * 512, (fbk + 1) * 512)
                        P.op("pool", lambda e, ws=ws, r=r, fsl=fsl: e.dma_start(out=wdt[ws][:], in_=Wd[:, r * 11:(r + 1) * 11, fsl]), writes=["wdt%d" % ws], dma=True)
                        for tp in range(4):
                            ob = [2 * (oset % 2), 2 * (oset % 2) + 1]
                            oset += 1
                            for cc in range(11):
                                for t2 in range(2):
                                    tb = 2 * tp + t2
                                    P.op("pe", lambda e, ws=ws, cc=cc, tb=tb, b_=ob[t2]: e.matmul(banks[b_][:], lhsT=hid[:, cc, tb * 128:(tb + 1) * 128], rhs=wdt[ws][:, cc, :], start=(cc == 0), stop=(cc == 10)),
                                         reads=["hid%d_%d" % (cc, tb // 4), "wdt%d" % ws], writes=[B(ob[t2])])
                            for t2 in range(2):
                                tb = 2 * tp + t2
                                P.op("dve", lambda e, t2=t2, b_=ob[t2], fsl=fsl: e.tensor_tensor(out=rt[t2][:], in0=banks[b_][:], in1=GT[:, fsl], op=ALU.mult),
                                     reads=[B(ob[t2]), "GT"], writes=["rt%d" % t2])
                                P.op("dve", lambda e, t2=t2, tb=tb, fsl=fsl: e.tensor_tensor(out=x_sb[:, tb, fsl], in0=x_sb[:, tb, fsl], in1=rt[t2][:], op=ALU.add),
                                     reads=["rt%d" % t2, "x%d_%d" % (tb, fbk)], writes=["x%d_%d" % (tb, fbk)])
            P.barrier()

        xnames = ["x%d_%d" % (tb, f) for tb in range(8) for f in range(4)]

        def load_x(x_sb, src):
            v = src.rearrange("(t p) d -> p t d", p=128)
            for tb in range(8):
                P.op("sp", lambda e, tb=tb: e.dma_start(out=x_sb[:, tb, :], in_=v[:, tb, :]), writes=["x%d_%d" % (tb, f) for f in range(4)], dma=True, key="ldx")

        sada = ExitStack()
        ada_pending = list(range(16, 72))
        if not mixer_only:
            ada_init(sada)
            for cb in range(16):
                ada_block(cb)
        else:
            with ExitStack() as sx0:
                xt0 = SB(sx0, "xt0", [128, 8 * D], F32)
                P.op("sp", lambda e: e.dma_start(out=xt0[:], in_=x1_i), writes=["xt0"], dma=True)
                P.op("sp", lambda e: e.dma_start(out=x1_d, in_=xt0[:]), reads=["xt0"], writes=["x1_d"], dma=True, key="st_x1")
            P.barrier()
        with ExitStack() as sx:
            if not mixer_only:
                x_sb = SB(sx, "x_sb", [128, 8, D], F32)
                hT = SB(sx, "hT", [128, 16, 1024], BF16)
                uT = hT
            for half in ((0, 1) if not mixer_only else ()):
                load_x(x_sb, xf[half * 1024:(half + 1) * 1024, :])
                norm_mod_T(x_sb, 0, hT, "hT")
                lim = 48 if half == 0 else 72

                def hook(c, lim=lim):
                    if ada_pending and ada_pending[0] < lim:
                        ada_block(ada_pending.pop(0))
                ffn_core(x_sb, hT, 0, 0, hook=hook)
                assert not ada_pending or ada_pending[0] >= lim
                norm_mod_T(x_sb, 1, uT, "uT")
                P.op("sp", lambda e, half=half: e.dma_start(out=uT_d[half], in_=uT[:].rearrange("p k t -> p (k t)")),
                     reads=["uT_%d" % tb for tb in range(8)], writes=["uT_d%d" % half], dma=True, key="st_uT")
                if half == 1:
                    P.op("sp", lambda e: e.dma_start(out=x1_d, in_=x_sb[:].rearrange("p t d -> p (t d)")), reads=xnames, writes=["x1_d"], dma=True, key="st_x1")
                    dump("x1", x_sb[:].rearrange("p t d -> p (t d)"), xnames)
                    dump("uT", uT[:].rearrange("p k t -> p (k t)"), ["uT_%d" % tb for tb in range(8)])
                P.barrier()
        sada.close()
        P.barrier()
        if stop_after == "ffn1":
            P.emit(nc, final_keys=final_keys)
            return nc

        Win = w_in.rearrange("(k p) n -> p k n", p=128)

        def load_w(tile, tname, col0, ncols, dst0=0):
            P.op("pool", lambda e: e.dma_start(out=tile[:, :, dst0:dst0 + ncols], in_=Win[:, :, col0:col0 + ncols]), writes=[tname], dma=True)

        def fm_proj(wt, wname, uT_, uname, chunks, bank_of, evac):
            for ci in chunks:
                bk = bank_of(ci)
                rn = ["%s_%d" % (uname, tb) for tb in range(4 * ci, 4 * ci + 4)]
                for k in range(16):
                    P.op("pe", lambda e, k=k, bk=bk, ci=ci: e.matmul(banks[bk][:], lhsT=wt[:, k, 0:128], rhs=uT_[:, k, ci * 512:(ci + 1) * 512], start=(k == 0), stop=(k == 15)),
                         reads=[wname] + rn, writes=[B(bk)])
                evac(bk, ci)

        def tm_proj(wt, wname, ncols, uT_, uname, tbs, bank_of, evac):
            for tb in tbs:
                bk = bank_of(tb)
                for k in range(16):
                    P.op("pe", lambda e, k=k, bk=bk, tb=tb: e.matmul(banks[bk][:, 0:ncols], lhsT=uT_[:, k, tb * 128:(tb + 1) * 128], rhs=wt[:, k, 0:ncols], start=(k == 0), stop=(k == 15)),
                         reads=[wname, "%s_%d" % (uname, tb)], writes=[B(bk)])
                evac(bk, tb)

        with ExitStack() as sm:
            uTo = SB(sm, "uTo", [128, 16, 1024], BF16)
            onT = SB(sm, "onT", [128, 16, 1024], BF16)
            uTo_n = ["uTo_%d" % tb for tb in range(8)]
            for tb in range(8):
                P.op("sp", lambda e, tb=tb: e.dma_start(out=uTo[:, :, tb * 128:(tb + 1) * 128], in_=uT_d[1].rearrange("p (k t) -> p k t", k=16)[:, :, tb * 128:(tb + 1) * 128]),
                     reads=["uT_d1"], writes=["uTo_%d" % tb], dma=True, key="ld_uTo")
            with ExitStack() as sa:
                uTc = SB(sa, "uTc", [128, 16, 1024], BF16)
                for tb in range(8):
                    P.op("sp", lambda e, tb=tb: e.dma_start(out=uTc[:, :, tb * 128:(tb + 1) * 128], in_=uT_d[0].rearrange("p (k t) -> p k t", k=16)[:, :, tb * 128:(tb + 1) * 128]),
                         reads=["uT_d0"], writes=["uTc_%d" % tb], dma=True, key="ld_uTc")
                gates = SB(sa, "gates", [128, 8, 48], F32)
                wt1 = [SB(sa, "wt1_%d" % k, [128, 16, 128], BF16) for k in range(3)]
                load_w(wt1[0], "wt1_0", O_GL, 48)

                def ev_gl(bk, tb):
                    P.op("act", lambda e: e.activation(out=gates[:, tb, :], in_=banks[bk][:, 0:48], func=AF.Sigmoid), reads=[B(bk)], writes=["gates"])
                tm_proj(wt1[0], "wt1_0", 48, uTo, "uTo", range(8), lambda tb: tb % 2, ev_gl)
                P.barrier()
                if stop_after == "gates":
                    P.op("sp", lambda e: e.dma_start(out=out[0:128, 0:384], in_=gates[:].rearrange("p a b -> p (a b)")), reads=["gates"], writes=["outg"], dma=True, key="st_out")
                    final_keys.append("st_out")
                    P.emit(nc, final_keys=final_keys)
                    return nc

                with ExitStack() as sr:
                    orT = SB(sr, "orT", [128, 16, 1024], BF16)
                    decT = SB(sr, "decT", [128, 1, 128], F32)
                    xi_t = SB(sr, "xi_t", [128, 1, 128], F32)
                    zt = SB(sr, "zt", [128, 16, RH], F32)
                    gnr_t = SB(sr, "gnr_t", [128, 1, RDV], F32)
                    P.op("sp", lambda e: e.dma_start(out=zt[:].rearrange("p t h -> p (t h)"), in_=tb_ap["zt"]), writes=["zt"], dma=True, key="tbl")
                    qT = SB(sr, "r_qT", [128, 1024], BF16)
                    qxT = SB(sr, "r_qxT", [128, 1024], BF16)
                    kT = SB(sr, "r_kT", [128, 1024], BF16)
                    kz = SB(sr, "r_kz", [128, 16, 128], BF16)
                    vv = SB(sr, "r_v", [128, 16, 256], BF16)
                    sg = SB(sr, "r_sg", [128, 8, 256], F32)
                    state = SB(sr, "r_state", [128, 256], F32)
                    stbf = SB(sr, "r_stbf", [128, 256], BF16)
                    sTb = [SB(sr, "r_sTb%d" % k, [128, 128], BF16) for k in range(2)]
                    t1 = [SB(sr, "r_t1%d" % k, [128, 256], F32) for k in range(2)]
                    ob = [SB(sr, "r_ob%d" % k, [128, 256], BF16) for k in range(2)]
                    ysum = SB(sr, "r_ysum", [128, 8], F32)
                    ysq = SB(sr, "r_ysq", [128, 8], F32)
                    mean = SB(sr, "r_mean", [128, 8], F32)
                    rstd = SB(sr, "r_rstd", [128, 8], F32)
                    junk2 = SB(sr, "r_junk", [128, 256], F32)
                    wkv = SB(sr, "r_wkv", [128, 16, 384], BF16)
                    wgr = SB(sr, "r_wgr", [128, 16, 256], BF16)
                    for h in range(RH):
                        P.op("sp", lambda e, h=h: e.dma_start(out=decT[:, 0, :], in_=tb_ap["decT"][:, h * 128:(h + 1) * 128]), writes=["decT"], dma=True)
                        P.op("sp", lambda e, h=h: e.dma_start(out=xi_t[:, 0, :], in_=tb_ap["xi"][:, h * 128:(h + 1) * 128]), writes=["xi_t"], dma=True)
                        P.op("sp", lambda e, h=h: e.dma_start(out=gnr_t[:, 0, :], in_=gnr[:, h * 256:(h + 1) * 256]), writes=["gnr_t"], dma=True)
                        load_w(wt1[1], "wt1_1", O_RQ + h * 128, 128)
                        load_w(wt1[2], "wt1_2", O_RK + h * 128, 128)
                        load_w(wkv, "r_wkv", O_RK + h * 128, 128, 0)
                        load_w(wkv, "r_wkv", O_RV + h * 256, 256, 128)
                        load_w(wgr, "r_wgr", O_RG + h * 256, 256)

                        RC = int(os.environ.get("RET_CUT2", "99"))
                        if RC < 2:
                            break

                        def ev_q(bk, ci, h=h):
                            P.op("act", lambda e: e.activation(out=qT[:, ci * 512:(ci + 1) * 512], in_=banks[bk][:], func=AF.Copy), reads=[B(bk)], writes=["r_qT%d" % ci])
                            for c4 in range(4):
                                P.op("dve", lambda e, c4=c4: e.tensor_tensor(out=qxT[:, ci * 512 + c4 * 128:ci * 512 + (c4 + 1) * 128], in0=banks[bk][:, c4 * 128:(c4 + 1) * 128],
                                                                             in1=xi_t[:, 0, :], op=ALU.mult), reads=[B(bk), "xi_t"], writes=["r_qxT%d" % ci])
                        fm_proj(wt1[1], "wt1_1", uTo, "uTo", range(2), lambda ci: ci, ev_q)

                        if RC < 3:
                            break

                        def ev_k(bk, ci):
                            P.op("act", lambda e: e.activation(out=kT[:, ci * 512:(ci + 1) * 512], in_=banks[bk][:], func=AF.Copy), reads=[B(bk)], writes=["r_kT%d" % ci])
                        fm_proj(wt1[2], "wt1_2", uTo, "uTo", range(2), lambda ci: 2 + ci, ev_k)

                        if RC < 4:
                            break

                        def ev_kv(bk, t, h=h):
                            P.op("dve", lambda e: e.tensor_scalar(out=kz[:, t, :], in0=banks[bk][:, 0:128], scalar1=zt[:, t, h:h + 1], scalar2=None, op0=ALU.mult), reads=[B(bk), "zt"], writes=["r_kz%d" % t])
                            P.op("act", lambda e: e.activation(out=vv[:, t, :], in_=banks[bk][:, 128:384], func=AF.Copy), reads=[B(bk)], writes=["r_v%d" % t])
                        tm_proj(wkv, "r_wkv", 384, uTc, "uTc", range(8), lambda tb: 4 + tb % 2, ev_kv)

                        if RC < 5:
                            break

                        def ev_kv2(bk, tb):
                            ev_kv(bk, 8 + tb)
                        tm_proj(wkv, "r_wkv", 384, uTo, "uTo", range(8), lambda tb: 4 + tb % 2, ev_kv2)

                        if RC < 6:
                            break

                        def ev_g(bk, tb):
                            P.op("act", lambda e: e.activation(out=sg[:, tb, :], in_=banks[bk][:, 0:256], func=AF.Silu), reads=[B(bk)], writes=["r_sg%d" % tb])
                        tm_proj(wgr, "r_wgr", 256, uTo, "uTo", range(8), lambda tb: 6 + tb % 2, ev_g)
                        if os.environ.get("RET_CUT") == "proj":
                            break
                        for t in range(8):
                            P.op("pe", lambda e, t=t: e.matmul(banks[0][:, 0:256], lhsT=kz[:, t, :], rhs=vv[:, t, :], start=(t == 0), stop=(t == 7)),
                                 reads=["r_kz%d" % t, "r_v%d" % t], writes=[B(0)])
                        P.op("dve", lambda e: e.tensor_copy(out=state[:], in_=banks[0][:, 0:256]), reads=[B(0)], writes=["r_state"])
                        P.op("act", lambda e: e.activation(out=stbf[:], in_=banks[0][:, 0:256], func=AF.Copy), reads=[B(0)], writes=["r_stbf"])
                        P.op("dve", lambda e: e.memset(ysum[:], 0.0), writes=["r_ysum"])
                        P.op("dve", lambda e: e.memset(ysq[:], 0.0), writes=["r_ysq"])
                        if os.environ.get("RET_CUT") == "state":
                            break
                        for c in range(8):
                            csl = slice(c * 128, (c + 1) * 128)
                            sb_ = 1 + c % 2
                            yb = 4 + c // 2
                            ysl = slice((c % 2) * 256, (c % 2) * 256 + 256)
                            s2 = c % 2
                            P.op("pe", lambda e, csl=csl, sb_=sb_: e.matmul(banks[sb_][:, 0:128], lhsT=kT[:, csl], rhs=qT[:, csl], start=True, stop=True),
                                 reads=["r_kT%d" % (c // 4), "r_qT%d" % (c // 4)], writes=[B(sb_)])
                            P.op("dve", lambda e, sb_=sb_, s2=s2, h=h: e.tensor_tensor(out=sTb[s2][:], in0=banks[sb_][:, 0:128], in1=decT[:, 0, :], op=ALU.mult),
                                 reads=[B(sb_), "decT"], writes=["r_sTb%d" % s2])
                            P.op("pe", lambda e, yb=yb, ysl=ysl, s2=s2, c=c: e.matmul(banks[yb][:, ysl], lhsT=sTb[s2][:], rhs=vv[:, 8 + c, :], start=True, stop=False),
                                 reads=["r_sTb%d" % s2, "r_v%d" % (8 + c)], writes=[B(yb)])
                            P.op("pe", lambda e, yb=yb, ysl=ysl, csl=csl: e.matmul(banks[yb][:, ysl], lhsT=qxT[:, csl], rhs=stbf[:], start=False, stop=True),
                                 reads=["r_qxT%d" % (c // 4), "r_stbf"], writes=[B(yb)])
                            if c < 7:
                                P.op("pe", lambda e, c=c: e.matmul(banks[3][:, 0:256], lhsT=kz[:, 8 + c, :], rhs=vv[:, 8 + c, :], start=True, stop=True),
                                     reads=["r_kz%d" % (8 + c), "r_v%d" % (8 + c)], writes=[B(3)])
                                P.op("dve", lambda e, h=h: e.scalar_tensor_tensor(out=state[:], in0=state[:], scalar=GC[h], in1=banks[3][:, 0:256], op0=ALU.mult, op1=ALU.add),
                                     reads=[B(3), "r_state"], writes=["r_state"])
                                P.op("act", lambda e: e.activation(out=stbf[:], in_=state[:], func=AF.Copy), reads=["r_state"], writes=["r_stbf"])
                            P.op("act", lambda e, yb=yb, ysl=ysl, c=c: e.activation(out=junk2[:], in_=banks[yb][:, ysl], func=AF.Identity, accum_out=ysum[:, c:c + 1]),
                                 reads=[B(yb)], writes=["r_junk", "r_ysum"])
                            P.op("act", lambda e, yb=yb, ysl=ysl, c=c: e.activation(out=junk2[:], in_=banks[yb][:, ysl], func=AF.Square, accum_out=ysq[:, c:c + 1]),
                                 reads=[B(yb)], writes=["r_junk", "r_ysq"])
                        if os.environ.get("RET_CUT") == "chunks":
                            break
                        P.op("dve", lambda e: e.tensor_scalar_mul(out=mean[:], in0=ysum[:], scalar1=1.0 / RDV), reads=["r_ysum"], writes=["r_mean"])
                        P.op("dve", lambda e: e.tensor_tensor(out=rstd[:], in0=mean[:], in1=mean[:], op=ALU.mult), reads=["r_mean"], writes=["r_rstd"])
                        P.op("dve", lambda e: e.scalar_tensor_tensor(out=rstd[:], in0=ysq[:], scalar=1.0 / RDV, in1=rstd[:], op0=ALU.mult, op1=ALU.subtract), reads=["r_ysq", "r_rstd"], writes=["r_rstd"])
                        P.op("dve", lambda e: e.tensor_scalar_add(out=rstd[:], in0=rstd[:], scalar1=EPS), reads=["r_rstd"], writes=["r_rstd"])
                        P.op("act", lambda e: e.activation(out=rstd[:], in_=rstd[:], func=AF.Sqrt), reads=["r_rstd"], writes=["r_rstd"])
                        P.op("dve", lambda e: e.reciprocal(out=rstd[:], in_=rstd[:]), reads=["r_rstd"], writes=["r_rstd"])
                        for c in range(8):
                            yb = 4 + c // 2
                            ysl = slice((c % 2) * 256, (c % 2) * 256 + 256)
                            s2 = c % 2
                            P.op("dve", lambda e, yb=yb, ysl=ysl, c=c, s2=s2: e.tensor_scalar(out=t1[s2][:], in0=banks[yb][:, ysl], scalar1=mean[:, c:c + 1], scalar2=rstd[:, c:c + 1], op0=ALU.subtract, op1=ALU.mult),
                                 reads=[B(yb), "r_mean", "r_rstd"], writes=["r_t1%d" % s2])
                            P.op("dve", lambda e, s2=s2, h=h: e.tensor_tensor(out=t1[s2][:], in0=t1[s2][:], in1=gnr_t[:, 0, :], op=ALU.mult), reads=["r_t1%d" % s2, "gnr_t"], writes=["r_t1%d" % s2])
                            P.op("dve", lambda e, s2=s2, c=c: e.tensor_tensor(out=ob[s2][:], in0=t1[s2][:], in1=sg[:, c, :], op=ALU.mult), reads=["r_t1%d" % s2, "r_sg%d" % c], writes=["r_ob%d" % s2])
                            for e2 in range(2):
                                P.op("pe", lambda e, s2=s2, e2=e2: e.matmul(banks[0][:, e2 * 128:(e2 + 1) * 128], lhsT=ob[s2][:, e2 * 128:(e2 + 1) * 128], rhs=identb[:], start=True, stop=True),
                                     reads=["r_ob%d" % s2, "identb"], writes=[B(0)])
                            P.op("act", lambda e, h=h, c=c: e.activation(out=orT[:, 2 * h:2 * h + 2, c * 128:(c + 1) * 128], in_=banks[0][:, 0:256].rearrange("p (k t) -> p k t", k=2), func=AF.Copy),
                                 reads=[B(0)], writes=["orT_%d" % c])
                    P.op("sp", lambda e: e.dma_start(out=orT_d, in_=orT[:].rearrange("p k t -> p (k t)")), reads=["orT_%d" % c for c in range(8)], writes=["orT_d"], dma=True, key="st_orT")
                    dump("orT", orT[:].rearrange("p k t -> p (k t)"), ["orT_%d" % c for c in range(8)])
                    P.barrier()
                if stop_after == "ret":
                    P.emit(nc, final_keys=final_keys)
                    return nc

                with ExitStack() as sn:
                    gq = SB(sn, "gq", [128, 4], F32)
                    gqr = SB(sn, "gqr", [128, 4, 128], F32)
                    krows = SB(sn, "krows", [64, 16, 128], BF16)
                    kcrows = SB(sn, "kcrows", [10, 128], BF16)
                    cmask = [SB(sn, "cmask%d" % k, [128, 512], BF16) for k in range(2)]
                    wt2 = [SB(sn, "wt2_0", [128, 16, 256], BF16)]
                    tri = SB(sn, "tri", [128, 2, 512], BF16)
                    vm_t = SB(sn, "vm_t", [128, 8, 32], F32)
                    fb_t = SB(sn, "fb_t", [128, 8, 32], F32)
                    ovb = SB(sn, "ovb", [128, 32], BF16)
                    cposf = SB(sn, "cposf", [128, 64], F32)
                    cposb = SB(sn, "cposb", [128, 64], BF16)
                    cb1_t = SB(sn, "cb1_t", [128, 2], F32)
                    P.op("sp", lambda e: e.dma_start(out=gq[:], in_=gqkT), writes=["gq"], dma=True, key="tbl")
                    P.op("sp", lambda e: e.dma_start(out=gqr[:].rearrange("p a d -> p (a d)"), in_=gqkR), writes=["gqr"], dma=True, key="tbl")
                    P.op("sp", lambda e: e.dma_start(out=krows[:].rearrange("p a d -> p (a d)"), in_=tb_ap["krows"]), writes=["krows"], dma=True, key="tbl")
                    P.op("sp", lambda e: e.dma_start(out=kcrows[:], in_=tb_ap["kcrows"]), writes=["kcrows"], dma=True, key="tbl")
                    P.op("sp", lambda e: e.dma_start(out=tri[:].rearrange("p a d -> p (a d)"), in_=tb_ap["tri"]), writes=["tri"], dma=True, key="tbl")
                    P.op("sp", lambda e: e.dma_start(out=vm_t[:].rearrange("p a d -> p (a d)"), in_=tb_ap["vm"]), writes=["vm_t"], dma=True, key="tbl")
                    P.op("sp", lambda e: e.dma_start(out=fb_t[:].rearrange("p a d -> p (a d)"), in_=tb_ap["fb"]), writes=["fb_t"], dma=True, key="tbl")
                    P.op("sp", lambda e: e.dma_start(out=ovb[:], in_=tb_ap["ov"]), writes=["ovb"], dma=True, key="tbl")
                    P.op("sp", lambda e: e.dma_start(out=cposf[:], in_=cpos), writes=["cposf"], dma=True, key="tbl")
                    P.op("sp", lambda e: e.dma_start(out=cb1_t[:], in_=cb1), writes=["cb1_t"], dma=True, key="tbl")
                    P.op("dve", lambda e: e.tensor_copy(out=cposb[:], in_=cposf[:]), reads=["cposf"], writes=["cposb"])
                    P.op("act", lambda e: e.mul(out=gq[:, 0:1], in_=gq[:, 0:1], mul=float(DH ** -0.5)), reads=["gq"], writes=["gq"])
                    qTg = SB(sn, "qTg", [128, 8, 512], BF16)
                    rawT = [SB(sn, "rawT%d" % k, [128, 2048], BF16) for k in range(2)]
                    kslcT = SB(sn, "kslcT", [128, 2048], BF16)
                    kwinT = SB(sn, "kwinT", [128, 2048], BF16)
                    vslc = SB(sn, "vslc", [128, 16, 129], BF16)
                    vwin = SB(sn, "vwin", [128, 16, 129], BF16)
                    RA = SB(sn, "RA", [64, 8, 512], BF16)
                    w1t_ = SB(sn, "w1t0", [128, 32, 128], BF16)
                    w1t = [w1t_, w1t_]
                    w2t = [SB(sn, "w2t%d" % k, [128, 128], BF16) for k in range(2)]
                    hTs = SB(sn, "hTs", [128, 128], BF16)
                    zf = SB(sn, "zf", [128, 128], F32)
                    zt_ = SB(sn, "zt_", [128, 128], F32)
                    btile = SB(sn, "btile", [128, 1], F32)
                    kcn = SB(sn, "kcn", [128, 128], F32)
                    kcmpT = SB(sn, "kcmpT", [128, 128], BF16)
                    vcmp = SB(sn, "vcmp", [128, 161], BF16)
                    sqt2 = [SB(sn, "sqt%d" % k, [128, 512], F32) for k in range(2)]
                    rtt2 = [SB(sn, "rtt%d" % k, [128, 512], F32) for k in range(2)]
                    qkc = [0]
                    Pb = [SB(sn, "Pb%d" % k, [128, 512], BF16) for k in range(4)]
                    rs4 = SB(sn, "rs4", [128, 4], F32)
                    coef = SB(sn, "coef", [128, 4], F32)
                    imp = SB(sn, "imp", [128, 32], F32)
                    work = SB(sn, "work", [128, 32], F32)
                    top8 = SB(sn, "top8", [128, 16], F32)
                    selb = SB(sn, "selb", [128, 64], F32)
                    otmp = SB(sn, "otmp", [128, 4, 128], F32)
                    obf = SB(sn, "obf", [128, 4, 128], BF16)
                    ss1 = SB(sn, "ss1", [128, 1], F32)
                    P.op("dve", lambda e: e.memset(vslc[:, :, 128:129], 1.0), writes=["vslc_ones"])
                    P.op("dve", lambda e: e.memset(vwin[:, :, 128:129], 1.0), writes=["vwin_ones"])
                    P.op("dve", lambda e: e.memset(selb[:], 0.0), writes=["selb"])

                    def qknorm(bk, gcol, dst_ap, dst_names, view=None):
                        qi = qkc[0] % 2
                        qkc[0] += 1
                        sqt, rtt, sb_ = sqt2[qi], rtt2[qi], 2 + qi
                        sn_, rn_ = "sqt%d" % qi, "rtt%d" % qi
                        P.op("act", lambda e: e.activation(out=sqt[:], in_=banks[bk][:], func=AF.Square), reads=[B(bk)], writes=[sn_])
                        P.op("pe", lambda e: e.matmul(banks[sb_][:], lhsT=ones[:], rhs=sqt[:], start=True, stop=True), reads=["ones", sn_], writes=[B(sb_)])
                        P.op("dve", lambda e: e.tensor_scalar(out=rtt[:], in0=banks[sb_][:], scalar1=1.0 / DH, scalar2=EPS, op0=ALU.mult, op1=ALU.add), reads=[B(sb_)], writes=[rn_])
                        P.op("act", lambda e: e.activation(out=rtt[:], in_=rtt[:], func=AF.Sqrt), reads=[rn_], writes=[rn_])
                        P.op("dve", lambda e: e.reciprocal(out=rtt[:], in_=rtt[:]), reads=[rn_], writes=[rn_])
                        if view is None:
                            P.op("dve", lambda e: e.scalar_tensor_tensor(out=dst_ap, in0=banks[bk][:], scalar=gq[:, gcol:gcol + 1], in1=rtt[:], op0=ALU.mult, op1=ALU.mult),
                                 reads=[B(bk), "gq", rn_], writes=dst_names)
                        else:
                            P.op("dve", lambda e: e.scalar_tensor_tensor(out=dst_ap, in0=banks[bk][:].rearrange("p (a q) -> p a q", a=4), scalar=gq[:, gcol:gcol + 1],
                                                                         in1=rtt[:].rearrange("p (a q) -> p a q", a=4), op0=ALU.mult, op1=ALU.mult),
                                 reads=[B(bk), "gq", rn_], writes=dst_names)

                    for g in range(NG):
                        P.op("sp", lambda e, g=g: e.dma_start(out=RA[:].rearrange("p a d -> p (a d)"), in_=tb_ap["rows"][g]), writes=["RA", "RAsel"], dma=True, key="RA")
                        for hh in range(4):
                            wsl = hh % 3
                            load_w(wt1[wsl], "wt1_%d" % wsl, O_Q + (4 * g + hh) * 128, 128)

                            def ev_qn2(bk, ci, hh=hh):
                                qknorm(bk, 0, qTg[:, 4 * ci:4 * ci + 4, hh * 128:(hh + 1) * 128], ["qTg"], view=4)
                            fm_proj(wt1[wsl], "wt1_%d" % wsl, uTo, "uTo", range(2), lambda ci: ci, ev_qn2)
                        for slot, kind in ((0, "raw0"), (1, "raw1"), (2, "kslc"), (4, "kwin")):
                            wsl = slot % 3
                            load_w(wt1[wsl], "wt1_%d" % wsl, O_KV + slot * 512 + g * 128, 128)
                            for src, uname, base in ((uTc, "uTc", 0), (uTo, "uTo", 2)):
                                def ev_f(bk, ci, kind=kind, base=base):
                                    fsl = slice((base + ci) * 512, (base + ci + 1) * 512)
                                    if kind == "raw0":
                                        P.op("act", lambda e: e.activation(out=rawT[0][:, fsl], in_=banks[bk][:], func=AF.Copy), reads=[B(bk)], writes=["rawT0"])
                                    elif kind == "raw1":
                                        P.op("act", lambda e: e.activation(out=rawT[1][:, fsl], in_=banks[bk][:], func=AF.Copy), reads=[B(bk)], writes=["rawT1"])
                                    elif kind == "kslc":
                                        qknorm(bk, 2, kslcT[:, fsl], ["kslcT"])
                                    else:
                                        qknorm(bk, 3, kwinT[:, fsl], ["kwinT"])
                                fm_proj(wt1[wsl], "wt1_%d" % wsl, src, uname, range(2), lambda ci: ci, ev_f)
                        ws2 = 0
                        load_w(wt2[ws2], "wt2_%d" % ws2, O_KV + 3 * 512 + g * 128, 128, 0)
                        load_w(wt2[ws2], "wt2_%d" % ws2, O_KV + 5 * 512 + g * 128, 128, 128)
                        for src, uname, base in ((uTc, "uTc", 0), (uTo, "uTo", 8)):
                            def ev_v(bk, tb, base=base):
                                t = base + tb
                                P.op("act", lambda e: e.activation(out=vslc[:, t, 0:128], in_=banks[bk][:, 0:128], func=AF.Copy), reads=[B(bk)], writes=["vslc%d" % t])
                                P.op("dve", lambda e: e.tensor_copy(out=vwin[:, t, 0:128], in_=banks[bk][:, 128:256]), reads=[B(bk)], writes=["vwin%d" % t])
                            tm_proj(wt2[ws2], "wt2_%d" % ws2, 256, src, uname, range(8), lambda tb: tb % 2, ev_v)
                        for i in range(2):
                            P.op("pool", lambda e, i=i: e.dma_start(out=w1t[i][:], in_=cw1[i].rearrange("(r d) j -> d r j", d=128)), writes=["w1t0"], dma=True)
                            P.op("pool", lambda e, i=i: e.dma_start(out=w2t[i][:], in_=cw2[i]), writes=["w2t%d" % i], dma=True)
                            for r in range(32):
                                P.op("pe", lambda e, i=i, r=r: e.matmul(banks[0][:, 0:127], lhsT=w1t[i][:, r, :], rhs=rawT[i][:, r:r + 16 * 126 + 1:16], start=(r == 0), stop=(r == 31)),
                                     reads=["w1t0", "rawT%d" % i], writes=[B(0)])
                            for r in range(32):
                                P.op("pe", lambda e, i=i, r=r: e.matmul(banks[1][:, 0:1], lhsT=w1t[i][:, r, :], rhs=cposb[:, i * 32 + r:i * 32 + r + 1], start=(r == 0), stop=(r == 31)),
                                     reads=["w1t0", "cposb"], writes=[B(1)])
                            P.op("dve", lambda e, i=i: e.tensor_tensor(out=btile[:], in0=banks[1][:, 0:1], in1=cb1_t[:, i:i + 1], op=ALU.add), reads=[B(1), "cb1_t"], writes=["btile"])
                            P.op("dve", lambda e: e.memset(zf[:], 0.0), writes=["zf"])
                            P.op("act", lambda e: e.activation(out=zf[:, 0:127], in_=banks[0][:, 0:127], func=AF.Identity, bias=btile[:, 0:1]), reads=[B(0), "btile", "zf"], writes=["zf"])
                            P.op("dve", lambda e: e.tensor_tensor(out=zt_[:], in0=zf[:], in1=zf[:], op=ALU.mult), reads=["zf"], writes=["zt_"])
                            P.op("dve", lambda e: e.tensor_tensor(out=zt_[:], in0=zt_[:], in1=zf[:], op=ALU.mult), reads=["zf", "zt_"], writes=["zt_"])
                            P.op("dve", lambda e: e.scalar_tensor_tensor(out=zt_[:], in0=zt_[:], scalar=0.044715, in1=zf[:], op0=ALU.mult, op1=ALU.add), reads=["zf", "zt_"], writes=["zt_"])
                            P.op("act", lambda e: e.activation(out=zt_[:], in_=zt_[:], func=AF.Tanh, scale=float(np.sqrt(2.0 / np.pi))), reads=["zt_"], writes=["zt_"])
                            P.op("dve", lambda e: e.scalar_tensor_tensor(out=zt_[:], in0=zt_[:], scalar=1.0, in1=zf[:], op0=ALU.add, op1=ALU.mult), reads=["zf", "zt_"], writes=["zt_"])
                            P.op("act", lambda e: e.mul(out=hTs[:], in_=zt_[:], mul=0.5), reads=["zt_"], writes=["hTs"])
                            P.op("pe", lambda e, i=i: e.matmul(banks[2][:, 0:128], lhsT=hTs[:], rhs=w2t[i][:], start=True, stop=True), reads=["hTs", "w2t%d" % i], writes=[B(2)])
                            if i == 0:
                                P.op("dve", lambda e: e.memset(ss1[:], 0.0), writes=["ss1"])
                                P.op("act", lambda e: e.activation(out=kcn[:], in_=banks[2][:, 0:128], func=AF.Square, accum_out=ss1[:]), reads=[B(2), "ss1"], writes=["kcn", "ss1"])
                                P.op("dve", lambda e: e.tensor_scalar(out=ss1[:], in0=ss1[:], scalar1=1.0 / DH, scalar2=EPS, op0=ALU.mult, op1=ALU.add), reads=["ss1"], writes=["ss1"])
                                P.op("act", lambda e: e.activation(out=ss1[:], in_=ss1[:], func=AF.Sqrt), reads=["ss1"], writes=["ss1"])
                                P.op("dve", lambda e: e.reciprocal(out=ss1[:], in_=ss1[:]), reads=["ss1"], writes=["ss1"])
                                P.op("dve", lambda e: e.scalar_tensor_tensor(out=kcn[:], in0=banks[2][:, 0:128], scalar=ss1[:, 0:1], in1=gqr[:, 1, :], op0=ALU.mult, op1=ALU.mult),
                                     reads=[B(2), "ss1", "gqr", "kcn"], writes=["kcn"])
                                P.op("pe", lambda e: e.transpose(banks[3][:, 0:128], kcn[:], ident[:]), reads=["kcn", "ident"], writes=[B(3)])
                                P.op("act", lambda e: e.activation(out=kcmpT[:], in_=banks[3][:, 0:128], func=AF.Copy), reads=[B(3)], writes=["kcmpT"])
                                P.op("dve", lambda e: e.memset(kcmpT[:, 127:128], 0.0), reads=["kcmpT"], writes=["kcmpT"])
                            else:
                                P.op("dve", lambda e: e.memset(vcmp[:], 0.0), writes=["vcmp"])
                                P.op("act", lambda e: e.activation(out=vcmp[0:127, 0:128], in_=banks[2][0:127, 0:128], func=AF.Copy), reads=[B(2), "vcmp"], writes=["vcmp"])
                                P.op("dve", lambda e: e.memset(vcmp[0:127, 128:129], 1.0), reads=["vcmp"], writes=["vcmp"])
                                P.op("dve", lambda e: e.tensor_copy(out=vcmp[0:127, 129:161], in_=ovb[0:127, :]), reads=["vcmp", "ovb"], writes=["vcmp"])
                        scount = [0]

                        def score_tile(kT_ap, k_names, qb, extra, pb_i):
                            bk = scount[0] % 3
                            scount[0] += 1
                            n = len(extra)
                            P.op("pe", lambda e: e.matmul(banks[bk][:], lhsT=kT_ap, rhs=qTg[:, qb, :], start=True, stop=(n == 0)), reads=k_names + ["qTg"], writes=[B(bk)])
                            for ii, (l_ap, r_ap, names) in enumerate(extra):
                                P.op("pe", lambda e, l_ap=l_ap, r_ap=r_ap, ii=ii: e.matmul(banks[bk][:], lhsT=l_ap, rhs=r_ap, start=False, stop=(ii == n - 1)), reads=names, writes=[B(bk)])
                            P.op("act", lambda e: e.activation(out=Pb[pb_i][:], in_=banks[bk][:], func=AF.Exp), reads=[B(bk)], writes=["Pb%d" % pb_i])

                        pcount = [0]
                        accset = [0]
                        pending_finish = [None]
                        for qb in range(8):
                            gcol = lambda br, hh: br * 16 + 4 * g + hh
                            a0 = 4 + 2 * (accset[0] % 2)
                            accset[0] += 1
                            pi = pcount[0] % 4
                            pcount[0] += 1
                            cmi = qb % 2
                            P.op("sp", lambda e, qb=qb, cmi=cmi: e.dma_start(out=cmask[cmi][:], in_=tb_ap["cmask"][:, qb * 512:(qb + 1) * 512]), writes=["cmask%d" % cmi], dma=True)
                            score_tile(kcmpT[:], ["kcmpT"], qb,
                                       [(kcrows[0:10, :], RA[0:10, qb, :], ["kcrows", "RA"]), (identb[:], cmask[cmi][:], ["identb", "cmask%d" % cmi])], pi)
                            for hh in range(4):
                                bk = a0 + hh // 2
                                o0 = (hh % 2) * 161
                                P.op("pe", lambda e, bk=bk, o0=o0, hh=hh, pi=pi: e.matmul(banks[bk][:, o0:o0 + 161], lhsT=Pb[pi][:, hh * 128:(hh + 1) * 128], rhs=vcmp[:], start=True, stop=True),
                                     reads=["Pb%d" % pi, "vcmp"], writes=[B(bk)])
                            if pending_finish[0] is not None:
                                pending_finish[0]()
                                pending_finish[0] = None
                            for hh in range(4):
                                bk = a0 + hh // 2
                                o0 = (hh % 2) * 161
                                P.op("dve", lambda e, bk=bk, o0=o0, hh=hh: e.tensor_scalar_max(out=rs4[:, hh:hh + 1], in0=banks[bk][:, o0 + 128:o0 + 129], scalar1=1e-30), reads=[B(bk)], writes=["rs4"])
                            P.op("dve", lambda e: e.reciprocal(out=rs4[:], in_=rs4[:]), reads=["rs4"], writes=["rs4"])
                            for hh in range(4):
                                bk = a0 + hh // 2
                                o0 = (hh % 2) * 161
                                if hh == 0:
                                    P.op("dve", lambda e, bk=bk, o0=o0: e.tensor_scalar(out=imp[:], in0=banks[bk][:, o0 + 129:o0 + 161], scalar1=rs4[:, 0:1], scalar2=None, op0=ALU.mult), reads=[B(bk), "rs4"], writes=["imp"])
                                else:
                                    P.op("dve", lambda e, bk=bk, o0=o0, hh=hh: e.scalar_tensor_tensor(out=imp[:], in0=banks[bk][:, o0 + 129:o0 + 161], scalar=rs4[:, hh:hh + 1], in1=imp[:], op0=ALU.mult, op1=ALU.add),
                                         reads=[B(bk), "rs4", "imp"], writes=["imp"])
                            P.op("dve", lambda e, qb=qb: e.tensor_tensor(out=imp[:], in0=imp[:], in1=vm_t[:, qb, :], op=ALU.mult), reads=["imp", "vm_t"], writes=["imp"])
                            P.op("dve", lambda e, qb=qb: e.tensor_tensor(out=imp[:], in0=imp[:], in1=fb_t[:, qb, :], op=ALU.add), reads=["imp", "fb_t"], writes=["imp"])
                            P.op("dve", lambda e: e.max(out=top8[:, 0:8], in_=imp[:]), reads=["imp"], writes=["top8"])
                            P.op("dve", lambda e: e.match_replace(out=work[:], in_to_replace=top8[:, 0:8], in_values=imp[:], imm_value=-1e30), reads=["imp", "top8"], writes=["work"])
                            P.op("dve", lambda e: e.max(out=top8[:, 8:16], in_=work[:]), reads=["work", "top8"], writes=["top8"])
                            P.op("dve", lambda e: e.tensor_scalar(out=selb[:, 32:64], in0=imp[:], scalar1=top8[:, 15:16], scalar2=None, op0=ALU.is_ge), reads=["imp", "top8", "selb"], writes=["selb"])
                            P.op("dve", lambda e: e.tensor_scalar(out=selb[:, 32:64], in0=selb[:, 32:64], scalar1=-1.0, scalar2=-NEG, op0=ALU.add, op1=ALU.mult), reads=["selb"], writes=["selb"])
                            def sel_finish(qb=qb):
                                P.op("pe", lambda e: e.transpose(banks[3][0:64, 0:128], selb[:], ident[:]), reads=["selb", "ident"], writes=[B(3)])
                                P.op("dve", lambda e, qb=qb: e.tensor_copy(out=RA[32:64, qb, :].rearrange("p (a q) -> p a q", a=4), in_=banks[3][32:64, 0:128].unsqueeze(1).to_broadcast([32, 4, 128])),
                                     reads=[B(3), "RAsel"], writes=["RAsel"])
                            for hh in range(4):
                                P.op("dve", lambda e, hh=hh, qb=qb, gc=gcol(0, hh): e.tensor_tensor(out=coef[:, hh:hh + 1], in0=rs4[:, hh:hh + 1], in1=gates[:, qb, gc:gc + 1], op=ALU.mult), reads=["rs4", "gates", "coef"], writes=["coef"])
                            for hh in range(4):
                                bk = a0 + hh // 2
                                o0 = (hh % 2) * 161
                                P.op("dve", lambda e, bk=bk, o0=o0, hh=hh: e.tensor_scalar(out=otmp[:, hh, :], in0=banks[bk][:, o0:o0 + 128], scalar1=coef[:, hh:hh + 1], scalar2=None, op0=ALU.mult),
                                     reads=[B(bk), "coef", "otmp"], writes=["otmp"])
                            tiles = []
                            for br in (2, 1):
                                a0 = 4 + 2 * (accset[0] % 2)
                                accset[0] += 1
                                kbs = list(range(0, 9 + qb)) if br == 1 else list(range(4 + qb, 9 + qb))
                                for kb in kbs:
                                    tiles.append((br, kb, a0, kb == kbs[0], kb == kbs[-1]))

                            def emit_score(t):
                                br, kb, a0, first, last = t
                                Dd = 8 + qb - kb
                                pi = pcount[0] % 4
                                pcount[0] += 1
                                ksl = slice(kb * 128, (kb + 1) * 128)
                                if br == 1:
                                    extra = [(krows[0:64, kb, :], RA[0:64, qb, :], ["krows", "RA", "RAsel"])]
                                    if Dd == 0:
                                        extra.append((identb[:], tri[:, 0, :], ["identb", "tri"]))
                                    score_tile(kslcT[:, ksl], ["kslcT"], qb, extra, pi)
                                else:
                                    extra = [(krows[0:10, kb, :], RA[0:10, qb, :], ["krows", "RA"])]
                                    if Dd == 0:
                                        extra.append((identb[:], tri[:, 0, :], ["identb", "tri"]))
                                    if Dd == 4:
                                        extra.append((identb[:], tri[:, 1, :], ["identb", "tri"]))
                                    score_tile(kwinT[:, ksl], ["kwinT"], qb, extra, pi)
                                return pi

                            def emit_pv(t, pi):
                                br, kb, a0, first, last = t
                                vt, vn = (vslc, "vslc%d" % kb) if br == 1 else (vwin, "vwin%d" % kb)
                                for hh in range(4):
                                    bk = a0 + hh // 2
                                    o0 = (hh % 2) * 161
                                    P.op("pe", lambda e, bk=bk, o0=o0, hh=hh, pi=pi, vt=vt, kb=kb, st_=(first and hh % 2 == 0), sp_=(last and hh % 2 == 1): e.matmul(banks[bk][:, o0:o0 + 129], lhsT=Pb[pi][:, hh * 128:(hh + 1) * 128], rhs=vt[:, kb, :], start=st_, stop=sp_),
                                         reads=["Pb%d" % pi, vn, ("vslc_ones" if br == 1 else "vwin_ones")], writes=[B(bk)])
                                if not last:
                                    return
                                for bq in range(2):
                                    bk = a0 + bq
                                    P.op("dve", lambda e, bk=bk, bq=bq: e.tensor_scalar_max(out=rs4[:, 2 * bq:2 * bq + 2], in0=banks[bk][:, 128:128 + 162:161], scalar1=1e-30), reads=[B(bk), "rs4"], writes=["rs4"])
                                P.op("dve", lambda e: e.reciprocal(out=rs4[:], in_=rs4[:]), reads=["rs4"], writes=["rs4"])
                                gc0 = br * 16 + 4 * g
                                P.op("dve", lambda e, gc0=gc0, qb=qb: e.tensor_tensor(out=coef[:], in0=rs4[:], in1=gates[:, qb, gc0:gc0 + 4], op=ALU.mult), reads=["rs4", "gates", "coef"], writes=["coef"])
                                for hh in range(4):
                                    bk = a0 + hh // 2
                                    o0 = (hh % 2) * 161
                                    dst = otmp[:, hh, :] if br == 2 else obf[:, hh, :]
                                    dn = "otmp" if br == 2 else "obf"
                                    P.op("dve", lambda e, bk=bk, o0=o0, hh=hh, dst=dst: e.scalar_tensor_tensor(out=dst, in0=banks[bk][:, o0:o0 + 128], scalar=coef[:, hh:hh + 1], in1=otmp[:, hh, :], op0=ALU.mult, op1=ALU.add),
                                         reads=[B(bk), "coef", "otmp", dn], writes=[dn])

                            DEPTH = 2
                            pis = []
                            nsc = [0]

                            def next_score():
                                tn = nsc[0]
                                if tn < len(tiles):
                                    if tn > 0 and tiles[tn][0] == 1 and tiles[tn - 1][0] == 2:
                                        sel_finish()
                                    pis.append(emit_score(tiles[tn]))
                                    nsc[0] += 1
                            for _ in range(DEPTH):
                                next_score()
                            for ti in range(len(tiles)):
                                next_score()
                                emit_pv(tiles[ti], pis[ti])
                            def finish(qb=qb, g=g):
                                for hh in range(4):
                                    P.op("pe", lambda e, hh=hh: e.matmul(banks[3][:, hh * 128:(hh + 1) * 128], lhsT=obf[:, hh, :], rhs=identb[:], start=(hh == 0), stop=(hh == 3)), reads=["obf", "identb"], writes=[B(3)])
                                P.op("act", lambda e: e.activation(out=onT[:, 4 * g:4 * g + 4, qb * 128:(qb + 1) * 128], in_=banks[3][:].rearrange("p (k t) -> p k t", k=4), func=AF.Copy),
                                     reads=[B(3)], writes=["onT_%d" % qb])
                            pending_finish[0] = finish
                        pending_finish[0]()
                        pending_finish[0] = None
                dump("onT", onT[:].rearrange("p k t -> p (k t)"), ["onT_%d" % c for c in range(8)])
            P.barrier()
            if stop_after == "nsa":
                P.emit(nc, final_keys=final_keys)
                return nc

            with ExitStack() as sg_:
                merged = SB(sg_, "merged", [128, 8, D], F32)
                orT2 = SB(sg_, "orT2", [128, 16, 1024], BF16)
                for tb in range(8):
                    P.op("sp", lambda e, tb=tb: e.dma_start(out=orT2[:, :, tb * 128:(tb + 1) * 128], in_=orT_d.rearrange("p (k t) -> p k t", k=16)[:, :, tb * 128:(tb + 1) * 128]),
                         reads=["orT_d"], writes=["orT_%d" % tb], dma=True, key="ld_orT")
                wpa = [SB(sg_, "wpa%d" % k, [128, 16, 256], BF16) for k in range(4)]
                sgt = [SB(sg_, "sgt%d" % k, [128, 256], F32) for k in range(2)]
                mt = [SB(sg_, "mt%d" % k, [128, 256], F32) for k in range(2)]
                for ph, (Wp, oT_, on_, gcol0) in enumerate(((wpn, onT, "onT", O_GA), (wpr, orT2, "orT", O_GB))):
                    Wpv = Wp.rearrange("(k p) n -> p k n", p=128)
                    for fb8 in range(8):
                        fsl = slice(fb8 * 256, (fb8 + 1) * 256)
                        wi = 2 * (fb8 % 2)
                        P.op("pool", lambda e, fsl=fsl, Wpv=Wpv, wi=wi: e.dma_start(out=wpa[wi][:], in_=Wpv[:, :, fsl]), writes=["wpa%d" % wi], dma=True)
                        P.op("pool", lambda e, fb8=fb8, gcol0=gcol0, wi=wi: e.dma_start(out=wpa[wi + 1][:], in_=Win[:, :, gcol0 + fb8 * 256:gcol0 + (fb8 + 1) * 256]), writes=["wpa%d" % (wi + 1)], dma=True)
                        for tb in range(8):
                            s2 = tb % 2
                            bp, bg_ = 2 * s2, 2 * s2 + 1
                            for k in range(16):
                                P.op("pe", lambda e, k=k, bp=bp, tb=tb, oT_=oT_, wi=wi: e.matmul(banks[bp][:, 0:256], lhsT=oT_[:, k, tb * 128:(tb + 1) * 128], rhs=wpa[wi][:, k, :], start=(k == 0), stop=(k == 15)),
                                     reads=["%s_%d" % (on_, tb), "wpa%d" % wi], writes=[B(bp)])
                            for k in range(16):
                                P.op("pe", lambda e, k=k, bg_=bg_, tb=tb, wi=wi: e.matmul(banks[bg_][:, 0:256], lhsT=uTo[:, k, tb * 128:(tb + 1) * 128], rhs=wpa[wi + 1][:, k, :], start=(k == 0), stop=(k == 15)),
                                     reads=["uTo_%d" % tb, "wpa%d" % (wi + 1)], writes=[B(bg_)])
                            P.op("act", lambda e, bg_=bg_, s2=s2: e.activation(out=sgt[s2][:], in_=banks[bg_][:, 0:256], func=AF.Sigmoid), reads=[B(bg_)], writes=["sgt%d" % s2])
                            if ph == 0:
                                P.op("dve", lambda e, bp=bp, s2=s2, tb=tb, fsl=fsl: e.tensor_tensor(out=merged[:, tb, fsl], in0=sgt[s2][:], in1=banks[bp][:, 0:256], op=ALU.mult),
                                     reads=["sgt%d" % s2, B(bp)], writes=["mg%d_%d" % (tb, fb8 // 2)])
                            else:
                                P.op("dve", lambda e, bp=bp, s2=s2: e.tensor_tensor(out=mt[s2][:], in0=sgt[s2][:], in1=banks[bp][:, 0:256], op=ALU.mult),
                                     reads=["sgt%d" % s2, B(bp)], writes=["mt%d" % s2])
                                P.op("dve", lambda e, s2=s2, tb=tb, fsl=fsl: e.tensor_tensor(out=merged[:, tb, fsl], in0=merged[:, tb, fsl], in1=mt[s2][:], op=ALU.add),
                                     reads=["mt%d" % s2, "mg%d_%d" % (tb, fb8 // 2)], writes=["mg%d_%d" % (tb, fb8 // 2)])
                tcount = 0
                for tb in range(8):
                    for kq in range(4):
                        bk = 4 + tcount % 4
                        tcount += 1
                        for kk in range(4):
                            P.op("pe", lambda e, bk=bk, kk=kk, kq=kq, tb=tb: e.transpose(banks[bk][:, kk * 128:(kk + 1) * 128], merged[:, tb, (4 * kq + kk) * 128:(4 * kq + kk + 1) * 128], ident[:]),
                                 reads=["mg%d_%d" % (tb, kq), "ident"], writes=[B(bk)])
                        if kq % 2 == 0:
                            P.op("act", lambda e, bk=bk, kq=kq, tb=tb: e.activation(out=onT[:, 4 * kq:4 * kq + 4, tb * 128:(tb + 1) * 128], in_=banks[bk][:].rearrange("p (k t) -> p k t", k=4), func=AF.Copy),
                                 reads=[B(bk)], writes=["onT_%d" % tb])
                        else:
                            P.op("dve", lambda e, bk=bk, kq=kq, tb=tb: e.tensor_copy(out=onT[:, 4 * kq:4 * kq + 4, tb * 128:(tb + 1) * 128], in_=banks[bk][:].rearrange("p (k t) -> p k t", k=4)),
                                 reads=[B(bk)], writes=["onT_%d" % tb])
            P.barrier()
            with ExitStack() as so:
                x_sb2 = SB(so, "x_sb2", [128, 8, D], F32)
                GT = SB(so, "GT2", [128, D], F32)
                wot = [SB(so, "wot%d" % k, [128, 16, 512], BF16) for k in range(2)]
                rt = [SB(so, "rt2_%d" % k, [128, 512], F32) for k in range(2)]
                for tb in range(8):
                    P.op("sp", lambda e, tb=tb: e.dma_start(out=x_sb2[:, tb, :], in_=x1_d[:, tb * D:(tb + 1) * D]), reads=["x1_d"], writes=["x%d_%d" % (tb, f) for f in range(4)], dma=True, key="ldx")
                P.op("sp", lambda e: e.dma_start(out=GT[:], in_=ada_d[:, 5 * D:6 * D]), reads=ada_names(5), writes=["GT2"], dma=True)
                Wov = wo.rearrange("(k p) n -> p k n", p=128)
                for fbk in range(4):
                    fsl = slice(fbk * 512, (fbk + 1) * 512)
                    ws = fbk % 2
                    P.op("pool", lambda e, fsl=fsl, ws=ws: e.dma_start(out=wot[ws][:], in_=Wov[:, :, fsl]), writes=["wot%d" % ws], dma=True)
                    for tb in range(8):
                        s2 = tb % 2
                        for k in range(16):
                            P.op("pe", lambda e, k=k, s2=s2, tb=tb, ws=ws: e.matmul(banks[s2][:], lhsT=onT[:, k, tb * 128:(tb + 1) * 128], rhs=wot[ws][:, k, :], start=(k == 0), stop=(k == 15)),
                                 reads=["onT_%d" % tb, "wot%d" % ws], writes=[B(s2)])
                        P.op("dve", lambda e, s2=s2, fsl=fsl: e.tensor_tensor(out=rt[s2][:], in0=banks[s2][:], in1=GT[:, fsl], op=ALU.mult), reads=[B(s2), "GT2"], writes=["rt2_%d" % s2])
                        P.op("dve", lambda e, s2=s2, tb=tb, fsl=fsl: e.tensor_tensor(out=x_sb2[:, tb, fsl], in0=x_sb2[:, tb, fsl], in1=rt[s2][:], op=ALU.add),
                             reads=["rt2_%d" % s2, "x%d_%d" % (tb, fbk)], writes=["x%d_%d" % (tb, fbk)])
                P.op("sp", lambda e: e.dma_start(out=x1_d, in_=x_sb2[:].rearrange("p t d -> p (t d)")), reads=xnames, writes=["x1_d"], dma=True, key="st_x1")
                dump("x2", x_sb2[:].rearrange("p t d -> p (t d)"), xnames)
            P.barrier()
        P.barrier()
        if stop_after == "mix":
            P.emit(nc, final_keys=final_keys)
            return nc

        with ExitStack() as sx:
            x_sb3 = SB(sx, "x_sb3", [128, 8, D], F32)
            hT3 = SB(sx, "hT3", [128, 16, 1024], BF16)
            for tb in range(8):
                P.op("sp", lambda e, tb=tb: e.dma_start(out=x_sb3[:, tb, :], in_=x1_d[:, tb * D:(tb + 1) * D]), reads=["x1_d"], writes=["x%d_%d" % (tb, f) for f in range(4)], dma=True, key="ldx")
            norm_mod_T(x_sb3, 2, hT3, "hT")
            ffn_core(x_sb3, hT3, 2, 1)
            ov = out.rearrange("(t p) d -> p t d", p=128)
            for tb in range(8):
                P.op("sp", lambda e, tb=tb: e.dma_start(out=ov[:, tb, :], in_=x_sb3[:, tb, :]), reads=["x%d_%d" % (tb, f) for f in range(4)], writes=["out%d" % tb], dma=True, key="st_out")
            final_keys.append("st_out")
        P.emit(nc, final_keys=final_keys)
    return nc


def prep_inputs(inp, n_pairs=4):
    f32 = np.float32
    x = np.asarray(inp["x"], f32)
    rep = lambda v: np.ascontiguousarray(np.broadcast_to(np.asarray(v, f32).reshape(1, -1), (128, np.asarray(v).size)))
    shared = {
        "w_ada": np.ascontiguousarray(np.asarray(inp["w_ada"], f32)[0]),
        "b_ada": rep(inp["b_ada"][0]),
        "gnorm": rep(inp["g_norm"][0]),
        "wg": np.ascontiguousarray(np.asarray(inp["w_ffn_gate"], f32)[0]),
        "wu": np.ascontiguousarray(np.asarray(inp["w_ffn_up"], f32)[0]),
        "wd": np.ascontiguousarray(np.asarray(inp["w_ffn_down"], f32)[0]),
        "w_in": np.ascontiguousarray(np.asarray(inp["w_in"], f32)[0]),
        "gqkT": np.ascontiguousarray(np.asarray(inp["g_qk"], f32)[0].T),
        "gqkR": rep(inp["g_qk"][0]),
        "cpos": np.ascontiguousarray(np.asarray(inp["cmp_pos"], f32)[0].transpose(2, 0, 1).reshape(128, 64)),
        "cw1": np.ascontiguousarray(np.asarray(inp["cmp_w1"], f32)[0]),
        "cb1": np.ascontiguousarray(np.asarray(inp["cmp_b1"], f32)[0].T),
        "cw2": np.ascontiguousarray(np.asarray(inp["cmp_w2"], f32)[0]),
        "gnr": rep(inp["ret_gn_gain"][0]),
        "wpn": np.ascontiguousarray(np.asarray(inp["w_proj_nsa"], f32)[0]),
        "wpr": np.ascontiguousarray(np.asarray(inp["w_proj_ret"], f32)[0]),
        "wo": np.ascontiguousarray(np.asarray(inp["w_out"], f32)[0]),
    }
    tabs = [make_tables(0), make_tables(1)]
    maps = []
    cvec = np.asarray(inp["c"], f32)
    for b in range(n_pairs):
        for j in range(2):
            m = dict(shared)
            if j == 0:
                fr = np.concatenate([np.zeros((1024, D), f32), x[b, :1024]], 0)
            else:
                fr = x[b]
            m["xf"] = np.ascontiguousarray(fr)
            m["c_l"] = np.ascontiguousarray(cvec[b].reshape(16, 128).T)
            for name, shape, dt in TABLE_SPECS:
                m["t_" + name] = np.ascontiguousarray(tabs[j][name]).reshape(shape)
            maps.append(m)
    return maps


def kernel(**inputs):
    nc = build()
    maps = prep_inputs(inputs)
    res = run_bass_kernel_spmd(nc, maps, core_ids=list(range(8)))
    outp = np.zeros((4, 2048, D), np.float32)
    for b in range(4):
        for j in range(2):
            outp[b, j * 1024:(j + 1) * 1024] = np.asarray(res.results[2 * b + j]["out"]).reshape(1024, D)
    return outp
```
